# Optimizing a Trainium2 kernel written in Bass

```python
import math
import jax, jax.numpy as jnp
from jax import lax
import numpy as np

D_MODEL = 1024
BATCH = 8
SEQ = 2048
DEPTH = 4

GRID_W = 64
CTX_LEN = 256
N_MIXERS = 2
N_S5_LAYERS = (DEPTH + 1) // 2
N_ATTN_LAYERS = DEPTH // 2
S5_GROUP = 16
S5_GROUPS = D_MODEL // S5_GROUP
S5_STATE = 64
DT_MIN = 0.001
DT_MAX = 0.1
N_HEADS = 16
N_KV_HEADS = 4
HEAD_DIM = D_MODEL // N_HEADS
Q_PER_KV = N_HEADS // N_KV_HEADS
ROPE_AXIS_DIM = HEAD_DIM // 2
ROPE_THETA = 10000.0
Q_BLOCK = 128
QKV_DIM = (N_HEADS + 2 * N_KV_HEADS) * HEAD_DIM
D_FF = 4 * D_MODEL
NORM_EPS = 1e-6

kernel_name = "hybrid_s5_gqa_dit_prefix_trunk"


def _rmsnorm(x, g):
    xf = x.astype(jnp.float32)
    r = lax.rsqrt(jnp.mean(xf * xf, axis=-1, keepdims=True) + NORM_EPS)
    return (xf * r).astype(x.dtype) * g


def _modulate(h, shift, scale):
    return h * (1.0 + scale) + shift


def _sq_relu_mlp(h, w1, w2):
    return jnp.square(jax.nn.relu(h @ w1)) @ w2


def _diag_scan(a_re, a_im, b_re, b_im, reverse):
    L = b_re.shape[1]
    a_re = jnp.broadcast_to(a_re, (1, L) + a_re.shape)
    a_im = jnp.broadcast_to(a_im, (1, L) + a_im.shape)

    def combine(e1, e2):
        ar1, ai1, br1, bi1 = e1
        ar2, ai2, br2, bi2 = e2
        return (ar2 * ar1 - ai2 * ai1,
                ar2 * ai1 + ai2 * ar1,
                ar2 * br1 - ai2 * bi1 + br2,
                ar2 * bi1 + ai2 * br1 + bi2)

    _, _, h_re, h_im = lax.associative_scan(combine, (a_re, a_im, b_re, b_im), axis=1, reverse=reverse)
    return h_re, h_im


def _s5_mixer(h, hc, a_re, a_im, log_dt, b_re, b_im, c_re, c_im, d, w_glu, need_ctx_out):
    Bsz, L, _ = h.shape
    CL = hc.shape[1]
    f32 = jnp.float32
    u = h.astype(f32).reshape(Bsz, L, S5_GROUPS, S5_GROUP)
    uc = hc.astype(f32).reshape(Bsz, CL, S5_GROUPS, S5_GROUP)
    d32 = d.astype(f32)
    y = d32 * h.astype(f32)
    yc = d32 * hc.astype(f32) if need_ctx_out else None
    for dirn in range(2):
        reverse = dirn == 1
        ar = a_re[dirn].astype(f32)
        ai = a_im[dirn].astype(f32)
        dt = jnp.exp(log_dt[dirn].astype(f32))[:, None]
        mag = jnp.exp(dt * ar)
        abar_re = mag * jnp.cos(dt * ai)
        abar_im = mag * jnp.sin(dt * ai)
        den = ar * ar + ai * ai
        nr = abar_re - 1.0
        f_re = (nr * ar + abar_im * ai) / den
        f_im = (abar_im * ar - nr * ai) / den
        br = b_re[dirn].astype(f32)
        bi = b_im[dirn].astype(f32)
        bb_re = f_re[..., None] * br - f_im[..., None] * bi
        bb_im = f_re[..., None] * bi + f_im[..., None] * br
        cr = c_re[dirn].astype(f32)
        ci = c_im[dirn].astype(f32)
        bu_re = jnp.einsum('blgh,gph->blgp', uc, bb_re)
        bu_im = jnp.einsum('blgh,gph->blgp', uc, bb_im)
        hc_re, hc_im = _diag_scan(abar_re, abar_im, bu_re, bu_im, reverse)
        pos_c = 0 if reverse else CL - 1
        init_re = hc_re[:, pos_c]
        init_im = hc_im[:, pos_c]
        if need_ctx_out:
            yc = yc + (jnp.einsum('blgp,ghp->blgh', hc_re, cr)
                       - jnp.einsum('blgp,ghp->blgh', hc_im, ci)).reshape(Bsz, CL, D_MODEL)
        bu_re = jnp.einsum('blgh,gph->blgp', u, bb_re)
        bu_im = jnp.einsum('blgh,gph->blgp', u, bb_im)
        pos = L - 1 if reverse else 0
        bu_re = bu_re.at[:, pos].add(abar_re * init_re - abar_im * init_im)
        bu_im = bu_im.at[:, pos].add(abar_re * init_im + abar_im * init_re)
        hl_re, hl_im = _diag_scan(abar_re, abar_im, bu_re, bu_im, reverse)
        y = y + (jnp.einsum('blgp,ghp->blgh', hl_re, cr)
                 - jnp.einsum('blgp,ghp->blgh', hl_im, ci)).reshape(Bsz, L, D_MODEL)

    def glu(t):
        z = jax.nn.gelu(t).astype(h.dtype) @ w_glu
        za, zb = jnp.split(z, 2, axis=-1)
        return za * jax.nn.sigmoid(zb)

    return glu(y), (glu(yc) if need_ctx_out else None)


def _rope_axis(x, ang):
    x1, x2 = jnp.split(x, 2, axis=-1)
    shape = (1, ang.shape[0]) + (1,) * (x.ndim - 3) + (ang.shape[1],)
    cos = jnp.cos(ang).reshape(shape).astype(x.dtype)
    sin = jnp.sin(ang).reshape(shape).astype(x.dtype)
    return jnp.concatenate([x1 * cos - x2 * sin, x2 * cos + x1 * sin], axis=-1)


def _rope_2d(x, ang_row, ang_col):
    return jnp.concatenate([_rope_axis(x[..., :ROPE_AXIS_DIM], ang_row),
                            _rope_axis(x[..., ROPE_AXIS_DIM:], ang_col)], axis=-1)


def _attend(q, k, v):
    s = jnp.einsum('bqkgd,bskd->bkgqs', q, k).astype(jnp.float32) * (HEAD_DIM ** -0.5)
    p = jax.nn.softmax(s, axis=-1).astype(v.dtype)
    return jnp.einsum('bkgqs,bskd->bqkgd', p, v)


def _attn_mixer(h, hc, w_qkv, q_g, k_g, w_o, ang_row, ang_col, need_ctx_out):
    Bsz, L, _ = h.shape
    CL = hc.shape[1]

    def project(t):
        n = t.shape[1]
        qkv = t @ w_qkv
        q = qkv[..., :N_HEADS * HEAD_DIM].reshape(Bsz, n, N_KV_HEADS, Q_PER_KV, HEAD_DIM)
        k = qkv[..., N_HEADS * HEAD_DIM:(N_HEADS + N_KV_HEADS) * HEAD_DIM].reshape(Bsz, n, N_KV_HEADS, HEAD_DIM)
        v = qkv[..., (N_HEADS + N_KV_HEADS) * HEAD_DIM:].reshape(Bsz, n, N_KV_HEADS, HEAD_DIM)
        return _rmsnorm(q, q_g), _rmsnorm(k, k_g), v

    q, k, v = project(h)
    qc, kc, vc = project(hc)
    q = _rope_2d(q, ang_row, ang_col)
    k = _rope_2d(k, ang_row, ang_col)
    k_all = jnp.concatenate([kc, k], axis=1)
    v_all = jnp.concatenate([vc, v], axis=1)
    nb = L // Q_BLOCK
    q_blocks = q.reshape(Bsz, nb, Q_BLOCK, N_KV_HEADS, Q_PER_KV, HEAD_DIM).transpose(1, 0, 2, 3, 4, 5)
    o = lax.map(lambda qb: _attend(qb, k_all, v_all), q_blocks)
    o = o.transpose(1, 0, 2, 3, 4, 5).reshape(Bsz, L, D_MODEL)
    y = o @ w_o
    yc = None
    if need_ctx_out:
        yc = _attend(qc, kc, vc).reshape(Bsz, CL, D_MODEL) @ w_o
    return y, yc


def setup_inputs(seed: int = 0) -> dict:
    key = jax.random.key(seed)
    ks = jax.random.split(key, 24)
    D = D_MODEL
    G, P, H = S5_GROUPS, S5_STATE, S5_GROUP
    nrm = jax.random.normal
    return {
        "x": nrm(ks[0], (BATCH, SEQ, D), jnp.float32),
        "c": nrm(ks[1], (BATCH, D), jnp.float32),
        "ctx": nrm(ks[2], (BATCH, CTX_LEN, D), jnp.float32),
        "c_ctx": nrm(ks[3], (D,), jnp.float32),
        "ada_w": nrm(ks[4], (DEPTH, D, 6 * D), jnp.float32) * (0.5 * D ** -0.5),
        "ada_b": nrm(ks[5], (DEPTH, 6 * D), jnp.float32) * 0.02,
        "norm_mix_g": 1.0 + 0.02 * nrm(ks[6], (DEPTH, D), jnp.float32),
        "norm_ffn_g": 1.0 + 0.02 * nrm(ks[7], (DEPTH, D), jnp.float32),
        "s5_a_re": -0.5 + 0.01 * nrm(ks[8], (N_S5_LAYERS, 2, G, P), jnp.float32),
        "s5_a_im": math.pi * jnp.arange(P, dtype=jnp.float32) + 0.01 * nrm(ks[9], (N_S5_LAYERS, 2, G, P), jnp.float32),
        "s5_log_dt": jax.random.uniform(ks[10], (N_S5_LAYERS, 2, G), jnp.float32, math.log(DT_MIN), math.log(DT_MAX)),
        "s5_b_re": nrm(ks[11], (N_S5_LAYERS, 2, G, P, H), jnp.float32) * (2 * H) ** -0.5,
        "s5_b_im": nrm(ks[12], (N_S5_LAYERS, 2, G, P, H), jnp.float32) * (2 * H) ** -0.5,
        "s5_c_re": nrm(ks[13], (N_S5_LAYERS, 2, G, H, P), jnp.float32) * P ** -0.5,
        "s5_c_im": nrm(ks[14], (N_S5_LAYERS, 2, G, H, P), jnp.float32) * P ** -0.5,
        "s5_d": nrm(ks[15], (N_S5_LAYERS, D), jnp.float32),
        "s5_w_glu": nrm(ks[16], (N_S5_LAYERS, D, 2 * D), jnp.float32) * D ** -0.5,
        "attn_w_qkv": nrm(ks[17], (N_ATTN_LAYERS, D, QKV_DIM), jnp.float32) * D ** -0.5,
        "attn_q_g": 1.0 + 0.02 * nrm(ks[18], (N_ATTN_LAYERS, HEAD_DIM), jnp.float32),
        "attn_k_g": 1.0 + 0.02 * nrm(ks[19], (N_ATTN_LAYERS, HEAD_DIM), jnp.float32),
        "attn_w_o": nrm(ks[20], (N_ATTN_LAYERS, D, D), jnp.float32) * D ** -0.5,
        "ffn_w1": nrm(ks[21], (DEPTH, D, D_FF), jnp.float32) * D ** -0.5,
        "ffn_w2": nrm(ks[22], (DEPTH, D_FF, D), jnp.float32) * D_FF ** -0.5,
        "final_g": 1.0 + 0.02 * nrm(ks[23], (D,), jnp.float32),
    }


def reference(x, c, ctx, c_ctx, ada_w, ada_b, norm_mix_g, norm_ffn_g, s5_a_re, s5_a_im, s5_log_dt,
              s5_b_re, s5_b_im, s5_c_re, s5_c_im, s5_d, s5_w_glu, attn_w_qkv, attn_q_g, attn_k_g,
              attn_w_o, ffn_w1, ffn_w2, final_g):
    L = x.shape[1]
    rows = L // GRID_W
    row = jnp.repeat(jnp.arange(rows, dtype=jnp.float32), GRID_W)
    col = jnp.tile(jnp.arange(GRID_W, dtype=jnp.float32), rows)
    n_freq = ROPE_AXIS_DIM // 2
    inv_freq = ROPE_THETA ** (-jnp.arange(n_freq, dtype=jnp.float32) / n_freq)
    ang_row = row[:, None] * inv_freq[None, :]
    ang_col = col[:, None] * inv_freq[None, :]

    silu_c = jax.nn.silu(c)
    silu_cc = jax.nn.silu(c_ctx)
    for i in range(DEPTH):
        last = i == DEPTH - 1
        mod = (silu_c @ ada_w[i] + ada_b[i])[:, None, :]
        mod_c = (silu_cc @ ada_w[i] + ada_b[i])[None, None, :]
        sh1, sc1, g1, sh2, sc2, g2 = jnp.split(mod, 6, axis=-1)
        csh1, csc1, cg1, csh2, csc2, cg2 = jnp.split(mod_c, 6, axis=-1)

        h = _modulate(_rmsnorm(x, norm_mix_g[i]), sh1, sc1)
        hc = _modulate(_rmsnorm(ctx, norm_mix_g[i]), csh1, csc1)
        j = i // N_MIXERS
        if i % N_MIXERS == 0:
            y, yc = _s5_mixer(h, hc, s5_a_re[j], s5_a_im[j], s5_log_dt[j], s5_b_re[j], s5_b_im[j],
                              s5_c_re[j], s5_c_im[j], s5_d[j], s5_w_glu[j], not last)
        else:
            y, yc = _attn_mixer(h, hc, attn_w_qkv[j], attn_q_g[j], attn_k_g[j], attn_w_o[j],
                                ang_row, ang_col, not last)
        x = x + g1 * y
        h = _modulate(_rmsnorm(x, norm_ffn_g[i]), sh2, sc2)
        x = x + g2 * _sq_relu_mlp(h, ffn_w1[i], ffn_w2[i])
        if not last:
            ctx = ctx + cg1 * yc
            hc = _modulate(_rmsnorm(ctx, norm_ffn_g[i]), csh2, csc2)
            ctx = ctx + cg2 * _sq_relu_mlp(hc, ffn_w1[i], ffn_w2[i])
    return _rmsnorm(x, final_g)
```

```python
import math
from contextlib import ExitStack
import numpy as np
import concourse.bass as bass
import concourse.mybir as mybir
from concourse.bass_utils import run_bass_kernel_spmd

F32 = mybir.dt.float32
BF = mybir.dt.bfloat16
AF = mybir.ActivationFunctionType
ALU = mybir.AluOpType

D = 1024
DEPTH = 4
L = 2048
CL = 256
T = L + CL
NG = 64
NP = 64
NH = 16
Q = 16
NCH = T // Q
NCC = CL // Q
NCL = L // Q
HD = 64
NHEAD = 16
NKV = 4
DFF = 4096
EPS = 1e-6

QPERM = [0, 4, 1, 5, 2, 6, 3, 7, 8, 12, 9, 13, 10, 14, 11, 15]
COMPUTE = ("tensor", "vector", "scalar", "gpsimd")
NSLOT = 8


class Op:
    __slots__ = ("eng", "fn", "reads", "writes", "dma", "idx", "deps", "waits",
                 "needs_inc", "ctr", "slot", "slot_use", "gidx", "force")

    def __init__(self, eng, fn, reads, writes, dma):
        self.eng, self.fn, self.reads, self.writes, self.dma = eng, fn, reads, writes, dma
        self.deps = []
        self.waits = []
        self.needs_inc = False
        self.ctr = 0
        self.slot = None
        self.slot_use = 0


class Prog:
    def __init__(self, nc, strict=True):
        self.nc = nc
        self.strict = strict
        self.ops = []
        self.stack = ExitStack()
        self.last_w = {}
        self.readers = {}
        self.bar_tile = None
        self.last_bar = None
        self.bar_from = 0
        self.fence = False
        self.last_pe = None

    def sb(self, name, shape, dtype):
        return self.stack.enter_context(self.nc.sbuf_tensor(name, list(shape), dtype))

    def ps(self, name, shape, dtype):
        return self.stack.enter_context(self.nc.psum_tensor(name, list(shape), dtype))

    def op(self, eng, fn, reads=(), writes=(), dma=False):
        o = Op(eng, fn, tuple(reads), tuple(writes), dma)
        o.gidx = len(self.ops)
        deps = set()
        for k in o.reads:
            w = self.last_w.get(k)
            if w is not None:
                deps.add(w)
        for k in o.writes:
            w = self.last_w.get(k)
            if w is not None:
                deps.add(w)
            for r in self.readers.get(k, ()):
                deps.add(r)
        deps.discard(o.gidx)
        if self.last_bar is not None:
            deps.add(self.last_bar)
        o.force = None
        if eng == "tensor":
            if self.fence and self.last_pe is not None:
                deps.add(self.last_pe)
                o.force = self.last_pe
            self.fence = False
            self.last_pe = o.gidx
        o.deps = sorted(deps)
        for k in o.reads:
            self.readers.setdefault(k, []).append(o.gidx)
        for k in o.writes:
            self.last_w[k] = o.gidx
            self.readers[k] = []
        self.ops.append(o)
        return o

    def pe_fence(self):
        self.fence = True

    def barrier(self):
        if self.bar_tile is None:
            self.bar_tile = self.sb("bar_tile", [128, 8], F32)
        o = Op("vector", lambda e: e.memset(self.bar_tile[:, 0:1], 0.0), (), (), False)
        o.force = None
        o.gidx = len(self.ops)
        lastc = {}
        deps = set()
        for q in self.ops[self.bar_from:]:
            if q.dma:
                deps.add(q.gidx)
            else:
                lastc[q.eng] = q.gidx
        deps.update(lastc.values())
        if self.last_bar is not None:
            deps.add(self.last_bar)
        o.deps = sorted(deps)
        self.ops.append(o)
        self.last_bar = o.gidx
        self.bar_from = len(self.ops)
        return o

    def dma(self, eng, out, in_, reads=(), writes=(), **kw):
        return self.op(eng, lambda e: e.dma_start(out=out, in_=in_, **kw), reads, writes, dma=True)

    def emit(self):
        nc = self.nc
        ops = self.ops
        engs = ("tensor", "vector", "scalar", "gpsimd", "sync")
        per = {e: [] for e in engs}
        for o in ops:
            o.idx = len(per[o.eng])
            per[o.eng].append(o)
        dcount = {e: 0 for e in engs}
        for o in ops:
            if o.dma:
                o.slot = dcount[o.eng] % NSLOT
                o.slot_use = dcount[o.eng] // NSLOT + 1
                dcount[o.eng] += 1
        known = {e: {f: -1 for f in COMPUTE} for e in engs}
        kdma = {e: {} for e in engs}
        snap = [None] * len(ops)
        for o in ops:
            E = o.eng
            kn = known[E]
            for d in reversed(o.deps):
                pr = ops[d]
                if pr.dma:
                    key = (pr.eng, pr.slot)
                    if kdma[E].get(key, 0) >= pr.slot_use:
                        continue
                    kdma[E][key] = pr.slot_use
                    o.waits.append(("dma", pr.eng, pr.slot, pr.slot_use))
                    psn = snap[d]
                    for f in COMPUTE:
                        if psn[f] > kn[f]:
                            kn[f] = psn[f]
                else:
                    Fe = pr.eng
                    if Fe == E and not o.dma and (Fe == "tensor" or not self.strict) and getattr(o, "force", None) != d:
                        continue
                    if kn[Fe] >= pr.idx:
                        continue
                    o.waits.append(("eng", d))
                    pr.needs_inc = True
                    psn = snap[d]
                    for f in COMPUTE:
                        if psn[f] > kn[f]:
                            kn[f] = psn[f]
                    if pr.idx > kn[Fe]:
                        kn[Fe] = pr.idx
            if o.dma and o.slot_use > 1:
                key = (o.eng, o.slot)
                if kdma[E].get(key, 0) < o.slot_use - 1:
                    kdma[E][key] = o.slot_use - 1
                    o.waits.append(("dma", o.eng, o.slot, o.slot_use - 1))
            snap[o.gidx] = dict(kn)
        for e in COMPUTE:
            c = 0
            for o in per[e]:
                if o.needs_inc and not o.dma:
                    c += 1
                    o.ctr = c
        st = self.stack
        esem = {e: st.enter_context(nc.semaphore("c_" + e)) for e in COMPUTE}
        dsem = {}
        for e in engs:
            for s in range(min(NSLOT, dcount[e])):
                dsem[(e, s)] = st.enter_context(nc.semaphore("d_%s_%d" % (e, s)))
        last_use = {}
        for o in ops:
            if o.dma:
                last_use[(o.eng, o.slot)] = o.slot_use

        def run(e, ename):
            for o in per[ename]:
                for w in o.waits:
                    if w[0] == "dma":
                        e.wait_ge(dsem[(w[1], w[2])], 16 * w[3])
                    else:
                        pr = ops[w[1]]
                        e.wait_ge(esem[pr.eng], pr.ctr)
                ins = o.fn(e)
                if o.dma:
                    ins.then_inc(dsem[(o.eng, o.slot)], 16)
                elif o.needs_inc:
                    ins.then_inc(esem[o.eng], 1)
            for (qe, s), u in last_use.items():
                if qe == ename:
                    e.wait_ge(dsem[(qe, s)], 16 * u)

        with nc.Block() as block:
            for ename in engs:
                if per[ename]:
                    getattr(block, ename)(lambda e, ename=ename: run(e, ename))
        st.close()
        self.stats = {e: len(per[e]) for e in engs}


class StopBuild(Exception):
    pass


class Arena:
    def __init__(self, p, name, words):
        self.t = p.sb(name, [128, words], F32)
        self.p = p
        self.words = words
        self.off = 0
        self.marks = []

    def take(self, shape, dtype):
        n = 1
        for s in shape:
            n *= s
        w = n if dtype == F32 else (n + 1) // 2
        assert self.off + w <= self.words, ("arena overflow", self.off, w, self.words)
        ap = self.t[:, self.off:self.off + w]
        self.off += w
        if dtype != F32:
            ap = ap.bitcast(dtype)[:, 0:n]
        if len(shape) == 2:
            ap = ap.rearrange("p (a b) -> p a b", a=shape[0])
        elif len(shape) == 3:
            ap = ap.rearrange("p (a b c) -> p a b c", a=shape[0], b=shape[1])
        elif len(shape) == 4:
            ap = ap.rearrange("p (a b c d) -> p a b c d", a=shape[0], b=shape[1], c=shape[2])
        return ap

    def mark(self):
        self.marks.append(self.off)

    def release(self):
        self.off = self.marks.pop()
        self.p.barrier()


def token_groups(include_ctx=True):
    g = []
    if include_ctx:
        g.append((0, CL, 1))
    for i in range(L // 512):
        g.append((CL + i * 512, 512, 0))
    return g


def build(opts=None):
    opts = opts or {}
    mixers = opts.get("mixers", True)
    nc = bass.Bass("TRN2", target_bir_lowering=False)
    dr = {}

    def din(name, shape, dtype=F32):
        dr[name] = nc.dram_tensor(name, list(shape), dtype, kind="ExternalInput").ap()
        return dr[name]

    xin = din("xin", [D, T])
    ccols = din("ccols", [128, 8, 2])
    adab = din("adab", [128, DEPTH, 48])
    gmix = din("gmix", [128, DEPTH, 8])
    gffn = din("gffn", [128, DEPTH, 8])
    gfin = din("gfin", [128, 8])
    ada_w = din("ada_w", [DEPTH, D, 6 * D])
    ffn_w1 = din("ffn_w1", [DEPTH, D, DFF])
    ffn_w2 = din("ffn_w2", [DEPTH, DFF, D])
    ident_in = din("ident", [128, 128])
    s5ar = din("s5ar", [2, 128, 64])
    s5ai = din("s5ai", [2, 128, 64])
    s5ldt = din("s5ldt", [2, 128, 64])
    s5b = din("s5b", [2, 128, 2, 64, 16])
    s5c = din("s5c", [2, 128, 2, 64, 16])
    s5d_in = din("s5d", [128, 2, 8])
    expo = din("expo", [128, 35])
    parm_in = din("parm", [128, 2])
    dmask_in = din("dmask", [128, 8, 16])
    s5_w_glu = din("s5_w_glu", [2, D, 2 * D])
    attn_w_qkv = din("attn_w_qkv", [2, D, 1536])
    attn_w_o = din("attn_w_o", [2, D, D])
    rope_c = din("rope_c", [128, L])
    rope_s = din("rope_s", [128, L])
    bd_in = din("bd", [128, 128])
    prot_in = din("prot", [128, 128])
    qkg = din("qkg", [128, 2, 2])
    outT = nc.dram_tensor("outT", [D, L], F32, kind="ExternalOutput").ap()
    dbg = nc.dram_tensor("dbg", [128, 20480], F32, kind="ExternalOutput").ap() if opts.get("debug") else None
    dbg_off = [0]

    def dump(ap2d, n, keys):
        if dbg is None:
            return
        p.dma("sync", dbg[:, dbg_off[0]:dbg_off[0] + n], ap2d, reads=keys)
        print("dump at", dbg_off[0], n, keys)
        dbg_off[0] += n

    p = Prog(nc)
    ar = Arena(p, "arena", 53200)
    psb = [p.ps("psb%d" % i, [128, 512], F32) for i in range(8)]
    PSK = ["psb%d" % i for i in range(8)]

    xT = ar.take([8, T], F32)
    modT = ar.take([DEPTH, 48, 2], F32)
    adab_s = ar.take([DEPTH, 48], F32)
    gmix_s = ar.take([DEPTH, 8], F32)
    gffn_s = ar.take([DEPTH, 8], F32)
    gfin_s = ar.take([8], F32)
    gs1 = ar.take([DEPTH, 8, 2], F32)
    gs2 = ar.take([DEPTH, 8, 2], F32)
    cc_s = ar.take([8, 2], F32)
    sc_bf = ar.take([8, 2], BF)
    ones_bf = ar.take([128], BF)
    ident_f = ar.take([128], F32)
    ident_bf = ar.take([128], BF)

    s5d_s = ar.take([2, 8], F32)
    parm_s = ar.take([2], F32)
    dmask_s = ar.take([8, 16], F32)
    p.dma("sync", s5d_s, s5d_in, writes=["s5d_s"])
    p.dma("sync", parm_s, parm_in, writes=["parm_s"])
    p.dma("sync", dmask_s, dmask_in, writes=["dmask_s"])
    for k in range(8):
        p.dma("sync", xT[:, k, :], xin[k * 128:(k + 1) * 128, :], writes=[("xT", k)])
    p.dma("sync", cc_s, ccols, writes=["cc_s"])
    p.dma("sync", adab_s, adab, writes=["adab_s"])
    p.dma("sync", gmix_s, gmix, writes=["gmix_s"])
    p.dma("sync", gffn_s, gffn, writes=["gffn_s"])
    p.dma("sync", gfin_s, gfin, writes=["gfin_s"])
    p.dma("sync", ident_f, ident_in, writes=["ident_f"])
    p.op("vector", lambda e: e.memset(ones_bf, 1.0), writes=["ones_bf"])
    p.op("vector", lambda e: e.tensor_copy(out=ident_bf, in_=ident_f), reads=["ident_f"], writes=["ident_bf"])
    p.op("scalar", lambda e: e.activation(out=sc_bf, in_=cc_s, func=AF.Silu), reads=["cc_s"], writes=["sc_bf"])

    def mod_phase(i):
        ar.mark()
        wA = [ar.take([8, 1024], BF) for _ in range(2)]
        pm = psb[7]
        for piece in range(6):
            wa = wA[piece % 2]
            wk = ("wA", piece % 2)
            p.dma("gpsimd", wa, ada_w[i][:, piece * 1024:(piece + 1) * 1024].rearrange("(k p) n -> p k n", p=128),
                  writes=[wk])
            for n in range(8):
                j = piece * 8 + n
                for k in range(8):
                    p.op("tensor", lambda e, wa=wa, k=k, n=n, j=j: e.matmul(
                        pm[:, 2 * j:2 * j + 2], lhsT=wa[:, k, n * 128:(n + 1) * 128], rhs=sc_bf[:, k, :],
                        start=(k == 0), stop=(k == 7)), reads=[wk, "sc_bf"], writes=[PSK[7]])
        p.op("vector", lambda e: e.tensor_tensor(
            out=modT[:, i], in0=pm[:, 0:96].rearrange("p (j c) -> p j c", c=2),
            in1=adab_s[:, i].unsqueeze(2).broadcast_to([128, 48, 2]), op=ALU.add),
            reads=[PSK[7], "adab_s"], writes=[("modT", i)])
        for (dst, dk, g_s, gk, which) in ((gs1, "gs1", gmix_s, "gmix_s", 1), (gs2, "gs2", gffn_s, "gffn_s", 4)):
            p.op("vector", lambda e, dst=dst, g_s=g_s, which=which: e.scalar_tensor_tensor(
                out=dst[:, i], in0=modT[:, i, which * 8:(which + 1) * 8, :], scalar=1.0,
                in1=g_s[:, i].unsqueeze(2).broadcast_to([128, 8, 2]), op0=ALU.add, op1=ALU.mult),
                reads=[("modT", i), gk], writes=[(dk, i)])
        ar.release()

    def mod_finish(i):
        pm = psb[7]
        p.op("vector", lambda e: e.tensor_tensor(
            out=modT[:, i], in0=pm[:, 0:96].rearrange("p (j c) -> p j c", c=2),
            in1=adab_s[:, i].unsqueeze(2).broadcast_to([128, 48, 2]), op=ALU.add),
            reads=[PSK[7], "adab_s"], writes=[("modT", i)])
        for (dst, dk, g_s, gk, which) in ((gs1, "gs1", gmix_s, "gmix_s", 1), (gs2, "gs2", gffn_s, "gffn_s", 4)):
            p.op("vector", lambda e, dst=dst, g_s=g_s, which=which: e.scalar_tensor_tensor(
                out=dst[:, i], in0=modT[:, i, which * 8:(which + 1) * 8, :], scalar=1.0,
                in1=g_s[:, i].unsqueeze(2).broadcast_to([128, 8, 2]), op0=ALU.add, op1=ALU.mult),
                reads=[("modT", i), gk], writes=[(dk, i)])

    def mod_side(i, bufs):
        pm = psb[7]
        pieces = []
        for jn in range(48):
            def piece(jn=jn):
                wa = bufs[jn % len(bufs)]
                wk = ("wAs", jn % len(bufs))
                p.dma("gpsimd", wa, ada_w[i][:, jn * 128:(jn + 1) * 128].rearrange("(k p) n -> p k n", p=128), writes=[wk])
                for k in range(8):
                    p.op("tensor", lambda e, wa=wa, k=k: e.matmul(pm[:, 2 * jn:2 * jn + 2], lhsT=wa[:, k, :], rhs=sc_bf[:, k, :],
                                                               start=(k == 0), stop=(k == 7)), reads=[wk, "sc_bf"], writes=[PSK[7]])
            pieces.append(piece)
        return pieces

    def mcol(i, which, k, c):
        return modT[:, i, which * 8 + k, c:c + 1]

    def norm_phase(hT, hkey, gcol, shcol, groups, rkeys):
        ar.mark()
        sq = ar.take([8, 512], BF)
        rstd = ar.take([512], F32)
        tmp = [ar.take([512], F32) for _ in range(2)]
        pss = psb[6]
        for gi, (t0, w, c) in enumerate(groups):
            for k in range(8):
                p.op("scalar", lambda e, k=k, t0=t0, w=w: e.activation(out=sq[:, k, :w], in_=xT[:, k, t0:t0 + w], func=AF.Square),
                     reads=[("xT", k)], writes=[("sq", k)])
            for k in range(8):
                p.op("tensor", lambda e, k=k, w=w: e.matmul(pss[:, :w], lhsT=ones_bf, rhs=sq[:, k, :w], start=(k == 0), stop=(k == 7)),
                     reads=[("sq", k), "ones_bf"], writes=[PSK[6]])
            p.op("scalar", lambda e, w=w: e.activation(out=rstd[:, :w], in_=pss[:, :w], func=AF.Sqrt, bias=EPS, scale=1.0 / D),
                 reads=[PSK[6]], writes=["rstd"])
            p.op("vector", lambda e, w=w: e.reciprocal(out=rstd[:, :w], in_=rstd[:, :w]), reads=["rstd"], writes=["rstd"])
            for k in range(8):
                tm = tmp[k % 2]
                p.op("vector", lambda e, k=k, t0=t0, w=w, c=c, tm=tm: e.scalar_tensor_tensor(
                    out=tm[:, :w], in0=xT[:, k, t0:t0 + w], scalar=gcol(k, c), in1=rstd[:, :w], op0=ALU.mult, op1=ALU.mult),
                    reads=[("xT", k), "rstd"] + rkeys, writes=[("ntmp", k % 2)])
                if shcol is None:
                    p.op("scalar", lambda e, k=k, t0=t0, w=w, tm=tm: e.activation(out=hT[:, k, t0:t0 + w], in_=tm[:, :w], func=AF.Identity),
                         reads=[("ntmp", k % 2)], writes=[(hkey, k, gi)])
                else:
                    p.op("scalar", lambda e, k=k, t0=t0, w=w, c=c, tm=tm: e.activation(
                        out=hT[:, k, t0:t0 + w], in_=tm[:, :w], func=AF.Identity, bias=shcol(k, c)),
                        reads=[("ntmp", k % 2)] + rkeys, writes=[(hkey, k, gi)])
        ar.release()

    def ffn_phase(i, hT, hkey, groups, side_layer=None):
        ar.mark()
        side = []
        if side_layer is not None:
            side = mod_side(side_layer, [ar.take([8, 128], BF) for _ in range(4)])
        nblk = 0
        w1q = [ar.take([8, 1024], BF) for _ in range(2)]
        w2q = [ar.take([8, 1024], BF) for _ in range(2)]
        aT = [ar.take([8, 512], BF) for _ in range(2)]
        rt = [ar.take([512], BF) for _ in range(2)]
        cnt = 0
        for q in range(4):
            b = q % 2
            p.dma("gpsimd", w1q[b], ffn_w1[i][:, q * 1024:(q + 1) * 1024].rearrange("(k p) n -> p k n", p=128), writes=[("w1q", b)])
            p.dma("gpsimd", w2q[b], ffn_w2[i][q * 1024:(q + 1) * 1024, :].rearrange("(f p) n -> p f n", p=128), writes=[("w2q", b)])
            for gi, (t0, w, c) in enumerate(groups):
                ab = cnt % 2
                cnt += 1
                a_t = aT[ab]
                for f in range(8):
                    ps = psb[f % 2]
                    for k in range(8):
                        p.op("tensor", lambda e, ps=ps, b=b, k=k, f=f, t0=t0, w=w: e.matmul(
                            ps[:, :w], lhsT=w1q[b][:, k, f * 128:(f + 1) * 128], rhs=hT[:, k, t0:t0 + w],
                            start=(k == 0), stop=(k == 7)), reads=[("w1q", b), (hkey, k, gi)], writes=[PSK[f % 2]])
                    r_t = rt[f % 2]
                    p.op("scalar", lambda e, ps=ps, r_t=r_t, w=w: e.activation(out=r_t[:, :w], in_=ps[:, :w], func=AF.Relu),
                         reads=[PSK[f % 2]], writes=[("rt", f % 2)])
                    p.op("vector", lambda e, r_t=r_t, a_t=a_t, f=f, w=w: e.tensor_tensor(out=a_t[:, f, :w], in0=r_t[:, :w], in1=r_t[:, :w], op=ALU.mult),
                         reads=[("rt", f % 2)], writes=[("aT", ab, f)])
                    nblk += 1
                    if side and nblk % 3 == 0:
                        side.pop(0)()
                for d in range(8):
                    ps2 = psb[2 + d % 2]
                    for f in range(8):
                        p.op("tensor", lambda e, ps2=ps2, b=b, f=f, d=d, a_t=a_t, w=w: e.matmul(
                            ps2[:, :w], lhsT=w2q[b][:, f, d * 128:(d + 1) * 128], rhs=a_t[:, f, :w],
                            start=(f == 0), stop=(f == 7)), reads=[("w2q", b), ("aT", ab, f)], writes=[PSK[2 + d % 2]])
                    p.op("vector", lambda e, ps2=ps2, d=d, t0=t0, w=w, c=c: e.scalar_tensor_tensor(
                        out=xT[:, d, t0:t0 + w], in0=ps2[:, :w], scalar=mcol(i, 5, d, c), in1=xT[:, d, t0:t0 + w],
                        op0=ALU.mult, op1=ALU.add), reads=[PSK[2 + d % 2], ("modT", i), ("xT", d)], writes=[("xT", d)])
        while side:
            side.pop(0)()
        if side_layer is not None:
            mod_finish(side_layer)
        ar.release()

    TWO_PI = 2.0 * math.pi
    MAGIC = 12582912.0
    CW1 = 6.28125
    CW2 = 0.0019350051879882812
    CW3 = TWO_PI - CW1 - CW2
    PI_LO = 3.1415925

    def s5_phase(i):
        j = i // 2
        V = "vector"
        G = "gpsimd"
        A = "scalar"
        ar.mark()
        PCr = ar.take([64, 17], F32)
        PCi = ar.take([64, 17], F32)
        BBb = ar.take([2, 64, 16], BF)
        A1v = ar.take([2, 64], F32)
        A2v = ar.take([2, 64], F32)
        NGH = 64
        VH = ar.take([2, NGH, NCH + 1], BF)
        ar.mark()
        PBr = ar.take([64, 16], F32)
        PBi = ar.take([64, 16], F32)
        ar.mark()
        ar_s = ar.take([64], F32); ai_s = ar.take([64], F32); ldt_s = ar.take([64], F32)
        ex_s = ar.take([35], F32)
        p.dma("sync", ar_s, s5ar[j], writes=["ar_s"])
        p.dma("sync", ai_s, s5ai[j], writes=["ai_s"])
        p.dma("sync", ldt_s, s5ldt[j], writes=["ldt_s"])
        p.dma("sync", ex_s, expo, writes=["ex_s"])
        dt = ar.take([64], F32); dar = ar.take([64], F32); th = ar.take([64], F32)
        p.op(A, lambda e: e.activation(out=dt, in_=ldt_s, func=AF.Exp), reads=["ldt_s"], writes=["dt"])
        p.op(V, lambda e: e.tensor_tensor(out=dar, in0=dt, in1=ar_s, op=ALU.mult), reads=["dt", "ar_s"], writes=["dar"])
        p.op(V, lambda e: e.tensor_tensor(out=th, in0=dt, in1=ai_s, op=ALU.mult), reads=["dt", "ai_s"], writes=["th"])
        GH = 32
        b_s = ar.take([2, GH, 16], F32)
        lm = ar.take([GH, 35], F32); ang = ar.take([GH, 35], F32); kk = ar.take([GH, 35], F32); rc = ar.take([GH, 35], F32)
        den = ar.take([GH], F32); t0_ = ar.take([GH], F32); nr = ar.take([GH], F32); fr = ar.take([GH], F32); fi = ar.take([GH], F32)
        tb1 = ar.take([GH, 16], F32); tb2 = ar.take([GH, 16], F32)
        for g2h in range(2):
            gs_ = slice(GH * g2h, GH * g2h + GH)
            p.dma("sync", b_s, s5b[j][:, :, gs_, :], writes=["b_s"])
            exb = ex_s.unsqueeze(1).broadcast_to([128, GH, 35])
            p.op(V, lambda e, gs_=gs_, exb=exb: e.tensor_tensor(out=lm, in0=dar[:, gs_].unsqueeze(2).broadcast_to([128, GH, 35]), in1=exb, op=ALU.mult),
                 reads=["dar", "ex_s"], writes=["lm"])
            p.op(V, lambda e, gs_=gs_, exb=exb: e.tensor_tensor(out=ang, in0=th[:, gs_].unsqueeze(2).broadcast_to([128, GH, 35]), in1=exb, op=ALU.mult),
                 reads=["th", "ex_s"], writes=["ang"])
            p.op(A, lambda e: e.activation(out=lm, in_=lm, func=AF.Exp), reads=["lm"], writes=["lm"])
            p.op(V, lambda e: e.tensor_scalar(out=kk, in0=ang, scalar1=1.0 / TWO_PI, scalar2=MAGIC, op0=ALU.mult, op1=ALU.add), reads=["ang"], writes=["kk"])
            p.op(V, lambda e: e.tensor_scalar(out=kk, in0=kk, scalar1=-MAGIC, scalar2=None, op0=ALU.add), reads=["kk"], writes=["kk"])
            p.op(V, lambda e: e.scalar_tensor_tensor(out=ang, in0=kk, scalar=-CW1, in1=ang, op0=ALU.mult, op1=ALU.add), reads=["kk", "ang"], writes=["ang"])
            p.op(V, lambda e: e.scalar_tensor_tensor(out=ang, in0=kk, scalar=-CW2, in1=ang, op0=ALU.mult, op1=ALU.add), reads=["kk", "ang"], writes=["ang"])
            p.op(V, lambda e: e.scalar_tensor_tensor(out=ang, in0=kk, scalar=-CW3, in1=ang, op0=ALU.mult, op1=ALU.add), reads=["kk", "ang"], writes=["ang"])
            p.op(V, lambda e: e.tensor_scalar(out=ang, in0=ang, scalar1=PI_LO, scalar2=-PI_LO, op0=ALU.min, op1=ALU.max), reads=["ang"], writes=["ang"])
            p.op(V, lambda e: e.tensor_scalar(out=kk, in0=ang, scalar1=math.pi / 2, scalar2=-TWO_PI, op0=ALU.is_gt, op1=ALU.mult), reads=["ang"], writes=["kk"])
            p.op(V, lambda e: e.scalar_tensor_tensor(out=rc, in0=ang, scalar=math.pi / 2, in1=kk, op0=ALU.add, op1=ALU.add), reads=["ang", "kk"], writes=["rc"])
            p.op(V, lambda e: e.tensor_scalar(out=rc, in0=rc, scalar1=PI_LO, scalar2=-PI_LO, op0=ALU.min, op1=ALU.max), reads=["rc"], writes=["rc"])
            p.op(A, lambda e: e.activation(out=ang, in_=ang, func=AF.Sin), reads=["ang"], writes=["ang"])
            p.op(A, lambda e: e.activation(out=rc, in_=rc, func=AF.Sin), reads=["rc"], writes=["rc"])
            p.op(V, lambda e: e.tensor_tensor(out=rc, in0=lm, in1=rc, op=ALU.mult), reads=["lm", "rc"], writes=["rc"])
            p.op(V, lambda e: e.tensor_tensor(out=ang, in0=lm, in1=ang, op=ALU.mult), reads=["lm", "ang"], writes=["ang"])
            p.op(V, lambda e, gs_=gs_: e.tensor_copy(out=PCr[:, gs_, :], in_=rc[:, :, 0:17]), reads=["rc"], writes=["PCr"])
            p.op(V, lambda e, gs_=gs_: e.tensor_copy(out=PCi[:, gs_, :], in_=ang[:, :, 0:17]), reads=["ang"], writes=["PCi"])
            p.op(V, lambda e, gs_=gs_: e.tensor_copy(out=PBr[:, gs_, :], in_=rc[:, :, 17:33]), reads=["rc"], writes=["PBr"])
            p.op(V, lambda e, gs_=gs_: e.tensor_copy(out=PBi[:, gs_, :], in_=ang[:, :, 17:33]), reads=["ang"], writes=["PBi"])
            a1r = rc[:, :, 33]
            a1i = ang[:, :, 33]
            ars = ar_s[:, gs_]
            ais = ai_s[:, gs_]
            p.op(V, lambda e, ars=ars: e.tensor_tensor(out=den, in0=ars, in1=ars, op=ALU.mult), reads=["ar_s"], writes=["den"])
            p.op(V, lambda e, ais=ais: e.tensor_tensor(out=t0_, in0=ais, in1=ais, op=ALU.mult), reads=["ai_s"], writes=["t0_"])
            p.op(V, lambda e: e.tensor_tensor(out=den, in0=den, in1=t0_, op=ALU.add), reads=["den", "t0_"], writes=["den"])
            p.op(V, lambda e: e.reciprocal(out=den, in_=den), reads=["den"], writes=["den"])
            p.op(V, lambda e, a1r=a1r: e.tensor_scalar(out=nr, in0=a1r, scalar1=-1.0, scalar2=None, op0=ALU.add), reads=["rc"], writes=["nr"])
            p.op(V, lambda e, ars=ars: e.tensor_tensor(out=fr, in0=nr, in1=ars, op=ALU.mult), reads=["nr", "ar_s"], writes=["fr"])
            p.op(V, lambda e, a1i=a1i, ais=ais: e.tensor_tensor(out=t0_, in0=a1i, in1=ais, op=ALU.mult), reads=["ang", "ai_s", "den"], writes=["t0_"])
            p.op(V, lambda e: e.tensor_tensor(out=fr, in0=fr, in1=t0_, op=ALU.add), reads=["fr", "t0_"], writes=["fr"])
            p.op(V, lambda e: e.tensor_tensor(out=fr, in0=fr, in1=den, op=ALU.mult), reads=["fr", "den"], writes=["fr"])
            p.op(V, lambda e, a1i=a1i, ars=ars: e.tensor_tensor(out=fi, in0=a1i, in1=ars, op=ALU.mult), reads=["ang", "ar_s"], writes=["fi"])
            p.op(V, lambda e, ais=ais: e.tensor_tensor(out=t0_, in0=nr, in1=ais, op=ALU.mult), reads=["nr", "ai_s", "fr"], writes=["t0_"])
            p.op(V, lambda e: e.tensor_tensor(out=fi, in0=fi, in1=t0_, op=ALU.subtract), reads=["fi", "t0_"], writes=["fi"])
            p.op(V, lambda e: e.tensor_tensor(out=fi, in0=fi, in1=den, op=ALU.mult), reads=["fi", "den"], writes=["fi"])
            frb = fr.unsqueeze(2).broadcast_to([128, GH, 16])
            fib = fi.unsqueeze(2).broadcast_to([128, GH, 16])
            p.op(V, lambda e, frb=frb: e.tensor_tensor(out=tb1, in0=b_s[:, 0], in1=frb, op=ALU.mult), reads=["b_s", "fr"], writes=["tb1"])
            p.op(V, lambda e, fib=fib: e.tensor_tensor(out=tb2, in0=b_s[:, 1], in1=fib, op=ALU.mult), reads=["b_s", "fi"], writes=["tb2"])
            p.op(V, lambda e, gs_=gs_: e.tensor_tensor(out=BBb[:, 0, gs_, :], in0=tb1, in1=tb2, op=ALU.subtract), reads=["tb1", "tb2"], writes=["BBb"])
            p.op(V, lambda e, frb=frb: e.tensor_tensor(out=tb1, in0=b_s[:, 1], in1=frb, op=ALU.mult), reads=["b_s", "fr", "BBb"], writes=["tb1"])
            p.op(V, lambda e, fib=fib: e.tensor_tensor(out=tb2, in0=b_s[:, 0], in1=fib, op=ALU.mult), reads=["b_s", "fi", "BBb"], writes=["tb2"])
            p.op(V, lambda e, gs_=gs_: e.tensor_tensor(out=BBb[:, 1, gs_, :], in0=tb1, in1=tb2, op=ALU.add), reads=["tb1", "tb2"], writes=["BBb"])
            p.op(V, lambda e, gs_=gs_: e.tensor_copy(out=A1v[:, 0, gs_], in_=rc[:, :, 34]), reads=["rc"], writes=["A1v"])
            p.op(V, lambda e, gs_=gs_: e.tensor_copy(out=A1v[:, 1, gs_], in_=rc[:, :, 34]), reads=["rc", "A1v"], writes=["A1v"])
            p.op(V, lambda e, gs_=gs_: e.tensor_scalar(out=A2v[:, 0, gs_], in0=ang[:, :, 34], scalar1=-1.0, scalar2=None, op0=ALU.mult), reads=["ang"], writes=["A2v"])
            p.op(V, lambda e, gs_=gs_: e.tensor_copy(out=A2v[:, 1, gs_], in_=ang[:, :, 34]), reads=["ang", "A2v"], writes=["A2v"])
        ar.release()

        norm_phase(hT, "hT", lambda k, c: gs1[:, i, k, c:c + 1], lambda k, c: mcol(i, 0, k, c), token_groups(True),
                   [("gs1", i), ("modT", i)])
        hkeys = lambda k: [("hT", k, gi) for gi in range(5)]

        for half in range(1):
            ks = range(8)
            p.op(G, lambda e: e.memset(VH[0:64, :, :, 0:1], 0.0), writes=["VH"])
            p.op(G, lambda e: e.memset(VH[64:128, :, :, NCH:NCH + 1], 0.0), reads=["VH"], writes=["VH"])
            ar.mark()
            ABb = [ar.take([2, 16, 8, 16], BF) for _ in range(2)]
            ABt = ar.take([2, 16, 2, 128], BF)
            vt1 = ar.take([16, 2, 16], F32)
            vt2 = ar.take([16, 2, 16], F32)
            vcnt = 0
            def gen_ab(k):
                AB_ = ABb[k % 2]
                abk = "AB%d" % (k % 2)
                for hh in range(4):
                    gsl = slice(8 * k + 2 * hh, 8 * k + 2 * hh + 2)
                    pbr = PBr[:, gsl, :].rearrange("p g s -> p s g").unsqueeze(3).broadcast_to([128, 16, 2, 16])
                    pbi = PBi[:, gsl, :].rearrange("p g s -> p s g").unsqueeze(3).broadcast_to([128, 16, 2, 16])
                    bbr = BBb[:, 0, gsl, :].unsqueeze(1).broadcast_to([128, 16, 2, 16])
                    bbi = BBb[:, 1, gsl, :].unsqueeze(1).broadcast_to([128, 16, 2, 16])
                    abr = AB_[:, 0, :, 2 * hh:2 * hh + 2, :]
                    abi = AB_[:, 1, :, 2 * hh:2 * hh + 2, :]
                    p.op(G, lambda e, pbr=pbr, bbr=bbr: e.tensor_tensor(out=vt1, in0=pbr, in1=bbr, op=ALU.mult), reads=["PBr", "BBb"], writes=["vt1"])
                    p.op(G, lambda e, pbi=pbi, bbi=bbi: e.tensor_tensor(out=vt2, in0=pbi, in1=bbi, op=ALU.mult), reads=["PBi", "BBb"], writes=["vt2"])
                    p.op(G, lambda e, abr=abr: e.tensor_tensor(out=abr, in0=vt1, in1=vt2, op=ALU.subtract), reads=["vt1", "vt2"], writes=[abk])
                    p.op(G, lambda e, pbr=pbr, bbi=bbi: e.tensor_tensor(out=vt1, in0=pbr, in1=bbi, op=ALU.mult), reads=["PBr", "BBb"], writes=["vt1"])
                    p.op(G, lambda e, pbi=pbi, bbr=bbr: e.tensor_tensor(out=vt2, in0=pbi, in1=bbr, op=ALU.mult), reads=["PBi", "BBb"], writes=["vt2"])
                    p.op(G, lambda e, abi=abi: e.tensor_tensor(out=abi, in0=vt1, in1=vt2, op=ALU.add), reads=["vt1", "vt2"], writes=[abk])

            gen_ab(0)
            for k in ks:
                AB = ABb[k % 2]
                ABK = "AB%d" % (k % 2)
                if k + 1 < 8:
                    gen_ab(k + 1)
                for ri in range(2):
                    for sb_ in range(2):
                        bank = 6 + (2 * ri + sb_) % 2
                        pbf = psb[bank].bitcast(BF)
                        p.pe_fence()
                        for s8 in range(8):
                            s = sb_ * 8 + s8
                            p.op("tensor", lambda e, pbf=pbf, ri=ri, s=s, s8=s8, AB=AB: e.transpose(
                                pbf[:, s8 * 128:(s8 + 1) * 128], AB[:, ri, s, :, :].rearrange("p g h -> p (g h)"), ident_bf),
                                reads=[ABK, "ident_bf"], writes=[PSK[bank]])
                        for g2 in range(2):
                            eng = A if g2 == 0 else V
                            src = pbf[:, 0:1024].rearrange("p (s q) -> p s q", s=8)
                            dst = ABt[:, g2, sb_ * 8:sb_ * 8 + 8, ri, :]
                            if g2 == 0:
                                p.op(A, lambda e, src=src, dst=dst, g2=g2: e.activation(out=dst, in_=src, func=AF.Identity, scale=parm_s[:, g2:g2 + 1]),
                                     reads=[PSK[bank], "parm_s"], writes=["ABt", PSK[bank]])
                            else:
                                p.op(V, lambda e, src=src, dst=dst, g2=g2: e.tensor_scalar(out=dst, in0=src, scalar1=parm_s[:, g2:g2 + 1], scalar2=None, op0=ALU.mult),
                                     reads=[PSK[bank], "parm_s"], writes=["ABt", PSK[bank]])
                for g2 in range(2):
                    for ri in range(2):
                        p.pe_fence()
                        for s in range(16):
                            for q in range(4):
                                p.op("tensor", lambda e, q=q, g2=g2, s=s, ri=ri, k=k: e.matmul(
                                    psb[q][:, 0:NCH], lhsT=ABt[32 * q:32 * q + 32, g2, s, ri, :], rhs=hT[32 * q:32 * q + 32, k, s:T:Q],
                                    start=(s == 0), stop=(s == 15), tile_position=(32 * q, 0)),
                                    reads=["ABt"] + hkeys(k), writes=[PSK[q]])
                        for q in range(4):
                            gl = 2 * q + g2
                            gh = 8 * k + gl
                            ps = psb[q]
                            p.op(A, lambda e, ps=ps, ri=ri, gh=gh: e.activation(out=VH[0:64, ri, gh, 1:NCH + 1], in_=ps[0:64, 0:NCH], func=AF.Identity),
                                 reads=[PSK[q]], writes=[("VH", ri, gh, 0)])
                            p.op(V, lambda e, ps=ps, ri=ri, gh=gh: e.tensor_copy(out=VH[64:128, ri, gh, 0:NCL], in_=ps[64:128, NCC:NCH]),
                                 reads=[PSK[q]], writes=[("VH", ri, gh, 1)])
                            p.op(V, lambda e, ps=ps, ri=ri, gh=gh: e.tensor_copy(out=VH[64:128, ri, gh, NCL:NCH], in_=ps[64:128, 0:NCC]),
                                 reads=[PSK[q]], writes=[("VH", ri, gh, 2)])
            ar.release()
            ar.release()
            vh_all = ["VH"] + [("VH", ri, gh, x) for ri in range(2) for gh in range(NGH) for x in range(3)]
            if opts.get("s5_stop") == 2:
                vdb = ar.take([2, 4, NCH + 1], F32)
                p.op(V, lambda e: e.tensor_copy(out=vdb, in_=VH[:, :, 0:4, :]), reads=vh_all, writes=["vdb"])
                dump(vdb.rearrange("p a g c -> p (a g c)"), 2 * 4 * (NCH + 1), ["vdb"])
                raise StopBuild()
            ar.mark()
            Hc = [ar.take([2, NGH], F32) for _ in range(2)]
            sP1 = ar.take([2, NGH], F32)
            sP2 = ar.take([2, NGH], F32)
            a1h = A1v
            a2h = A2v
            first = True
            SCAN_ENG = {"f": V, "r": V}
            for st in range(NCH - 1):
                cur = Hc[st % 2]
                nxt = Hc[(st + 1) % 2]
                chains = ((0, 64, st + 1, "f"), (64, 128, NCH - 1 - st, "r"))
                rk = vh_all if first else []
                if st == 0:
                    for (lo, hi_, col, tag) in chains:
                        vcol = VH[lo:hi_, :, :, col]
                        p.op(SCAN_ENG[tag], lambda e, nxt=nxt, lo=lo, hi_=hi_, vcol=vcol: e.tensor_copy(out=nxt[lo:hi_], in_=vcol),
                             reads=rk, writes=[("Hc", (st + 1) % 2, tag)])
                    first = False
                    continue
                for stage in range(6):
                    for (lo, hi_, col, tag) in chains:
                        eng = SCAN_ENG[tag]
                        vcol = VH[lo:hi_, :, :, col]
                        if stage == 0:
                            p.op(eng, lambda e, cur=cur, lo=lo, hi_=hi_, a1h=a1h: e.tensor_tensor(out=sP1[lo:hi_], in0=cur[lo:hi_], in1=a1h[lo:hi_], op=ALU.mult),
                                 reads=[("Hc", st % 2, tag), "A1v"], writes=[("sP1", tag)])
                        elif stage == 1:
                            p.op(eng, lambda e, cur=cur, lo=lo, hi_=hi_, a2h=a2h: e.tensor_tensor(out=sP2[lo:hi_], in0=cur[lo:hi_, ::-1, :], in1=a2h[lo:hi_], op=ALU.mult),
                                 reads=[("Hc", st % 2, tag), "A2v"], writes=[("sP2", tag, 0), ("sP2", tag, 1)])
                        elif stage == 2:
                            continue
                        elif stage == 3:
                            p.op(eng, lambda e, lo=lo, hi_=hi_: e.tensor_tensor(out=sP1[lo:hi_], in0=sP1[lo:hi_], in1=sP2[lo:hi_], op=ALU.add),
                                 reads=[("sP1", tag), ("sP2", tag, 0), ("sP2", tag, 1)], writes=[("sP1", tag)])
                        elif stage == 4:
                            p.op(eng, lambda e, nxt=nxt, lo=lo, hi_=hi_, vcol=vcol: e.tensor_tensor(out=nxt[lo:hi_], in0=sP1[lo:hi_], in1=vcol, op=ALU.add),
                                 reads=[("sP1", tag)], writes=[("Hc", (st + 1) % 2, tag)])
                        else:
                            p.op(A, lambda e, nxt=nxt, lo=lo, hi_=hi_, vcol=vcol: e.activation(out=vcol, in_=nxt[lo:hi_], func=AF.Identity),
                                 reads=[("Hc", (st + 1) % 2, tag)], writes=[("VHc", tag)])
            ar.release()
            vh_done = [("VHc", "f"), ("VHc", "r")] + vh_all
            if opts.get("s5_stop") == 3:
                vdb = ar.take([2, 4, NCH + 1], F32)
                p.op(V, lambda e: e.tensor_copy(out=vdb, in_=VH[:, :, 0:4, :]), reads=vh_done, writes=["vdb"])
                dump(vdb.rearrange("p a g c -> p (a g c)"), 2 * 4 * (NCH + 1), ["vdb"])
                raise StopBuild()
            ar.mark()
            c_kb = [ar.take([2, 8, 16], F32) for _ in range(2)]
            ncr_kb = [ar.take([8, 16], F32) for _ in range(2)]
            Cqb = [ar.take([2, 8, 17, 16], BF) for _ in range(2)]
            BBp = ar.take([2, 8, 128], BF)
            Kc = ar.take([8, 31, 16], BF)
            Yl = ar.take([16, 8, 16], BF)
            Yc = Yl
            ct1 = ar.take([2, 17, 16], F32)
            ct2 = ar.take([2, 17, 16], F32)
            kt = ar.take([16], F32)
            kt2 = ar.take([16], F32)
            ycar = [ar.take([512], F32)] * 2
            ycnt = 0

            def gen_cq(k):
                kb = k % 2
                c_k = c_kb[kb]
                ncr_k = ncr_kb[kb]
                Cq_ = Cqb[kb]
                ck, nk, qk = "c_k%d" % kb, "ncr_k%d" % kb, "Cq%d" % kb
                p.dma("sync", c_k, s5c[j][:, :, 8 * k:8 * k + 8, :], writes=[ck])
                p.op(G, lambda e: e.tensor_scalar(out=ncr_k, in0=c_k[:, 0], scalar1=-1.0, scalar2=None, op0=ALU.mult), reads=[ck], writes=[nk])
                for hh in range(4):
                    gsl = slice(8 * k + 2 * hh, 8 * k + 2 * hh + 2)
                    lsl = slice(2 * hh, 2 * hh + 2)
                    pcr = PCr[:, gsl, :].unsqueeze(3).broadcast_to([128, 2, 17, 16])
                    pci = PCi[:, gsl, :].unsqueeze(3).broadcast_to([128, 2, 17, 16])
                    cr = c_k[:, 0, lsl, :].unsqueeze(2).broadcast_to([128, 2, 17, 16])
                    ci = c_k[:, 1, lsl, :].unsqueeze(2).broadcast_to([128, 2, 17, 16])
                    ncr = ncr_k[:, lsl, :].unsqueeze(2).broadcast_to([128, 2, 17, 16])
                    p.op(G, lambda e, cr=cr, pcr=pcr: e.tensor_tensor(out=ct1, in0=cr, in1=pcr, op=ALU.mult), reads=[ck, "PCr"], writes=["ct1"])
                    p.op(G, lambda e, ci=ci, pci=pci: e.tensor_tensor(out=ct2, in0=ci, in1=pci, op=ALU.mult), reads=[ck, "PCi"], writes=["ct2"])
                    p.op(G, lambda e, lsl=lsl, Cq_=Cq_: e.tensor_tensor(out=Cq_[:, 0, lsl], in0=ct1, in1=ct2, op=ALU.subtract), reads=["ct1", "ct2"], writes=[qk])
                    p.op(G, lambda e, ncr=ncr, pci=pci: e.tensor_tensor(out=ct1, in0=ncr, in1=pci, op=ALU.mult), reads=[nk, "PCi"], writes=["ct1"])
                    p.op(G, lambda e, ci=ci, pcr=pcr: e.tensor_tensor(out=ct2, in0=ci, in1=pcr, op=ALU.mult), reads=[ck, "PCr"], writes=["ct2"])
                    p.op(G, lambda e, lsl=lsl, Cq_=Cq_: e.tensor_tensor(out=Cq_[:, 1, lsl], in0=ct1, in1=ct2, op=ALU.subtract), reads=["ct1", "ct2"], writes=[qk])

            for k in ks:
                kb = k % 2
                Cq = Cqb[kb]
                CQK = "Cq%d" % kb
                if k == ks[0]:
                    gen_cq(k)
                p.op(G, lambda e: e.memset(BBp, 0.0), writes=["BBp"])
                for gl in range(8):
                    p.op(G, lambda e, gl=gl, k=k: e.tensor_copy(out=BBp[:, :, gl, gl * 16:(gl + 1) * 16], in_=BBb[:, :, 8 * k + gl, :]),
                         reads=["BBb", "BBp"], writes=["BBp"])
                if k + 1 < 8:
                    gen_cq(k + 1)
                if opts.get("s5_tsub") == 1:
                    raise StopBuild()
                for gl in range(8):
                    bf_, br_ = (4, 5) if gl % 2 == 0 else (6, 7)
                    for dh, bank in ((0, bf_), (1, br_)):
                        lo = 64 * dh
                        p.pe_fence()
                        for ri in range(2):
                            p.op("tensor", lambda e, bank=bank, lo=lo, gl=gl, ri=ri, Cq=Cq: e.matmul(
                                psb[bank][:, 0:272], lhsT=BBp[lo:lo + 64, ri, gl, :], rhs=Cq[lo:lo + 64, ri, gl].rearrange("p j h -> p (j h)"),
                                start=(ri == 0), stop=(ri == 1)), reads=["BBp", CQK], writes=[PSK[bank]])
                    pf = psb[bf_]
                    pr_ = psb[br_]
                    p.op(A, lambda e, gl=gl, pr_=pr_: e.activation(out=Kc[:, gl, 0:15, :].rearrange("p l h -> p (l h)"), in_=pr_[:, 16:256], func=AF.Identity),
                         reads=[PSK[br_]], writes=[("Kc", gl, 0), PSK[br_]])
                    p.op(A, lambda e, gl=gl, pf=pf: e.activation(out=Kc[:, gl, 16:31, :].rearrange("p l h -> p (l h)"), in_=pf[:, 16:256], func=AF.Identity),
                         reads=[PSK[bf_]], writes=[("Kc", gl, 1), PSK[bf_]])
                    p.op(V, lambda e, gl=gl, k=k: e.tensor_scalar(out=kt2, in0=dmask_s[:, gl, :], scalar1=s5d_s[:, j, k:k + 1], scalar2=None, op0=ALU.mult),
                         reads=["dmask_s", "s5d_s"], writes=["kt2"])
                    p.op(V, lambda e, pf=pf: e.tensor_tensor(out=kt, in0=pf[:, 0:16], in1=kt2, op=ALU.add), reads=[PSK[bf_], "kt2"], writes=["kt", PSK[bf_]])
                    p.op(V, lambda e, gl=gl, pr_=pr_: e.tensor_tensor(out=Kc[:, gl, 15, :], in0=pr_[:, 256:272], in1=kt, op=ALU.add),
                         reads=["kt", PSK[br_]], writes=[("Kc", gl, 2), PSK[br_]])
                for (rows, tok0, fcol, rcol, is_lat) in ((NCC, 0, 0, NCL + 1, False), (NCL, CL, NCC, 1, True)):
                    hk_ = [("hT", k, gi_) for gi_ in range(1, 5)] if is_lat else [("hT", k, 0)]
                    for gp in range(4):
                        kkeys = [("Kc", gl, x) for gl in (2 * gp, 2 * gp + 1) for x in range(3)]
                        bank = ycnt % 2
                        bankb = 2 + ycnt % 2
                        ycnt += 1
                        py = psb[bank]
                        pyb = psb[bankb]
                        p.pe_fence()
                        for s in range(16):
                            p.op("tensor", lambda e, py=py, rows=rows, tok0=tok0, gp=gp, s=s, k=k: e.matmul(
                                py[0:rows, 0:512], lhsT=hT[:, k, tok0 + s:tok0 + rows * Q:Q],
                                rhs=Kc[:, 2 * gp:2 * gp + 2, 15 - s:31 - s, :].rearrange("p g l h -> p g (l h)"),
                                start=(s == 0), stop=(s == 15)), reads=kkeys + hk_, writes=[PSK[bank]])
                        for g2 in range(2):
                            gl = 2 * gp + g2
                            gh = 8 * k + gl
                            p.pe_fence()
                            for ri in range(2):
                                p.op("tensor", lambda e, pyb=pyb, rows=rows, fcol=fcol, ri=ri, gh=gh, gl=gl, g2=g2, Cq=Cq: e.matmul(
                                    pyb[0:rows, 256 * g2:256 * g2 + 256], lhsT=VH[0:64, ri, gh, fcol:fcol + rows], rhs=Cq[0:64, ri, gl, 1:17, :].rearrange("p j h -> p (j h)"),
                                    start=(ri == 0), stop=False), reads=vh_done + [CQK], writes=[PSK[bankb]])
                            p.pe_fence()
                            for ri in range(2):
                                p.op("tensor", lambda e, pyb=pyb, rows=rows, rcol=rcol, ri=ri, gh=gh, gl=gl, g2=g2, Cq=Cq: e.matmul(
                                    pyb[0:rows, 256 * g2:256 * g2 + 256], lhsT=VH[64:128, ri, gh, rcol:rcol + rows], rhs=Cq[64:128, ri, gl, 0:16, :].rearrange("p j h -> p (j h)"),
                                    start=False, stop=(ri == 1)), reads=vh_done + [CQK], writes=[PSK[bankb]])
                        ycs = ycar[0]
                        p.op(A, lambda e, pyb=pyb, rows=rows, ycs=ycs: e.activation(out=ycs[0:rows, :], in_=pyb[0:rows, 0:512], func=AF.Identity),
                             reads=[PSK[bankb]], writes=["ycar"])
                        p.op(V, lambda e, py=py, rows=rows, gp=gp, ycs=ycs: e.tensor_tensor(
                            out=Yl[0:rows, :, 2 * gp:2 * gp + 2, :], in0=py[0:rows, 0:512].rearrange("p (g t h) -> p t g h", g=2, t=16),
                            in1=ycs[0:rows, :].rearrange("p (g t h) -> p t g h", g=2, t=16), op=ALU.add),
                            reads=[PSK[bank], "ycar"], writes=[("Y", 2 * gp), ("Y", 2 * gp + 1)])
                    ylk = [("Y", gl) for gl in range(8)]
                    if is_lat:
                        for tb in range(2):
                            bank = 6 + tb
                            pbf = psb[bank].bitcast(BF)
                            p.pe_fence()
                            for t8 in range(8):
                                t = tb * 8 + t8
                                p.op("tensor", lambda e, pbf=pbf, t=t, t8=t8: e.transpose(pbf[:, t8 * 128:(t8 + 1) * 128], Yl[:, t, :, :].rearrange("p g h -> p (g h)"), ident_bf),
                                     reads=ylk + ["ident_bf"], writes=[PSK[bank]])
                            dst = hT[:, k, CL:T].rearrange("p (c t) -> p t c", t=Q)[:, tb * 8:tb * 8 + 8, :]
                            p.op(A, lambda e, pbf=pbf, dst=dst: e.activation(out=dst, in_=pbf[:, 0:1024].rearrange("p (t c) -> p t c", t=8), func=AF.Gelu_apprx_tanh),
                                 reads=[PSK[bank]] + hk_, writes=hk_ + [("yT", k, tb)])
                    else:
                        bank = 6
                        pbf = psb[bank].bitcast(BF)
                        p.pe_fence()
                        for t in range(16):
                            p.op("tensor", lambda e, pbf=pbf, t=t: e.transpose(pbf[:, t * 16:(t + 1) * 16], Yl[0:NCC, t, :, :].rearrange("p g h -> p (g h)"), ident_bf[0:NCC, 0:NCC]),
                                 reads=ylk + ["ident_bf"], writes=[PSK[bank]])
                        dst = hT[:, k, 0:CL].rearrange("p (c t) -> p t c", t=Q)
                        p.op(A, lambda e, pbf=pbf, dst=dst: e.activation(out=dst, in_=pbf[:, 0:256].rearrange("p (t c) -> p t c", t=16), func=AF.Gelu_apprx_tanh),
                             reads=[PSK[bank]] + hk_, writes=hk_ + [("yT", k, 2)])
            if opts.get("s5_stop") == 5:
                ydb = ar.take([2, 1024], F32)
                p.op(V, lambda e: e.tensor_copy(out=ydb[:, 0, :], in_=hT[:, 4, 0:1024]), reads=[("yT", 4, x) for x in range(3)], writes=["ydb"])
                p.op(V, lambda e: e.tensor_copy(out=ydb[:, 1, :], in_=hT[:, 7, 1280:2304]), reads=[("yT", 7, x) for x in range(3)] + ["ydb"], writes=["ydb"])
                dump(ydb.rearrange("p a t -> p (a t)"), 2048, ["ydb"])
                raise StopBuild()
            if opts.get("s5_stop") == 4:
                ydb = ar.take([2, 1024], F32)
                p.op(V, lambda e: e.tensor_copy(out=ydb[:, 0, :], in_=hT[:, 0, 0:1024]), reads=[("yT", 0, x) for x in range(3)], writes=["ydb"])
                p.op(V, lambda e: e.tensor_copy(out=ydb[:, 1, :], in_=hT[:, 3, 1280:2304]), reads=[("yT", 3, x) for x in range(3)] + ["ydb"], writes=["ydb"])
                dump(ydb.rearrange("p a t -> p (a t)"), 2048, ["ydb"])
                raise StopBuild()
            ar.release()
        ar.release()
        if opts.get("s5_stop") == 6:
            ydb = ar.take([2304], F32)
            for k in range(8):
                p.op(V, lambda e, k=k: e.tensor_copy(out=ydb, in_=hT[:, k, :]), reads=[("yT", k, x) for x in range(3)] + hkeys(k), writes=["ydb"])
                dump(ydb, 2304, ["ydb"])
            raise StopBuild()
        ar.mark()
        wg = ar.take([8, 2048], BF)
        sg = [ar.take([512], F32) for _ in range(2)]
        gt = [ar.take([512], F32) for _ in range(2)]
        p.dma("gpsimd", wg, s5_w_glu[j].rearrange("(k p) n -> p k n", p=128), writes=["wg"])
        ykeys = lambda k: [("yT", k, x) for x in range(3)] + hkeys(k)
        for gi, (t0, w, c) in enumerate(token_groups(True)):
            for d in range(8):
                pa = psb[(2 * d) % 4]
                pb_ = psb[(2 * d + 1) % 4]
                for k in range(8):
                    p.op("tensor", lambda e, pa=pa, k=k, d=d, t0=t0, w=w: e.matmul(pa[:, :w], lhsT=wg[:, k, d * 128:(d + 1) * 128], rhs=hT[:, k, t0:t0 + w],
                                                                               start=(k == 0), stop=(k == 7)), reads=["wg"] + ykeys(k), writes=[PSK[(2 * d) % 4]])
                for k in range(8):
                    p.op("tensor", lambda e, pb_=pb_, k=k, d=d, t0=t0, w=w: e.matmul(pb_[:, :w], lhsT=wg[:, k, 1024 + d * 128:1024 + (d + 1) * 128], rhs=hT[:, k, t0:t0 + w],
                                                                                start=(k == 0), stop=(k == 7)), reads=["wg"] + ykeys(k), writes=[PSK[(2 * d + 1) % 4]])
                s_t = sg[d % 2]
                g_t = gt[d % 2]
                p.op(A, lambda e, pb_=pb_, s_t=s_t, w=w: e.activation(out=s_t[:, :w], in_=pb_[:, :w], func=AF.Sigmoid),
                     reads=[PSK[(2 * d + 1) % 4]], writes=[("sg", d % 2)])
                p.op(V, lambda e, pa=pa, s_t=s_t, g_t=g_t, w=w: e.tensor_tensor(out=g_t[:, :w], in0=pa[:, :w], in1=s_t[:, :w], op=ALU.mult),
                     reads=[PSK[(2 * d) % 4], ("sg", d % 2)], writes=[("gt", d % 2)])
                p.op(V, lambda e, g_t=g_t, d=d, t0=t0, w=w, c=c: e.scalar_tensor_tensor(
                    out=xT[:, d, t0:t0 + w], in0=g_t[:, :w], scalar=mcol(i, 2, d, c), in1=xT[:, d, t0:t0 + w], op0=ALU.mult, op1=ALU.add),
                    reads=[("gt", d % 2), ("modT", i), ("xT", d)], writes=[("xT", d)])
        ar.release()

    def attn_phase(i):
        j = i // 2
        last = i == DEPTH - 1
        V = "vector"
        G = "gpsimd"
        A = "scalar"
        groups = token_groups(True)
        norm_phase(hT, "hT", lambda k, c: gs1[:, i, k, c:c + 1], lambda k, c: mcol(i, 0, k, c), groups,
                   [("gs1", i), ("modT", i)])
        hk = lambda k: [("hT", k, gi) for gi in range(5)]
        hall = [x for k in range(8) for x in hk(k)]
        ar.mark()
        qT = ar.take([8, T], BF)
        kT = ar.take([4, T], BF)
        vS = ar.take([18, 4, 65], BF)
        p.op(G, lambda e: e.memset(kT, 0.0), writes=["kTz"])
        ar.mark()
        cs_c = ar.take([L], BF)
        cs_s = ar.take([L], BF)
        bd = ar.take([128], BF)
        prm = ar.take([128], BF)
        gq = ar.take([2], F32)
        sqb = [ar.take([512], BF) for _ in range(2)]
        rsb = [ar.take([512], F32) for _ in range(2)]
        qnb = [ar.take([512], BF) for _ in range(2)]
        t1b = [ar.take([512], BF) for _ in range(2)]
        t2b = [ar.take([512], BF) for _ in range(2)]
        p.dma("gpsimd", cs_c, rope_c, writes=["cs_c"])
        p.dma("gpsimd", cs_s, rope_s, writes=["cs_s"])
        p.dma("gpsimd", bd, bd_in, writes=["bd"])
        p.dma("gpsimd", prm, prot_in, writes=["prm"])
        p.dma("sync", gq, qkg[:, j, :], writes=["gq"])
        p.op(G, lambda e: e.memset(vS[:, :, :, 64:65], 1.0), writes=["vS1"])
        itc = [0]

        def qk_chunk(wt, wkey, col0, dst, dkey, gidx, dst2=None):
            for gi, (t0, w, c) in enumerate(groups):
                it = itc[0] % 2
                itc[0] += 1
                sq, rs, qn, t1, t2 = sqb[it], rsb[it], qnb[it], t1b[it], t2b[it]
                ksq, krs, kqn, kt1, kt2 = "asq%d" % it, "ars%d" % it, "aqn%d" % it, "at1%d" % it, "at2%d" % it
                bq, bss, brp = it, 2 + 4 * it, 3 + 4 * it
                pq, pss_, prp = psb[bq], psb[bss], psb[brp]
                for k in range(8):
                    p.op("tensor", lambda e, pq=pq, k=k, t0=t0, w=w: e.matmul(pq[:, :w], lhsT=wt[:, k, col0:col0 + 128], rhs=hT[:, k, t0:t0 + w],
                                                                          start=(k == 0), stop=(k == 7)), reads=[wkey, ("hT", k, gi)], writes=[PSK[bq]])
                p.op(A, lambda e, pq=pq, w=w, sq=sq: e.activation(out=sq[:, :w], in_=pq[:, :w], func=AF.Square), reads=[PSK[bq]], writes=[ksq, PSK[bq]])
                p.op("tensor", lambda e, w=w, sq=sq, pss_=pss_: e.matmul(pss_[:, :w], lhsT=bd, rhs=sq[:, :w], start=True, stop=True), reads=["bd", ksq], writes=[PSK[bss]])
                p.op(A, lambda e, w=w, rs=rs, pss_=pss_: e.activation(out=rs[:, :w], in_=pss_[:, :w], func=AF.Sqrt, bias=EPS, scale=1.0), reads=[PSK[bss]], writes=[krs])
                p.op(V, lambda e, w=w, rs=rs: e.reciprocal(out=rs[:, :w], in_=rs[:, :w]), reads=[krs], writes=[krs])
                if c == 1 and dst2 is None:
                    p.op(V, lambda e, pq=pq, t0=t0, w=w, rs=rs: e.scalar_tensor_tensor(out=dst[:, t0:t0 + w], in0=pq[:, :w], scalar=gq[:, gidx:gidx + 1], in1=rs[:, :w],
                                                                                    op0=ALU.mult, op1=ALU.mult), reads=[PSK[bq], krs, "gq"], writes=[(dkey, gi), PSK[bq]])
                elif c == 1:
                    p.op(V, lambda e, pq=pq, t0=t0, w=w, rs=rs: e.scalar_tensor_tensor(out=dst[0:64, t0:t0 + w], in0=pq[0:64, :w], scalar=gq[0:64, gidx:gidx + 1], in1=rs[0:64, :w],
                                                                                    op0=ALU.mult, op1=ALU.mult), reads=[PSK[bq], krs, "gq", "kTz"], writes=[(dkey, gi), PSK[bq]])
                    p.op(V, lambda e, pq=pq, t0=t0, w=w, rs=rs: e.scalar_tensor_tensor(out=dst2[64:128, t0:t0 + w], in0=pq[64:128, :w], scalar=gq[64:128, gidx:gidx + 1], in1=rs[64:128, :w],
                                                                                    op0=ALU.mult, op1=ALU.mult), reads=[PSK[bq], krs, "gq", "kTz"], writes=[(dkey, gi, 1), PSK[bq]])
                else:
                    l0 = t0 - CL
                    p.op(V, lambda e, pq=pq, w=w, rs=rs, qn=qn: e.scalar_tensor_tensor(out=qn[:, :w], in0=pq[:, :w], scalar=gq[:, gidx:gidx + 1], in1=rs[:, :w],
                                                                                    op0=ALU.mult, op1=ALU.mult), reads=[PSK[bq], krs, "gq"], writes=[kqn, PSK[bq]])
                    p.op("tensor", lambda e, w=w, qn=qn, prp=prp: e.matmul(prp[:, :w], lhsT=prm, rhs=qn[:, :w], start=True, stop=True), reads=["prm", kqn], writes=[PSK[brp]])
                    p.op(G, lambda e, w=w, l0=l0, qn=qn, t1=t1: e.tensor_tensor(out=t1[:, :w], in0=qn[:, :w], in1=cs_c[:, l0:l0 + w], op=ALU.mult), reads=[kqn, "cs_c"], writes=[kt1])
                    p.op(V, lambda e, w=w, l0=l0, prp=prp, t2=t2: e.tensor_tensor(out=t2[:, :w], in0=prp[:, :w], in1=cs_s[:, l0:l0 + w], op=ALU.mult), reads=[PSK[brp], "cs_s"], writes=[kt2])
                    if dst2 is None:
                        p.op(V, lambda e, t0=t0, w=w, t1=t1, t2=t2: e.tensor_tensor(out=dst[:, t0:t0 + w], in0=t1[:, :w], in1=t2[:, :w], op=ALU.add), reads=[kt1, kt2], writes=[(dkey, gi)])
                    else:
                        p.op(V, lambda e, t0=t0, w=w, t1=t1, t2=t2: e.tensor_tensor(out=dst[0:64, t0:t0 + w], in0=t1[0:64, :w], in1=t2[0:64, :w], op=ALU.add),
                             reads=[kt1, kt2, "kTz"], writes=[(dkey, gi)])
                        p.op(G, lambda e, t0=t0, w=w, t1=t1, t2=t2: e.tensor_tensor(out=dst2[64:128, t0:t0 + w], in0=t1[64:128, :w], in1=t2[64:128, :w], op=ALU.add),
                             reads=[kt1, kt2, "kTz"], writes=[(dkey, gi, 1)])

        ar.mark()
        wkv = ar.take([8, 512], BF)
        p.dma("gpsimd", wkv, attn_w_qkv[j][:, 1024:1536].rearrange("(k p) n -> p k n", p=128), writes=["wkv"])
        for n in range(2):
            qk_chunk(wkv, "wkv", n * 128, kT[:, 2 * n, :], ("kT", n), 1, dst2=kT[:, 2 * n + 1, :])
        for tt in range(18):
            pv = psb[4 + tt % 2]
            for k in range(8):
                p.op("tensor", lambda e, pv=pv, k=k, tt=tt: e.matmul(pv[:, 0:256], lhsT=hT[:, k, tt * 128:(tt + 1) * 128], rhs=wkv[:, k, 256:512],
                                                                  start=(k == 0), stop=(k == 7)), reads=["wkv"] + hk(k), writes=[PSK[4 + tt % 2]])
            p.op(A, lambda e, pv=pv, tt=tt: e.activation(out=vS[:, tt, :, 0:64], in_=pv[:, 0:256].rearrange("p (h d) -> p h d", h=4), func=AF.Identity),
                 reads=[PSK[4 + tt % 2]], writes=[("vS", tt)])
        ar.release()
        ar.mark()
        wq = ar.take([8, 512], BF)
        for qh in range(2):
            p.dma("gpsimd", wq, attn_w_qkv[j][:, qh * 512:(qh + 1) * 512].rearrange("(k p) n -> p k n", p=128), writes=["wq"])
            for n4 in range(4):
                n = qh * 4 + n4
                qk_chunk(wq, "wq", n4 * 128, qT[:, n, :], ("qT", n), 0)
        ar.release()
        ar.release()
        ar.mark()
        wo = hT.rearrange("p k t -> p (k t)")[:, 0:16 * 1024].rearrange("p (h n) -> p h n", h=16)
        oTg = ar.take([16, 512], BF)
        pT = [ar.take([512], BF) for _ in range(4)]
        bcs = ar.take([512], F32)
        rec = ar.take([512], F32)
        ones_f = ar.take([64], F32)
        p.dma("gpsimd", wo[0:64], attn_w_o[j].rearrange("(h d) n -> d h n", d=64), writes=["wo"] + hall)
        p.op(G, lambda e: e.memset(wo[64:128], 0.0), writes=["wo2"] + hall)
        p.op(V, lambda e: e.memset(oTg[64:128], 0.0), writes=["oTg2"])
        p.op(V, lambda e: e.memset(ones_f, 1.0), writes=["ones_f"])
        vkeys = [("vS", tt) for tt in range(18)] + ["vS1"]
        SCALE = HD ** -0.5
        qgroups = [(gi, t0, w, c) for gi, (t0, w, c) in enumerate(groups) if not (c == 1 and last)]
        scnt = 0
        for (gi, t0, w, c) in qgroups:
            ktiles = range(2) if c == 1 else range(18)
            nkt = len(ktiles)
            pending = []
            for h in range(16):
                kv = h // 4
                half = kv % 2
                lo = 64 * half
                perm_pos = QPERM.index(h)
                qn_, qhalf = perm_pos // 2, perm_pos % 2
                assert qhalf == half
                po = psb[4 + h % 2]
                kts = list(ktiles)

                def score(ti, ps_, bs, lo=lo, kv=kv, qn_=qn_, t0=t0, w=w, gi=gi):
                    kt = kts[ti]
                    p.op("tensor", lambda e: e.matmul(
                        ps_[:, :w], lhsT=kT[:, kv, kt * 128:(kt + 1) * 128], rhs=qT[:, qn_, t0:t0 + w], start=True, stop=True),
                        reads=[(("kT", kv // 2), g_) for g_ in range(5)] + [(("kT", kv // 2), g_, 1) for g_ in range(5)] + ["kTz", (("qT", qn_), gi)], writes=[PSK[bs]])

                LOOK = 3
                slots = []
                for ti in range(min(LOOK, nkt)):
                    bs = scnt % 4
                    scnt += 1
                    slots.append(bs)
                    score(ti, psb[bs], bs)
                for ti in range(nkt):
                    bs = slots[ti]
                    ps_ = psb[bs]
                    pt_ = pT[bs]
                    kt = kts[ti]
                    p.op(A, lambda e, ps_=ps_, pt_=pt_, w=w: e.activation(out=pt_[:, :w], in_=ps_[:, :w], func=AF.Exp, scale=SCALE),
                         reads=[PSK[bs]], writes=[("pT", bs)])
                    if ti == min(2, nkt - 1) and pending:
                        pending.pop(0)()
                    if ti + LOOK < nkt:
                        nb_ = scnt % 4
                        scnt += 1
                        slots.append(nb_)
                        score(ti + LOOK, psb[nb_], nb_)
                    p.op("tensor", lambda e, po=po, pt_=pt_, kt=kt, kv=kv, w=w, ti=ti, nkt=nkt: e.matmul(
                        po[0:65, :w], lhsT=vS[:, kt, kv, :], rhs=pt_[:, :w], start=(ti == 0), stop=(ti == nkt - 1)),
                        reads=vkeys + [("pT", bs)], writes=[PSK[4 + h % 2]])
                def norm_head(po=po, h=h, w=w):
                    p.op(V, lambda e: e.reciprocal(out=rec[64:65, :w], in_=po[64:65, :w]), reads=[PSK[4 + h % 2]], writes=["rec", PSK[4 + h % 2]])
                    p.op("tensor", lambda e: e.matmul(psb[6][0:64, :w], lhsT=ones_f[64:65, 0:64], rhs=rec[64:65, :w], start=True, stop=True),
                         reads=["rec", "ones_f"], writes=[PSK[6]])
                    p.op(V, lambda e: e.tensor_copy(out=bcs[0:64, :w], in_=psb[6][0:64, :w]), reads=[PSK[6]], writes=["bcs", PSK[6]])
                    p.op(V, lambda e: e.tensor_tensor(out=oTg[0:64, h, :w], in0=po[0:64, :w], in1=bcs[0:64, :w], op=ALU.mult),
                         reads=[PSK[4 + h % 2], "bcs"], writes=[("oTg", h), PSK[4 + h % 2]])
                pending.append(norm_head)
            while pending:
                pending.pop(0)()
            for n in range(8):
                pw = psb[7] if n % 2 == 0 else psb[6]
                pwk = PSK[7] if n % 2 == 0 else PSK[6]
                for h in range(16):
                    p.op("tensor", lambda e, pw=pw, h=h, n=n, w=w: e.matmul(pw[:, :w], lhsT=wo[:, h, n * 128:(n + 1) * 128], rhs=oTg[:, h, :w],
                                                                       start=(h == 0), stop=(h == 15)), reads=["wo", "wo2", "oTg2", ("oTg", h)], writes=[pwk])
                p.op(V, lambda e, pw=pw, n=n, t0=t0, w=w, c=c: e.scalar_tensor_tensor(
                    out=xT[:, n, t0:t0 + w], in0=pw[:, :w], scalar=mcol(i, 2, n, c), in1=xT[:, n, t0:t0 + w], op0=ALU.mult, op1=ALU.add),
                    reads=[pwk, ("modT", i), ("xT", n)], writes=[("xT", n)])
        ar.release()
        ar.release()

    hT = ar.take([8, T], BF)
    for i in range(DEPTH) if not opts.get("s5_stop") else []:
        last = i == DEPTH - 1
        if i == 0 or not opts.get("mod_side", True):
            mod_phase(i)
        if opts.get("only_mod"):
            dump(modT[:, 0].rearrange("p a b -> p (a b)"), 96, [("modT", 0)])
            break
        if mixers and i % 2 == 1 and opts.get("attn", True):
            attn_phase(i)
            if opts.get("stop_after") == (i, "mix"):
                break
        if mixers and i % 2 == 0 and opts.get("s5", True):
            s5_phase(i)
            if opts.get("stop_after") == (i, "mix"):
                break
        groups = token_groups(include_ctx=not last)
        norm_phase(hT, "hT", lambda k, c, i=i: gs2[:, i, k, c:c + 1], lambda k, c, i=i: mcol(i, 3, k, c), groups,
                   [("gs2", i), ("modT", i)])
        ffn_phase(i, hT, "hT", groups, side_layer=(i + 1 if (not last and opts.get("mod_side", True)) else None))

    if opts.get("s5_stop"):
        try:
            mod_phase(0)
            s5_phase(0)
        except StopBuild:
            pass
        p.emit()
        return nc, p
    if opts.get("only_mod"):
        p.emit()
        return nc, p
    if opts.get("stop_after"):
        for k in range(8):
            p.dma("sync", outT[k * 128:(k + 1) * 128, :], xT[:, k, CL:T], reads=[("xT", k)])
            p.dma("sync", dbg[:, k * 256:(k + 1) * 256], xT[:, k, 0:CL], reads=[("xT", k)])
        p.emit()
        return nc, p
    ar.mark()
    sq = ar.take([8, 512], BF)
    rstd = ar.take([512], F32)
    ost = [ar.take([8, 512], F32) for _ in range(2)]
    pss = psb[6]
    for gi, (t0, w, c) in enumerate(token_groups(include_ctx=False)):
        o_t = ost[gi % 2]
        for k in range(8):
            p.op("scalar", lambda e, k=k, t0=t0, w=w: e.activation(out=sq[:, k, :w], in_=xT[:, k, t0:t0 + w], func=AF.Square),
                 reads=[("xT", k)], writes=[("fsq", k)])
        for k in range(8):
            p.op("tensor", lambda e, k=k, w=w: e.matmul(pss[:, :w], lhsT=ones_bf, rhs=sq[:, k, :w], start=(k == 0), stop=(k == 7)),
                 reads=[("fsq", k), "ones_bf"], writes=[PSK[6]])
        p.op("scalar", lambda e, w=w: e.activation(out=rstd[:, :w], in_=pss[:, :w], func=AF.Sqrt, bias=EPS, scale=1.0 / D),
             reads=[PSK[6]], writes=["frstd"])
        p.op("vector", lambda e, w=w: e.reciprocal(out=rstd[:, :w], in_=rstd[:, :w]), reads=["frstd"], writes=["frstd"])
        for k in range(8):
            p.op("vector", lambda e, k=k, t0=t0, w=w, o_t=o_t: e.scalar_tensor_tensor(
                out=o_t[:, k, :w], in0=xT[:, k, t0:t0 + w], scalar=gfin_s[:, k:k + 1], in1=rstd[:, :w], op0=ALU.mult, op1=ALU.mult),
                reads=[("xT", k), "frstd", "gfin_s"], writes=[("ost", gi % 2, k)])
            p.dma("sync", outT[k * 128:(k + 1) * 128, t0 - CL:t0 - CL + w], o_t[:, k, :w], reads=[("ost", gi % 2, k)])
    ar.release()
    p.emit()
    return nc, p


def _cols(v):
    return np.ascontiguousarray(np.asarray(v, np.float32).reshape(-1, 128).T)


def make_in_maps(inputs):
    x = np.asarray(inputs["x"], np.float32)
    ctx = np.asarray(inputs["ctx"], np.float32)
    c = np.asarray(inputs["c"], np.float32)
    c_ctx = np.asarray(inputs["c_ctx"], np.float32)
    shared = {
        "adab": np.ascontiguousarray(np.stack([_cols(inputs["ada_b"][i]) for i in range(DEPTH)], axis=1)),
        "gmix": np.ascontiguousarray(np.stack([_cols(inputs["norm_mix_g"][i]) for i in range(DEPTH)], axis=1)),
        "gffn": np.ascontiguousarray(np.stack([_cols(inputs["norm_ffn_g"][i]) for i in range(DEPTH)], axis=1)),
        "gfin": _cols(inputs["final_g"]),
        "ada_w": np.ascontiguousarray(inputs["ada_w"], np.float32),
        "ffn_w1": np.ascontiguousarray(inputs["ffn_w1"], np.float32),
        "ffn_w2": np.ascontiguousarray(inputs["ffn_w2"], np.float32),
        "ident": np.eye(128, dtype=np.float32),
    }
    f32 = lambda a: np.ascontiguousarray(a, dtype=np.float32)
    a_re = np.asarray(inputs["s5_a_re"]); a_im = np.asarray(inputs["s5_a_im"]); ldt = np.asarray(inputs["s5_log_dt"])
    shared["s5ar"] = f32(a_re.transpose(0, 1, 3, 2).reshape(2, 128, 64))
    shared["s5ai"] = f32(a_im.transpose(0, 1, 3, 2).reshape(2, 128, 64))
    shared["s5ldt"] = f32(np.broadcast_to(ldt[:, :, None, :], (2, 2, 64, 64)).reshape(2, 128, 64))
    bre = np.asarray(inputs["s5_b_re"]).transpose(0, 1, 3, 2, 4).reshape(2, 128, 64, 16)
    bim = np.asarray(inputs["s5_b_im"]).transpose(0, 1, 3, 2, 4).reshape(2, 128, 64, 16)
    shared["s5b"] = f32(np.stack([bre, bim], axis=2))
    cre = np.asarray(inputs["s5_c_re"]).transpose(0, 1, 4, 2, 3).reshape(2, 128, 64, 16)
    cim = np.asarray(inputs["s5_c_im"]).transpose(0, 1, 4, 2, 3).reshape(2, 128, 64, 16)
    shared["s5c"] = f32(np.stack([cre, cim], axis=2))
    shared["s5d"] = f32(np.stack([_cols(inputs["s5_d"][j]) for j in range(2)], axis=1))
    ex = np.zeros((128, 35), np.float32)
    ex[:64, 0:17] = np.arange(17); ex[64:, 0:17] = 16 - np.arange(17)
    ex[:64, 17:33] = 15 - np.arange(16); ex[64:, 17:33] = np.arange(16)
    ex[:, 33] = 1.0; ex[:, 34] = 16.0
    shared["expo"] = ex
    gl = np.arange(128) // 16
    hi = np.arange(128) % 16
    parm = np.stack([(gl % 2 == 0), (gl % 2 == 1)], axis=1).astype(np.float32)
    shared["parm"] = f32(parm)
    shared["dmask"] = f32((gl[:, None, None] == np.arange(8)[None, :, None]) * (hi[:, None, None] == np.arange(16)[None, None, :]))
    shared["s5_w_glu"] = f32(inputs["s5_w_glu"])
    wqkv = np.asarray(inputs["attn_w_qkv"], np.float32)
    qcols = np.concatenate([np.arange(h * 64, (h + 1) * 64) for h in QPERM])
    shared["attn_w_qkv"] = f32(np.concatenate([wqkv[:, :, qcols], wqkv[:, :, 1024:]], axis=2))
    shared["attn_w_o"] = f32(inputs["attn_w_o"])
    tpos = np.arange(L)
    rowp = (tpos // 64).astype(np.float64); colp = (tpos % 64).astype(np.float64)
    invf = 10000.0 ** (-np.arange(16, dtype=np.float64) / 16)
    ang = np.zeros((64, L))
    ang[0:16] = invf[:, None] * rowp[None, :]; ang[16:32] = ang[0:16]
    ang[32:48] = invf[:, None] * colp[None, :]; ang[48:64] = ang[32:48]
    shared["rope_c"] = f32(np.tile(np.cos(ang), (2, 1)))
    shared["rope_s"] = f32(np.tile(np.sin(ang), (2, 1)))
    bdm = np.zeros((128, 128), np.float32)
    bdm[:64, :64] = 1.0 / 64; bdm[64:, 64:] = 1.0 / 64
    shared["bd"] = bdm
    pr = np.zeros((128, 128), np.float32)
    for base in (0, 32, 64, 96):
        for d_ in range(16):
            pr[base + d_ + 16, base + d_] = -1.0
            pr[base + d_, base + d_ + 16] = 1.0
    shared["prot"] = pr
    qg = np.asarray(inputs["attn_q_g"], np.float32); kg = np.asarray(inputs["attn_k_g"], np.float32)
    shared["qkg"] = f32(np.stack([np.tile(qg, (1, 2)).T, np.tile(kg, (1, 2)).T], axis=2))
    maps = []
    for b in range(8):
        m = dict(shared)
        m["xin"] = np.ascontiguousarray(np.concatenate([ctx[b], x[b]], axis=0).T)
        m["ccols"] = np.ascontiguousarray(np.stack([_cols(c[b]), _cols(c_ctx)], axis=2))
        maps.append(m)
    return maps


def kernel(**inputs):
    nc, _ = build()
    maps = make_in_maps(inputs)
    res = run_bass_kernel_spmd(nc, maps, core_ids=list(range(8)))
    out = np.stack([np.ascontiguousarray(res.results[b]["outT"].T) for b in range(8)], axis=0)
    return out.astype(np.float32)
```

```python
import math
from contextlib import ExitStack
import numpy as np
import concourse.bass as bass
import concourse.mybir as mybir
from concourse.bass_utils import run_bass_kernel_spmd

F32 = mybir.dt.float32
BF = mybir.dt.bfloat16
AF = mybir.ActivationFunctionType
ALU = mybir.AluOpType

D = 1024
DEPTH = 4
L = 2048
CL = 256
T = L + CL
NG = 64
NP = 64
NH = 16
Q = 16
NCH = T // Q
NCC = CL // Q
NCL = L // Q
HD = 64
NHEAD = 16
NKV = 4
DFF = 4096
EPS = 1e-6

QPERM = [0, 4, 1, 5, 2, 6, 3, 7, 8, 12, 9, 13, 10, 14, 11, 15]
COMPUTE = ("tensor", "vector", "scalar", "gpsimd")
NSLOT = 8


class Op:
    __slots__ = ("eng", "fn", "reads", "writes", "dma", "idx", "deps", "waits",
                 "needs_inc", "ctr", "slot", "slot_use", "gidx", "force")

    def __init__(self, eng, fn, reads, writes, dma):
        self.eng, self.fn, self.reads, self.writes, self.dma = eng, fn, reads, writes, dma
        self.deps = []
        self.waits = []
        self.needs_inc = False
        self.ctr = 0
        self.slot = None
        self.slot_use = 0


class Prog:
    def __init__(self, nc, strict=True):
        self.nc = nc
        self.strict = strict
        self.ops = []
        self.stack = ExitStack()
        self.last_w = {}
        self.readers = {}
        self.bar_tile = None
        self.last_bar = None
        self.bar_from = 0
        self.fence = False
        self.last_pe = None

    def sb(self, name, shape, dtype):
        return self.stack.enter_context(self.nc.sbuf_tensor(name, list(shape), dtype))

    def ps(self, name, shape, dtype):
        return self.stack.enter_context(self.nc.psum_tensor(name, list(shape), dtype))

    def op(self, eng, fn, reads=(), writes=(), dma=False):
        o = Op(eng, fn, tuple(reads), tuple(writes), dma)
        o.gidx = len(self.ops)
        deps = set()
        for k in o.reads:
            w = self.last_w.get(k)
            if w is not None:
                deps.add(w)
        for k in o.writes:
            w = self.last_w.get(k)
            if w is not None:
                deps.add(w)
            for r in self.readers.get(k, ()):
                deps.add(r)
        deps.discard(o.gidx)
        if self.last_bar is not None:
            deps.add(self.last_bar)
        o.force = None
        if eng == "tensor":
            if self.fence and self.last_pe is not None:
                deps.add(self.last_pe)
                o.force = self.last_pe
            self.fence = False
            self.last_pe = o.gidx
        o.deps = sorted(deps)
        for k in o.reads:
            self.readers.setdefault(k, []).append(o.gidx)
        for k in o.writes:
            self.last_w[k] = o.gidx
            self.readers[k] = []
        self.ops.append(o)
        return o

    def pe_fence(self):
        self.fence = True

    def barrier(self):
        if self.bar_tile is None:
            self.bar_tile = self.sb("bar_tile", [128, 8], F32)
        o = Op("vector", lambda e: e.memset(self.bar_tile[:, 0:1], 0.0), (), (), False)
        o.force = None
        o.gidx = len(self.ops)
        lastc = {}
        deps = set()
        for q in self.ops[self.bar_from:]:
            if q.dma:
                deps.add(q.gidx)
            else:
                lastc[q.eng] = q.gidx
        deps.update(lastc.values())
        if self.last_bar is not None:
            deps.add(self.last_bar)
        o.deps = sorted(deps)
        self.ops.append(o)
        self.last_bar = o.gidx
        self.bar_from = len(self.ops)
        return o

    def dma(self, eng, out, in_, reads=(), writes=(), **kw):
        return self.op(eng, lambda e: e.dma_start(out=out, in_=in_, **kw), reads, writes, dma=True)

    def emit(self):
        nc = self.nc
        ops = self.ops
        engs = ("tensor", "vector", "scalar", "gpsimd", "sync")
        per = {e: [] for e in engs}
        for o in ops:
            o.idx = len(per[o.eng])
            per[o.eng].append(o)
        dcount = {e: 0 for e in engs}
        for o in ops:
            if o.dma:
                o.slot = dcount[o.eng] % NSLOT
                o.slot_use = dcount[o.eng] // NSLOT + 1
                dcount[o.eng] += 1
        known = {e: {f: -1 for f in COMPUTE} for e in engs}
        kdma = {e: {} for e in engs}
        snap = [None] * len(ops)
        for o in ops:
            E = o.eng
            kn = known[E]
            for d in reversed(o.deps):
                pr = ops[d]
                if pr.dma:
                    key = (pr.eng, pr.slot)
                    if kdma[E].get(key, 0) >= pr.slot_use:
                        continue
                    kdma[E][key] = pr.slot_use
                    o.waits.append(("dma", pr.eng, pr.slot, pr.slot_use))
                    psn = snap[d]
                    for f in COMPUTE:
                        if psn[f] > kn[f]:
                            kn[f] = psn[f]
                else:
                    Fe = pr.eng
                    if Fe == E and not o.dma and (Fe == "tensor" or not self.strict) and getattr(o, "force", None) != d:
                        continue
                    if kn[Fe] >= pr.idx:
                        continue
                    o.waits.append(("eng", d))
                    pr.needs_inc = True
                    psn = snap[d]
                    for f in COMPUTE:
                        if psn[f] > kn[f]:
                            kn[f] = psn[f]
                    if pr.idx > kn[Fe]:
                        kn[Fe] = pr.idx
            if o.dma and o.slot_use > 1:
                key = (o.eng, o.slot)
                if kdma[E].get(key, 0) < o.slot_use - 1:
                    kdma[E][key] = o.slot_use - 1
                    o.waits.append(("dma", o.eng, o.slot, o.slot_use - 1))
            snap[o.gidx] = dict(kn)
        for e in COMPUTE:
            c = 0
            for o in per[e]:
                if o.needs_inc and not o.dma:
                    c += 1
                    o.ctr = c
        st = self.stack
        esem = {e: st.enter_context(nc.semaphore("c_" + e)) for e in COMPUTE}
        dsem = {}
        for e in engs:
            for s in range(min(NSLOT, dcount[e])):
                dsem[(e, s)] = st.enter_context(nc.semaphore("d_%s_%d" % (e, s)))
        last_use = {}
        for o in ops:
            if o.dma:
                last_use[(o.eng, o.slot)] = o.slot_use

        def run(e, ename):
            for o in per[ename]:
                for w in o.waits:
                    if w[0] == "dma":
                        e.wait_ge(dsem[(w[1], w[2])], 16 * w[3])
                    else:
                        pr = ops[w[1]]
                        e.wait_ge(esem[pr.eng], pr.ctr)
                ins = o.fn(e)
                if o.dma:
                    ins.then_inc(dsem[(o.eng, o.slot)], 16)
                elif o.needs_inc:
                    ins.then_inc(esem[o.eng], 1)
            for (qe, s), u in last_use.items():
                if qe == ename:
                    e.wait_ge(dsem[(qe, s)], 16 * u)

        with nc.Block() as block:
            for ename in engs:
                if per[ename]:
                    getattr(block, ename)(lambda e, ename=ename: run(e, ename))
        st.close()
        self.stats = {e: len(per[e]) for e in engs}


class StopBuild(Exception):
    pass


class Arena:
    def __init__(self, p, name, words):
        self.t = p.sb(name, [128, words], F32)
        self.p = p
        self.words = words
        self.off = 0
        self.marks = []

    def take(self, shape, dtype):
        n = 1
        for s in shape:
            n *= s
        w = n if dtype == F32 else (n + 1) // 2
        assert self.off + w <= self.words, ("arena overflow", self.off, w, self.words)
        ap = self.t[:, self.off:self.off + w]
        self.off += w
        if dtype != F32:
            ap = ap.bitcast(dtype)[:, 0:n]
        if len(shape) == 2:
            ap = ap.rearrange("p (a b) -> p a b", a=shape[0])
        elif len(shape) == 3:
            ap = ap.rearrange("p (a b c) -> p a b c", a=shape[0], b=shape[1])
        elif len(shape) == 4:
            ap = ap.rearrange("p (a b c d) -> p a b c d", a=shape[0], b=shape[1], c=shape[2])
        return ap

    def mark(self):
        self.marks.append(self.off)

    def release(self):
        self.off = self.marks.pop()
        self.p.barrier()


def token_groups(include_ctx=True):
    g = []
    if include_ctx:
        g.append((0, CL, 1))
    for i in range(L // 512):
        g.append((CL + i * 512, 512, 0))
    return g


def build(opts=None):
    opts = opts or {}
    mixers = opts.get("mixers", True)
    nc = bass.Bass("TRN2", target_bir_lowering=False)
    dr = {}

    def din(name, shape, dtype=F32):
        dr[name] = nc.dram_tensor(name, list(shape), dtype, kind="ExternalInput").ap()
        return dr[name]

    xin = din("xin", [D, T])
    ccols = din("ccols", [128, 8, 2])
    adab = din("adab", [128, DEPTH, 48])
    gmix = din("gmix", [128, DEPTH, 8])
    gffn = din("gffn", [128, DEPTH, 8])
    gfin = din("gfin", [128, 8])
    ada_w = din("ada_w", [DEPTH, D, 6 * D])
    ffn_w1 = din("ffn_w1", [DEPTH, D, DFF])
    ffn_w2 = din("ffn_w2", [DEPTH, DFF, D])
    ident_in = din("ident", [128, 128])
    s5ar = din("s5ar", [2, 128, 64])
    s5ai = din("s5ai", [2, 128, 64])
    s5ldt = din("s5ldt", [2, 128, 64])
    s5b = din("s5b", [2, 128, 2, 64, 16])
    s5c = din("s5c", [2, 128, 2, 64, 16])
    s5d_in = din("s5d", [128, 2, 8])
    expo = din("expo", [128, 35])
    parm_in = din("parm", [128, 2])
    dmask_in = din("dmask", [128, 8, 16])
    s5_w_glu = din("s5_w_glu", [2, D, 2 * D])
    attn_w_qkv = din("attn_w_qkv", [2, D, 1536])
    attn_w_o = din("attn_w_o", [2, D, D])
    rope_c = din("rope_c", [128, L])
    rope_s = din("rope_s", [128, L])
    bd_in = din("bd", [128, 128])
    prot_in = din("prot", [128, 128])
    qkg = din("qkg", [128, 2, 2])
    outT = nc.dram_tensor("outT", [D, L], F32, kind="ExternalOutput").ap()
    dbg = nc.dram_tensor("dbg", [128, 20480], F32, kind="ExternalOutput").ap() if opts.get("debug") else None
    dbg_off = [0]

    def dump(ap2d, n, keys):
        if dbg is None:
            return
        p.dma("sync", dbg[:, dbg_off[0]:dbg_off[0] + n], ap2d, reads=keys)
        print("dump at", dbg_off[0], n, keys)
        dbg_off[0] += n

    p = Prog(nc)
    ar = Arena(p, "arena", 53200)
    psb = [p.ps("psb%d" % i, [128, 512], F32) for i in range(8)]
    PSK = ["psb%d" % i for i in range(8)]

    xT = ar.take([8, T], F32)
    modT = ar.take([DEPTH, 48, 2], F32)
    adab_s = ar.take([DEPTH, 48], F32)
    gmix_s = ar.take([DEPTH, 8], F32)
    gffn_s = ar.take([DEPTH, 8], F32)
    gfin_s = ar.take([8], F32)
    gs1 = ar.take([DEPTH, 8, 2], F32)
    gs2 = ar.take([DEPTH, 8, 2], F32)
    cc_s = ar.take([8, 2], F32)
    sc_bf = ar.take([8, 2], BF)
    ones_bf = ar.take([128], BF)
    ident_f = ar.take([128], F32)
    ident_bf = ar.take([128], BF)

    s5d_s = ar.take([2, 8], F32)
    parm_s = ar.take([2], F32)
    dmask_s = ar.take([8, 16], F32)
    p.dma("sync", s5d_s, s5d_in, writes=["s5d_s"])
    p.dma("sync", parm_s, parm_in, writes=["parm_s"])
    p.dma("sync", dmask_s, dmask_in, writes=["dmask_s"])
    for k in range(8):
        p.dma("sync", xT[:, k, :], xin[k * 128:(k + 1) * 128, :], writes=[("xT", k)])
    p.dma("sync", cc_s, ccols, writes=["cc_s"])
    p.dma("sync", adab_s, adab, writes=["adab_s"])
    p.dma("sync", gmix_s, gmix, writes=["gmix_s"])
    p.dma("sync", gffn_s, gffn, writes=["gffn_s"])
    p.dma("sync", gfin_s, gfin, writes=["gfin_s"])
    p.dma("sync", ident_f, ident_in, writes=["ident_f"])
    p.op("vector", lambda e: e.memset(ones_bf, 1.0), writes=["ones_bf"])
    p.op("vector", lambda e: e.tensor_copy(out=ident_bf, in_=ident_f), reads=["ident_f"], writes=["ident_bf"])
    p.op("scalar", lambda e: e.activation(out=sc_bf, in_=cc_s, func=AF.Silu), reads=["cc_s"], writes=["sc_bf"])

    def mod_phase(i):
        ar.mark()
        wA = [ar.take([8, 1024], BF) for _ in range(2)]
        pm = psb[7]
        for piece in range(6):
            wa = wA[piece % 2]
            wk = ("wA", piece % 2)
            p.dma("gpsimd", wa, ada_w[i][:, piece * 1024:(piece + 1) * 1024].rearrange("(k p) n -> p k n", p=128),
                  writes=[wk])
            for n in range(8):
                j = piece * 8 + n
                for k in range(8):
                    p.op("tensor", lambda e, wa=wa, k=k, n=n, j=j: e.matmul(
                        pm[:, 2 * j:2 * j + 2], lhsT=wa[:, k, n * 128:(n + 1) * 128], rhs=sc_bf[:, k, :],
                        start=(k == 0), stop=(k == 7)), reads=[wk, "sc_bf"], writes=[PSK[7]])
        p.op("vector", lambda e: e.tensor_tensor(
            out=modT[:, i], in0=pm[:, 0:96].rearrange("p (j c) -> p j c", c=2),
            in1=adab_s[:, i].unsqueeze(2).broadcast_to([128, 48, 2]), op=ALU.add),
            reads=[PSK[7], "adab_s"], writes=[("modT", i)])
        for (dst, dk, g_s, gk, which) in ((gs1, "gs1", gmix_s, "gmix_s", 1), (gs2, "gs2", gffn_s, "gffn_s", 4)):
            p.op("vector", lambda e, dst=dst, g_s=g_s, which=which: e.scalar_tensor_tensor(
                out=dst[:, i], in0=modT[:, i, which * 8:(which + 1) * 8, :], scalar=1.0,
                in1=g_s[:, i].unsqueeze(2).broadcast_to([128, 8, 2]), op0=ALU.add, op1=ALU.mult),
                reads=[("modT", i), gk], writes=[(dk, i)])
        ar.release()

    def mod_finish(i):
        pm = psb[7]
        p.op("vector", lambda e: e.tensor_tensor(
            out=modT[:, i], in0=pm[:, 0:96].rearrange("p (j c) -> p j c", c=2),
            in1=adab_s[:, i].unsqueeze(2).broadcast_to([128, 48, 2]), op=ALU.add),
            reads=[PSK[7], "adab_s"], writes=[("modT", i)])
        for (dst, dk, g_s, gk, which) in ((gs1, "gs1", gmix_s, "gmix_s", 1), (gs2, "gs2", gffn_s, "gffn_s", 4)):
            p.op("vector", lambda e, dst=dst, g_s=g_s, which=which: e.scalar_tensor_tensor(
                out=dst[:, i], in0=modT[:, i, which * 8:(which + 1) * 8, :], scalar=1.0,
                in1=g_s[:, i].unsqueeze(2).broadcast_to([128, 8, 2]), op0=ALU.add, op1=ALU.mult),
                reads=[("modT", i), gk], writes=[(dk, i)])

    def mod_side(i, bufs):
        pm = psb[7]
        pieces = []
        for jn in range(48):
            def piece(jn=jn):
                wa = bufs[jn % len(bufs)]
                wk = ("wAs", jn % len(bufs))
                p.dma("gpsimd", wa, ada_w[i][:, jn * 128:(jn + 1) * 128].rearrange("(k p) n -> p k n", p=128), writes=[wk])
                for k in range(8):
                    p.op("tensor", lambda e, wa=wa, k=k: e.matmul(pm[:, 2 * jn:2 * jn + 2], lhsT=wa[:, k, :], rhs=sc_bf[:, k, :],
                                                               start=(k == 0), stop=(k == 7)), reads=[wk, "sc_bf"], writes=[PSK[7]])
            pieces.append(piece)
        return pieces

    def mcol(i, which, k, c):
        return modT[:, i, which * 8 + k, c:c + 1]

    def norm_phase(hT, hkey, gcol, shcol, groups, rkeys):
        ar.mark()
        sq = ar.take([8, 512], BF)
        rstd = ar.take([512], F32)
        tmp = [ar.take([512], F32) for _ in range(2)]
        pss = psb[6]
        for gi, (t0, w, c) in enumerate(groups):
            for k in range(8):
                p.op("scalar", lambda e, k=k, t0=t0, w=w: e.activation(out=sq[:, k, :w], in_=xT[:, k, t0:t0 + w], func=AF.Square),
                     reads=[("xT", k)], writes=[("sq", k)])
            for k in range(8):
                p.op("tensor", lambda e, k=k, w=w: e.matmul(pss[:, :w], lhsT=ones_bf, rhs=sq[:, k, :w], start=(k == 0), stop=(k == 7)),
                     reads=[("sq", k), "ones_bf"], writes=[PSK[6]])
            p.op("scalar", lambda e, w=w: e.activation(out=rstd[:, :w], in_=pss[:, :w], func=AF.Sqrt, bias=EPS, scale=1.0 / D),
                 reads=[PSK[6]], writes=["rstd"])
            p.op("vector", lambda e, w=w: e.reciprocal(out=rstd[:, :w], in_=rstd[:, :w]), reads=["rstd"], writes=["rstd"])
            for k in range(8):
                tm = tmp[k % 2]
                p.op("vector", lambda e, k=k, t0=t0, w=w, c=c, tm=tm: e.scalar_tensor_tensor(
                    out=tm[:, :w], in0=xT[:, k, t0:t0 + w], scalar=gcol(k, c), in1=rstd[:, :w], op0=ALU.mult, op1=ALU.mult),
                    reads=[("xT", k), "rstd"] + rkeys, writes=[("ntmp", k % 2)])
                if shcol is None:
                    p.op("scalar", lambda e, k=k, t0=t0, w=w, tm=tm: e.activation(out=hT[:, k, t0:t0 + w], in_=tm[:, :w], func=AF.Identity),
                         reads=[("ntmp", k % 2)], writes=[(hkey, k, gi)])
                else:
                    p.op("scalar", lambda e, k=k, t0=t0, w=w, c=c, tm=tm: e.activation(
                        out=hT[:, k, t0:t0 + w], in_=tm[:, :w], func=AF.Identity, bias=shcol(k, c)),
                        reads=[("ntmp", k % 2)] + rkeys, writes=[(hkey, k, gi)])
        ar.release()

    def ffn_phase(i, hT, hkey, groups, side_layer=None):
        ar.mark()
        side = []
        if side_layer is not None:
            side = mod_side(side_layer, [ar.take([8, 128], BF) for _ in range(4)])
        nblk = 0
        w1q = [ar.take([8, 1024], BF) for _ in range(2)]
        w2q = [ar.take([8, 1024], BF) for _ in range(2)]
        aT = [ar.take([8, 512], BF) for _ in range(2)]
        rt = [ar.take([512], BF) for _ in range(2)]
        cnt = 0
        for q in range(4):
            b = q % 2
            p.dma("gpsimd", w1q[b], ffn_w1[i][:, q * 1024:(q + 1) * 1024].rearrange("(k p) n -> p k n", p=128), writes=[("w1q", b)])
            p.dma("gpsimd", w2q[b], ffn_w2[i][q * 1024:(q + 1) * 1024, :].rearrange("(f p) n -> p f n", p=128), writes=[("w2q", b)])
            for gi, (t0, w, c) in enumerate(groups):
                ab = cnt % 2
                cnt += 1
                a_t = aT[ab]
                for f in range(8):
                    ps = psb[f % 2]
                    for k in range(8):
                        p.op("tensor", lambda e, ps=ps, b=b, k=k, f=f, t0=t0, w=w: e.matmul(
                            ps[:, :w], lhsT=w1q[b][:, k, f * 128:(f + 1) * 128], rhs=hT[:, k, t0:t0 + w],
                            start=(k == 0), stop=(k == 7)), reads=[("w1q", b), (hkey, k, gi)], writes=[PSK[f % 2]])
                    r_t = rt[f % 2]
                    p.op("scalar", lambda e, ps=ps, r_t=r_t, w=w: e.activation(out=r_t[:, :w], in_=ps[:, :w], func=AF.Relu),
                         reads=[PSK[f % 2]], writes=[("rt", f % 2)])
                    p.op("vector", lambda e, r_t=r_t, a_t=a_t, f=f, w=w: e.tensor_tensor(out=a_t[:, f, :w], in0=r_t[:, :w], in1=r_t[:, :w], op=ALU.mult),
                         reads=[("rt", f % 2)], writes=[("aT", ab, f)])
                    nblk += 1
                    if side and nblk % 3 == 0:
                        side.pop(0)()
                for d in range(8):
                    ps2 = psb[2 + d % 2]
                    for f in range(8):
                        p.op("tensor", lambda e, ps2=ps2, b=b, f=f, d=d, a_t=a_t, w=w: e.matmul(
                            ps2[:, :w], lhsT=w2q[b][:, f, d * 128:(d + 1) * 128], rhs=a_t[:, f, :w],
                            start=(f == 0), stop=(f == 7)), reads=[("w2q", b), ("aT", ab, f)], writes=[PSK[2 + d % 2]])
                    p.op("vector", lambda e, ps2=ps2, d=d, t0=t0, w=w, c=c: e.scalar_tensor_tensor(
                        out=xT[:, d, t0:t0 + w], in0=ps2[:, :w], scalar=mcol(i, 5, d, c), in1=xT[:, d, t0:t0 + w],
                        op0=ALU.mult, op1=ALU.add), reads=[PSK[2 + d % 2], ("modT", i), ("xT", d)], writes=[("xT", d)])
        while side:
            side.pop(0)()
        if side_layer is not None:
            mod_finish(side_layer)
        ar.release()

    TWO_PI = 2.0 * math.pi
    MAGIC = 12582912.0
    CW1 = 6.28125
    CW2 = 0.0019350051879882812
    CW3 = TWO_PI - CW1 - CW2
    PI_LO = 3.1415925

    def s5_phase(i):
        j = i // 2
        V = "vector"
        G = "gpsimd"
        A = "scalar"
        ar.mark()
        PCr = ar.take([64, 17], F32)
        PCi = ar.take([64, 17], F32)
        BBb = ar.take([2, 64, 16], BF)
        A1v = ar.take([2, 64], F32)
        A2v = ar.take([2, 64], F32)
        NGH = 64
        VH = ar.take([2, NGH, NCH + 1], BF)
        ar.mark()
        PBr = ar.take([64, 16], F32)
        PBi = ar.take([64, 16], F32)
        ar.mark()
        ar_s = ar.take([64], F32); ai_s = ar.take([64], F32); ldt_s = ar.take([64], F32)
        ex_s = ar.take([35], F32)
        p.dma("sync", ar_s, s5ar[j], writes=["ar_s"])
        p.dma("sync", ai_s, s5ai[j], writes=["ai_s"])
        p.dma("sync", ldt_s, s5ldt[j], writes=["ldt_s"])
        p.dma("sync", ex_s, expo, writes=["ex_s"])
        dt = ar.take([64], F32); dar = ar.take([64], F32); th = ar.take([64], F32)
        p.op(A, lambda e: e.activation(out=dt, in_=ldt_s, func=AF.Exp), reads=["ldt_s"], writes=["dt"])
        p.op(V, lambda e: e.tensor_tensor(out=dar, in0=dt, in1=ar_s, op=ALU.mult), reads=["dt", "ar_s"], writes=["dar"])
        p.op(V, lambda e: e.tensor_tensor(out=th, in0=dt, in1=ai_s, op=ALU.mult), reads=["dt", "ai_s"], writes=["th"])
        GH = 32
        b_s = ar.take([2, GH, 16], F32)
        lm = ar.take([GH, 35], F32); ang = ar.take([GH, 35], F32); kk = ar.take([GH, 35], F32); rc = ar.take([GH, 35], F32)
        den = ar.take([GH], F32); t0_ = ar.take([GH], F32); nr = ar.take([GH], F32); fr = ar.take([GH], F32); fi = ar.take([GH], F32)
        tb1 = ar.take([GH, 16], F32); tb2 = ar.take([GH, 16], F32)
        for g2h in range(2):
            gs_ = slice(GH * g2h, GH * g2h + GH)
            p.dma("sync", b_s, s5b[j][:, :, gs_, :], writes=["b_s"])
            exb = ex_s.unsqueeze(1).broadcast_to([128, GH, 35])
            p.op(V, lambda e, gs_=gs_, exb=exb: e.tensor_tensor(out=lm, in0=dar[:, gs_].unsqueeze(2).broadcast_to([128, GH, 35]), in1=exb, op=ALU.mult),
                 reads=["dar", "ex_s"], writes=["lm"])
            p.op(V, lambda e, gs_=gs_, exb=exb: e.tensor_tensor(out=ang, in0=th[:, gs_].unsqueeze(2).broadcast_to([128, GH, 35]), in1=exb, op=ALU.mult),
                 reads=["th", "ex_s"], writes=["ang"])
            p.op(A, lambda e: e.activation(out=lm, in_=lm, func=AF.Exp), reads=["lm"], writes=["lm"])
            p.op(V, lambda e: e.tensor_scalar(out=kk, in0=ang, scalar1=1.0 / TWO_PI, scalar2=MAGIC, op0=ALU.mult, op1=ALU.add), reads=["ang"], writes=["kk"])
            p.op(V, lambda e: e.tensor_scalar(out=kk, in0=kk, scalar1=-MAGIC, scalar2=None, op0=ALU.add), reads=["kk"], writes=["kk"])
            p.op(V, lambda e: e.scalar_tensor_tensor(out=ang, in0=kk, scalar=-CW1, in1=ang, op0=ALU.mult, op1=ALU.add), reads=["kk", "ang"], writes=["ang"])
            p.op(V, lambda e: e.scalar_tensor_tensor(out=ang, in0=kk, scalar=-CW2, in1=ang, op0=ALU.mult, op1=ALU.add), reads=["kk", "ang"], writes=["ang"])
            p.op(V, lambda e: e.scalar_tensor_tensor(out=ang, in0=kk, scalar=-CW3, in1=ang, op0=ALU.mult, op1=ALU.add), reads=["kk", "ang"], writes=["ang"])
            p.op(V, lambda e: e.tensor_scalar(out=ang, in0=ang, scalar1=PI_LO, scalar2=-PI_LO, op0=ALU.min, op1=ALU.max), reads=["ang"], writes=["ang"])
            p.op(V, lambda e: e.tensor_scalar(out=kk, in0=ang, scalar1=math.pi / 2, scalar2=-TWO_PI, op0=ALU.is_gt, op1=ALU.mult), reads=["ang"], writes=["kk"])
            p.op(V, lambda e: e.scalar_tensor_tensor(out=rc, in0=ang, scalar=math.pi / 2, in1=kk, op0=ALU.add, op1=ALU.add), reads=["ang", "kk"], writes=["rc"])
            p.op(V, lambda e: e.tensor_scalar(out=rc, in0=rc, scalar1=PI_LO, scalar2=-PI_LO, op0=ALU.min, op1=ALU.max), reads=["rc"], writes=["rc"])
            p.op(A, lambda e: e.activation(out=ang, in_=ang, func=AF.Sin), reads=["ang"], writes=["ang"])
            p.op(A, lambda e: e.activation(out=rc, in_=rc, func=AF.Sin), reads=["rc"], writes=["rc"])
            p.op(V, lambda e: e.tensor_tensor(out=rc, in0=lm, in1=rc, op=ALU.mult), reads=["lm", "rc"], writes=["rc"])
            p.op(V, lambda e: e.tensor_tensor(out=ang, in0=lm, in1=ang, op=ALU.mult), reads=["lm", "ang"], writes=["ang"])
            p.op(V, lambda e, gs_=gs_: e.tensor_copy(out=PCr[:, gs_, :], in_=rc[:, :, 0:17]), reads=["rc"], writes=["PCr"])
            p.op(V, lambda e, gs_=gs_: e.tensor_copy(out=PCi[:, gs_, :], in_=ang[:, :, 0:17]), reads=["ang"], writes=["PCi"])
            p.op(V, lambda e, gs_=gs_: e.tensor_copy(out=PBr[:, gs_, :], in_=rc[:, :, 17:33]), reads=["rc"], writes=["PBr"])
            p.op(V, lambda e, gs_=gs_: e.tensor_copy(out=PBi[:, gs_, :], in_=ang[:, :, 17:33]), reads=["ang"], writes=["PBi"])
            a1r = rc[:, :, 33]
            a1i = ang[:, :, 33]
            ars = ar_s[:, gs_]
            ais = ai_s[:, gs_]
            p.op(V, lambda e, ars=ars: e.tensor_tensor(out=den, in0=ars, in1=ars, op=ALU.mult), reads=["ar_s"], writes=["den"])
            p.op(V, lambda e, ais=ais: e.tensor_tensor(out=t0_, in0=ais, in1=ais, op=ALU.mult), reads=["ai_s"], writes=["t0_"])
            p.op(V, lambda e: e.tensor_tensor(out=den, in0=den, in1=t0_, op=ALU.add), reads=["den", "t0_"], writes=["den"])
            p.op(V, lambda e: e.reciprocal(out=den, in_=den), reads=["den"], writes=["den"])
            p.op(V, lambda e, a1r=a1r: e.tensor_scalar(out=nr, in0=a1r, scalar1=-1.0, scalar2=None, op0=ALU.add), reads=["rc"], writes=["nr"])
            p.op(V, lambda e, ars=ars: e.tensor_tensor(out=fr, in0=nr, in1=ars, op=ALU.mult), reads=["nr", "ar_s"], writes=["fr"])
            p.op(V, lambda e, a1i=a1i, ais=ais: e.tensor_tensor(out=t0_, in0=a1i, in1=ais, op=ALU.mult), reads=["ang", "ai_s", "den"], writes=["t0_"])
            p.op(V, lambda e: e.tensor_tensor(out=fr, in0=fr, in1=t0_, op=ALU.add), reads=["fr", "t0_"], writes=["fr"])
            p.op(V, lambda e: e.tensor_tensor(out=fr, in0=fr, in1=den, op=ALU.mult), reads=["fr", "den"], writes=["fr"])
            p.op(V, lambda e, a1i=a1i, ars=ars: e.tensor_tensor(out=fi, in0=a1i, in1=ars, op=ALU.mult), reads=["ang", "ar_s"], writes=["fi"])
            p.op(V, lambda e, ais=ais: e.tensor_tensor(out=t0_, in0=nr, in1=ais, op=ALU.mult), reads=["nr", "ai_s", "fr"], writes=["t0_"])
            p.op(V, lambda e: e.tensor_tensor(out=fi, in0=fi, in1=t0_, op=ALU.subtract), reads=["fi", "t0_"], writes=["fi"])
            p.op(V, lambda e: e.tensor_tensor(out=fi, in0=fi, in1=den, op=ALU.mult), reads=["fi", "den"], writes=["fi"])
            frb = fr.unsqueeze(2).broadcast_to([128, GH, 16])
            fib = fi.unsqueeze(2).broadcast_to([128, GH, 16])
            p.op(V, lambda e, frb=frb: e.tensor_tensor(out=tb1, in0=b_s[:, 0], in1=frb, op=ALU.mult), reads=["b_s", "fr"], writes=["tb1"])
            p.op(V, lambda e, fib=fib: e.tensor_tensor(out=tb2, in0=b_s[:, 1], in1=fib, op=ALU.mult), reads=["b_s", "fi"], writes=["tb2"])
            p.op(V, lambda e, gs_=gs_: e.tensor_tensor(out=BBb[:, 0, gs_, :], in0=tb1, in1=tb2, op=ALU.subtract), reads=["tb1", "tb2"], writes=["BBb"])
            p.op(V, lambda e, frb=frb: e.tensor_tensor(out=tb1, in0=b_s[:, 1], in1=frb, op=ALU.mult), reads=["b_s", "fr", "BBb"], writes=["tb1"])
            p.op(V, lambda e, fib=fib: e.tensor_tensor(out=tb2, in0=b_s[:, 0], in1=fib, op=ALU.mult), reads=["b_s", "fi", "BBb"], writes=["tb2"])
            p.op(V, lambda e, gs_=gs_: e.tensor_tensor(out=BBb[:, 1, gs_, :], in0=tb1, in1=tb2, op=ALU.add), reads=["tb1", "tb2"], writes=["BBb"])
            p.op(V, lambda e, gs_=gs_: e.tensor_copy(out=A1v[:, 0, gs_], in_=rc[:, :, 34]), reads=["rc"], writes=["A1v"])
            p.op(V, lambda e, gs_=gs_: e.tensor_copy(out=A1v[:, 1, gs_], in_=rc[:, :, 34]), reads=["rc", "A1v"], writes=["A1v"])
            p.op(V, lambda e, gs_=gs_: e.tensor_scalar(out=A2v[:, 0, gs_], in0=ang[:, :, 34], scalar1=-1.0, scalar2=None, op0=ALU.mult), reads=["ang"], writes=["A2v"])
            p.op(V, lambda e, gs_=gs_: e.tensor_copy(out=A2v[:, 1, gs_], in_=ang[:, :, 34]), reads=["ang", "A2v"], writes=["A2v"])
        ar.release()

        norm_phase(hT, "hT", lambda k, c: gs1[:, i, k, c:c + 1], lambda k, c: mcol(i, 0, k, c), token_groups(True),
                   [("gs1", i), ("modT", i)])
        hkeys = lambda k: [("hT", k, gi) for gi in range(5)]

        for half in range(1):
            ks = range(8)
            p.op(G, lambda e: e.memset(VH[0:64, :, :, 0:1], 0.0), writes=["VH"])
            p.op(G, lambda e: e.memset(VH[64:128, :, :, NCH:NCH + 1], 0.0), reads=["VH"], writes=["VH"])
            ar.mark()
            ABb = [ar.take([2, 16, 8, 16], BF) for _ in range(2)]
            ABt = ar.take([2, 16, 2, 128], BF)
            vt1 = ar.take([16, 2, 16], F32)
            vt2 = ar.take([16, 2, 16], F32)
            vcnt = 0
            def gen_ab(k):
                AB_ = ABb[k % 2]
                abk = "AB%d" % (k % 2)
                for hh in range(4):
                    gsl = slice(8 * k + 2 * hh, 8 * k + 2 * hh + 2)
                    pbr = PBr[:, gsl, :].rearrange("p g s -> p s g").unsqueeze(3).broadcast_to([128, 16, 2, 16])
                    pbi = PBi[:, gsl, :].rearrange("p g s -> p s g").unsqueeze(3).broadcast_to([128, 16, 2, 16])
                    bbr = BBb[:, 0, gsl, :].unsqueeze(1).broadcast_to([128, 16, 2, 16])
                    bbi = BBb[:, 1, gsl, :].unsqueeze(1).broadcast_to([128, 16, 2, 16])
                    abr = AB_[:, 0, :, 2 * hh:2 * hh + 2, :]
                    abi = AB_[:, 1, :, 2 * hh:2 * hh + 2, :]
                    p.op(G, lambda e, pbr=pbr, bbr=bbr: e.tensor_tensor(out=vt1, in0=pbr, in1=bbr, op=ALU.mult), reads=["PBr", "BBb"], writes=["vt1"])
                    p.op(G, lambda e, pbi=pbi, bbi=bbi: e.tensor_tensor(out=vt2, in0=pbi, in1=bbi, op=ALU.mult), reads=["PBi", "BBb"], writes=["vt2"])
                    p.op(G, lambda e, abr=abr: e.tensor_tensor(out=abr, in0=vt1, in1=vt2, op=ALU.subtract), reads=["vt1", "vt2"], writes=[abk])
                    p.op(G, lambda e, pbr=pbr, bbi=bbi: e.tensor_tensor(out=vt1, in0=pbr, in1=bbi, op=ALU.mult), reads=["PBr", "BBb"], writes=["vt1"])
                    p.op(G, lambda e, pbi=pbi, bbr=bbr: e.tensor_tensor(out=vt2, in0=pbi, in1=bbr, op=ALU.mult), reads=["PBi", "BBb"], writes=["vt2"])
                    p.op(G, lambda e, abi=abi: e.tensor_tensor(out=abi, in0=vt1, in1=vt2, op=ALU.add), reads=["vt1", "vt2"], writes=[abk])

            gen_ab(0)
            for k in ks:
                AB = ABb[k % 2]
                ABK = "AB%d" % (k % 2)
                if k + 1 < 8:
                    gen_ab(k + 1)
                for ri in range(2):
                    for sb_ in range(2):
                        bank = 6 + (2 * ri + sb_) % 2
                        pbf = psb[bank].bitcast(BF)
                        p.pe_fence()
                        for s8 in range(8):
                            s = sb_ * 8 + s8
                            p.op("tensor", lambda e, pbf=pbf, ri=ri, s=s, s8=s8, AB=AB: e.transpose(
                                pbf[:, s8 * 128:(s8 + 1) * 128], AB[:, ri, s, :, :].rearrange("p g h -> p (g h)"), ident_bf),
                                reads=[ABK, "ident_bf"], writes=[PSK[bank]])
                        for g2 in range(2):
                            eng = A if g2 == 0 else V
                            src = pbf[:, 0:1024].rearrange("p (s q) -> p s q", s=8)
                            dst = ABt[:, g2, sb_ * 8:sb_ * 8 + 8, ri, :]
                            if g2 == 0:
                                p.op(A, lambda e, src=src, dst=dst, g2=g2: e.activation(out=dst, in_=src, func=AF.Identity, scale=parm_s[:, g2:g2 + 1]),
                                     reads=[PSK[bank], "parm_s"], writes=["ABt", PSK[bank]])
                            else:
                                p.op(V, lambda e, src=src, dst=dst, g2=g2: e.tensor_scalar(out=dst, in0=src, scalar1=parm_s[:, g2:g2 + 1], scalar2=None, op0=ALU.mult),
                                     reads=[PSK[bank], "parm_s"], writes=["ABt", PSK[bank]])
                for g2 in range(2):
                    for ri in range(2):
                        p.pe_fence()
                        for s in range(16):
                            for q in range(4):
                                p.op("tensor", lambda e, q=q, g2=g2, s=s, ri=ri, k=k: e.matmul(
                                    psb[q][:, 0:NCH], lhsT=ABt[32 * q:32 * q + 32, g2, s, ri, :], rhs=hT[32 * q:32 * q + 32, k, s:T:Q],
                                    start=(s == 0), stop=(s == 15), tile_position=(32 * q, 0)),
                                    reads=["ABt"] + hkeys(k), writes=[PSK[q]])
                        for q in range(4):
                            gl = 2 * q + g2
                            gh = 8 * k + gl
                            ps = psb[q]
                            p.op(A, lambda e, ps=ps, ri=ri, gh=gh: e.activation(out=VH[0:64, ri, gh, 1:NCH + 1], in_=ps[0:64, 0:NCH], func=AF.Identity),
                                 reads=[PSK[q]], writes=[("VH", ri, gh, 0)])
                            p.op(V, lambda e, ps=ps, ri=ri, gh=gh: e.tensor_copy(out=VH[64:128, ri, gh, 0:NCL], in_=ps[64:128, NCC:NCH]),
                                 reads=[PSK[q]], writes=[("VH", ri, gh, 1)])
                            p.op(V, lambda e, ps=ps, ri=ri, gh=gh: e.tensor_copy(out=VH[64:128, ri, gh, NCL:NCH], in_=ps[64:128, 0:NCC]),
                                 reads=[PSK[q]], writes=[("VH", ri, gh, 2)])
            ar.release()
            ar.release()
            vh_all = ["VH"] + [("VH", ri, gh, x) for ri in range(2) for gh in range(NGH) for x in range(3)]
            if opts.get("s5_stop") == 2:
                vdb = ar.take([2, 4, NCH + 1], F32)
                p.op(V, lambda e: e.tensor_copy(out=vdb, in_=VH[:, :, 0:4, :]), reads=vh_all, writes=["vdb"])
                dump(vdb.rearrange("p a g c -> p (a g c)"), 2 * 4 * (NCH + 1), ["vdb"])
                raise StopBuild()
            ar.mark()
            Hc = [ar.take([2, NGH], F32) for _ in range(2)]
            sP1 = ar.take([2, NGH], F32)
            sP2 = ar.take([2, NGH], F32)
            a1h = A1v
            a2h = A2v
            first = True
            SCAN_ENG = {"f": V, "r": V}
            for st in range(NCH - 1):
                cur = Hc[st % 2]
                nxt = Hc[(st + 1) % 2]
                chains = ((0, 64, st + 1, "f"), (64, 128, NCH - 1 - st, "r"))
                rk = vh_all if first else []
                if st == 0:
                    for (lo, hi_, col, tag) in chains:
                        vcol = VH[lo:hi_, :, :, col]
                        p.op(SCAN_ENG[tag], lambda e, nxt=nxt, lo=lo, hi_=hi_, vcol=vcol: e.tensor_copy(out=nxt[lo:hi_], in_=vcol),
                             reads=rk, writes=[("Hc", (st + 1) % 2, tag)])
                    first = False
                    continue
                for stage in range(6):
                    for (lo, hi_, col, tag) in chains:
                        eng = SCAN_ENG[tag]
                        vcol = VH[lo:hi_, :, :, col]
                        if stage == 0:
                            p.op(eng, lambda e, cur=cur, lo=lo, hi_=hi_, a1h=a1h: e.tensor_tensor(out=sP1[lo:hi_], in0=cur[lo:hi_], in1=a1h[lo:hi_], op=ALU.mult),
                                 reads=[("Hc", st % 2, tag), "A1v"], writes=[("sP1", tag)])
                        elif stage == 1:
                            p.op(eng, lambda e, cur=cur, lo=lo, hi_=hi_, a2h=a2h: e.tensor_tensor(out=sP2[lo:hi_], in0=cur[lo:hi_, ::-1, :], in1=a2h[lo:hi_], op=ALU.mult),
                                 reads=[("Hc", st % 2, tag), "A2v"], writes=[("sP2", tag, 0), ("sP2", tag, 1)])
                        elif stage == 2:
                            continue
                        elif stage == 3:
                            p.op(eng, lambda e, lo=lo, hi_=hi_: e.tensor_tensor(out=sP1[lo:hi_], in0=sP1[lo:hi_], in1=sP2[lo:hi_], op=ALU.add),
                                 reads=[("sP1", tag), ("sP2", tag, 0), ("sP2", tag, 1)], writes=[("sP1", tag)])
                        elif stage == 4:
                            p.op(eng, lambda e, nxt=nxt, lo=lo, hi_=hi_, vcol=vcol: e.tensor_tensor(out=nxt[lo:hi_], in0=sP1[lo:hi_], in1=vcol, op=ALU.add),
                                 reads=[("sP1", tag)], writes=[("Hc", (st + 1) % 2, tag)])
                        else:
                            p.op(A, lambda e, nxt=nxt, lo=lo, hi_=hi_, vcol=vcol: e.activation(out=vcol, in_=nxt[lo:hi_], func=AF.Identity),
                                 reads=[("Hc", (st + 1) % 2, tag)], writes=[("VHc", tag)])
            ar.release()
            vh_done = [("VHc", "f"), ("VHc", "r")] + vh_all
            if opts.get("s5_stop") == 3:
                vdb = ar.take([2, 4, NCH + 1], F32)
                p.op(V, lambda e: e.tensor_copy(out=vdb, in_=VH[:, :, 0:4, :]), reads=vh_done, writes=["vdb"])
                dump(vdb.rearrange("p a g c -> p (a g c)"), 2 * 4 * (NCH + 1), ["vdb"])
                raise StopBuild()
            ar.mark()
            c_kb = [ar.take([2, 8, 16], F32) for _ in range(2)]
            ncr_kb = [ar.take([8, 16], F32) for _ in range(2)]
            Cqb = [ar.take([2, 8, 17, 16], BF) for _ in range(2)]
            BBp = ar.take([2, 8, 128], BF)
            Kc = ar.take([8, 31, 16], BF)
            Yl = ar.take([16, 8, 16], BF)
            Yc = Yl
            ct1 = ar.take([2, 17, 16], F32)
            ct2 = ar.take([2, 17, 16], F32)
            kt = ar.take([16], F32)
            kt2 = ar.take([16], F32)
            ycar = [ar.take([512], F32)] * 2
            ycnt = 0

            def gen_cq(k):
                kb = k % 2
                c_k = c_kb[kb]
                ncr_k = ncr_kb[kb]
                Cq_ = Cqb[kb]
                ck, nk, qk = "c_k%d" % kb, "ncr_k%d" % kb, "Cq%d" % kb
                p.dma("sync", c_k, s5c[j][:, :, 8 * k:8 * k + 8, :], writes=[ck])
                p.op(G, lambda e: e.tensor_scalar(out=ncr_k, in0=c_k[:, 0], scalar1=-1.0, scalar2=None, op0=ALU.mult), reads=[ck], writes=[nk])
                for hh in range(4):
                    gsl = slice(8 * k + 2 * hh, 8 * k + 2 * hh + 2)
                    lsl = slice(2 * hh, 2 * hh + 2)
                    pcr = PCr[:, gsl, :].unsqueeze(3).broadcast_to([128, 2, 17, 16])
                    pci = PCi[:, gsl, :].unsqueeze(3).broadcast_to([128, 2, 17, 16])
                    cr = c_k[:, 0, lsl, :].unsqueeze(2).broadcast_to([128, 2, 17, 16])
                    ci = c_k[:, 1, lsl, :].unsqueeze(2).broadcast_to([128, 2, 17, 16])
                    ncr = ncr_k[:, lsl, :].unsqueeze(2).broadcast_to([128, 2, 17, 16])
                    p.op(G, lambda e, cr=cr, pcr=pcr: e.tensor_tensor(out=ct1, in0=cr, in1=pcr, op=ALU.mult), reads=[ck, "PCr"], writes=["ct1"])
                    p.op(G, lambda e, ci=ci, pci=pci: e.tensor_tensor(out=ct2, in0=ci, in1=pci, op=ALU.mult), reads=[ck, "PCi"], writes=["ct2"])
                    p.op(G, lambda e, lsl=lsl, Cq_=Cq_: e.tensor_tensor(out=Cq_[:, 0, lsl], in0=ct1, in1=ct2, op=ALU.subtract), reads=["ct1", "ct2"], writes=[qk])
                    p.op(G, lambda e, ncr=ncr, pci=pci: e.tensor_tensor(out=ct1, in0=ncr, in1=pci, op=ALU.mult), reads=[nk, "PCi"], writes=["ct1"])
                    p.op(G, lambda e, ci=ci, pcr=pcr: e.tensor_tensor(out=ct2, in0=ci, in1=pcr, op=ALU.mult), reads=[ck, "PCr"], writes=["ct2"])
                    p.op(G, lambda e, lsl=lsl, Cq_=Cq_: e.tensor_tensor(out=Cq_[:, 1, lsl], in0=ct1, in1=ct2, op=ALU.subtract), reads=["ct1", "ct2"], writes=[qk])

            for k in ks:
                kb = k % 2
                Cq = Cqb[kb]
                CQK = "Cq%d" % kb
                if k == ks[0]:
                    gen_cq(k)
                p.op(G, lambda e: e.memset(BBp, 0.0), writes=["BBp"])
                for gl in range(8):
                    p.op(G, lambda e, gl=gl, k=k: e.tensor_copy(out=BBp[:, :, gl, gl * 16:(gl + 1) * 16], in_=BBb[:, :, 8 * k + gl, :]),
                         reads=["BBb", "BBp"], writes=["BBp"])
                if k + 1 < 8:
                    gen_cq(k + 1)
                if opts.get("s5_tsub") == 1:
                    raise StopBuild()
                for gl in range(8):
                    bf_, br_ = (4, 5) if gl % 2 == 0 else (6, 7)
                    for dh, bank in ((0, bf_), (1, br_)):
                        lo = 64 * dh
                        p.pe_fence()
                        for ri in range(2):
                            p.op("tensor", lambda e, bank=bank, lo=lo, gl=gl, ri=ri, Cq=Cq: e.matmul(
                                psb[bank][:, 0:272], lhsT=BBp[lo:lo + 64, ri, gl, :], rhs=Cq[lo:lo + 64, ri, gl].rearrange("p j h -> p (j h)"),
                                start=(ri == 0), stop=(ri == 1)), reads=["BBp", CQK], writes=[PSK[bank]])
                    pf = psb[bf_]
                    pr_ = psb[br_]
                    p.op(A, lambda e, gl=gl, pr_=pr_: e.activation(out=Kc[:, gl, 0:15, :].rearrange("p l h -> p (l h)"), in_=pr_[:, 16:256], func=AF.Identity),
                         reads=[PSK[br_]], writes=[("Kc", gl, 0), PSK[br_]])
                    p.op(A, lambda e, gl=gl, pf=pf: e.activation(out=Kc[:, gl, 16:31, :].rearrange("p l h -> p (l h)"), in_=pf[:, 16:256], func=AF.Identity),
                         reads=[PSK[bf_]], writes=[("Kc", gl, 1), PSK[bf_]])
                    p.op(V, lambda e, gl=gl, k=k: e.tensor_scalar(out=kt2, in0=dmask_s[:, gl, :], scalar1=s5d_s[:, j, k:k + 1], scalar2=None, op0=ALU.mult),
                         reads=["dmask_s", "s5d_s"], writes=["kt2"])
                    p.op(V, lambda e, pf=pf: e.tensor_tensor(out=kt, in0=pf[:, 0:16], in1=kt2, op=ALU.add), reads=[PSK[bf_], "kt2"], writes=["kt", PSK[bf_]])
                    p.op(V, lambda e, gl=gl, pr_=pr_: e.tensor_tensor(out=Kc[:, gl, 15, :], in0=pr_[:, 256:272], in1=kt, op=ALU.add),
                         reads=["kt", PSK[br_]], writes=[("Kc", gl, 2), PSK[br_]])
                for (rows, tok0, fcol, rcol, is_lat) in ((NCC, 0, 0, NCL + 1, False), (NCL, CL, NCC, 1, True)):
                    hk_ = [("hT", k, gi_) for gi_ in range(1, 5)] if is_lat else [("hT", k, 0)]
                    for gp in range(4):
                        kkeys = [("Kc", gl, x) for gl in (2 * gp, 2 * gp + 1) for x in range(3)]
                        bank = ycnt % 2
                        bankb = 2 + ycnt % 2
                        ycnt += 1
                        py = psb[bank]
                        pyb = psb[bankb]
                        p.pe_fence()
                        for s in range(16):
                            p.op("tensor", lambda e, py=py, rows=rows, tok0=tok0, gp=gp, s=s, k=k: e.matmul(
                                py[0:rows, 0:512], lhsT=hT[:, k, tok0 + s:tok0 + rows * Q:Q],
                                rhs=Kc[:, 2 * gp:2 * gp + 2, 15 - s:31 - s, :].rearrange("p g l h -> p g (l h)"),
                                start=(s == 0), stop=(s == 15)), reads=kkeys + hk_, writes=[PSK[bank]])
                        for g2 in range(2):
                            gl = 2 * gp + g2
                            gh = 8 * k + gl
                            p.pe_fence()
                            for ri in range(2):
                                p.op("tensor", lambda e, pyb=pyb, rows=rows, fcol=fcol, ri=ri, gh=gh, gl=gl, g2=g2, Cq=Cq: e.matmul(
                                    pyb[0:rows, 256 * g2:256 * g2 + 256], lhsT=VH[0:64, ri, gh, fcol:fcol + rows], rhs=Cq[0:64, ri, gl, 1:17, :].rearrange("p j h -> p (j h)"),
                                    start=(ri == 0), stop=False), reads=vh_done + [CQK], writes=[PSK[bankb]])
                            p.pe_fence()
                            for ri in range(2):
                                p.op("tensor", lambda e, pyb=pyb, rows=rows, rcol=rcol, ri=ri, gh=gh, gl=gl, g2=g2, Cq=Cq: e.matmul(
                                    pyb[0:rows, 256 * g2:256 * g2 + 256], lhsT=VH[64:128, ri, gh, rcol:rcol + rows], rhs=Cq[64:128, ri, gl, 0:16, :].rearrange("p j h -> p (j h)"),
                                    start=False, stop=(ri == 1)), reads=vh_done + [CQK], writes=[PSK[bankb]])
                        ycs = ycar[0]
                        p.op(A, lambda e, pyb=pyb, rows=rows, ycs=ycs: e.activation(out=ycs[0:rows, :], in_=pyb[0:rows, 0:512], func=AF.Identity),
                             reads=[PSK[bankb]], writes=["ycar"])
                        p.op(V, lambda e, py=py, rows=rows, gp=gp, ycs=ycs: e.tensor_tensor(
                            out=Yl[0:rows, :, 2 * gp:2 * gp + 2, :], in0=py[0:rows, 0:512].rearrange("p (g t h) -> p t g h", g=2, t=16),
                            in1=ycs[0:rows, :].rearrange("p (g t h) -> p t g h", g=2, t=16), op=ALU.add),
                            reads=[PSK[bank], "ycar"], writes=[("Y", 2 * gp), ("Y", 2 * gp + 1)])
                    ylk = [("Y", gl) for gl in range(8)]
                    if is_lat:
                        for tb in range(2):
                            bank = 6 + tb
                            pbf = psb[bank].bitcast(BF)
                            p.pe_fence()
                            for t8 in range(8):
                                t = tb * 8 + t8
                                p.op("tensor", lambda e, pbf=pbf, t=t, t8=t8: e.transpose(pbf[:, t8 * 128:(t8 + 1) * 128], Yl[:, t, :, :].rearrange("p g h -> p (g h)"), ident_bf),
                                     reads=ylk + ["ident_bf"], writes=[PSK[bank]])
                            dst = hT[:, k, CL:T].rearrange("p (c t) -> p t c", t=Q)[:, tb * 8:tb * 8 + 8, :]
                            p.op(A, lambda e, pbf=pbf, dst=dst: e.activation(out=dst, in_=pbf[:, 0:1024].rearrange("p (t c) -> p t c", t=8), func=AF.Gelu_apprx_tanh),
                                 reads=[PSK[bank]] + hk_, writes=hk_ + [("yT", k, tb)])
                    else:
                        bank = 6
                        pbf = psb[bank].bitcast(BF)
                        p.pe_fence()
                        for t in range(16):
                            p.op("tensor", lambda e, pbf=pbf, t=t: e.transpose(pbf[:, t * 16:(t + 1) * 16], Yl[0:NCC, t, :, :].rearrange("p g h -> p (g h)"), ident_bf[0:NCC, 0:NCC]),
                                 reads=ylk + ["ident_bf"], writes=[PSK[bank]])
                        dst = hT[:, k, 0:CL].rearrange("p (c t) -> p t c", t=Q)
                        p.op(A, lambda e, pbf=pbf, dst=dst: e.activation(out=dst, in_=pbf[:, 0:256].rearrange("p (t c) -> p t c", t=16), func=AF.Gelu_apprx_tanh),
                             reads=[PSK[bank]] + hk_, writes=hk_ + [("yT", k, 2)])
            if opts.get("s5_stop") == 5:
                ydb = ar.take([2, 1024], F32)
                p.op(V, lambda e: e.tensor_copy(out=ydb[:, 0, :], in_=hT[:, 4, 0:1024]), reads=[("yT", 4, x) for x in range(3)], writes=["ydb"])
                p.op(V, lambda e: e.tensor_copy(out=ydb[:, 1, :], in_=hT[:, 7, 1280:2304]), reads=[("yT", 7, x) for x in range(3)] + ["ydb"], writes=["ydb"])
                dump(ydb.rearrange("p a t -> p (a t)"), 2048, ["ydb"])
                raise StopBuild()
            if opts.get("s5_stop") == 4:
                ydb = ar.take([2, 1024], F32)
                p.op(V, lambda e: e.tensor_copy(out=ydb[:, 0, :], in_=hT[:, 0, 0:1024]), reads=[("yT", 0, x) for x in range(3)], writes=["ydb"])
                p.op(V, lambda e: e.tensor_copy(out=ydb[:, 1, :], in_=hT[:, 3, 1280:2304]), reads=[("yT", 3, x) for x in range(3)] + ["ydb"], writes=["ydb"])
                dump(ydb.rearrange("p a t -> p (a t)"), 2048, ["ydb"])
                raise StopBuild()
            ar.release()
        ar.release()
        if opts.get("s5_stop") == 6:
            ydb = ar.take([2304], F32)
            for k in range(8):
                p.op(V, lambda e, k=k: e.tensor_copy(out=ydb, in_=hT[:, k, :]), reads=[("yT", k, x) for x in range(3)] + hkeys(k), writes=["ydb"])
                dump(ydb, 2304, ["ydb"])
            raise StopBuild()
        ar.mark()
        wg = ar.take([8, 2048], BF)
        sg = [ar.take([512], F32) for _ in range(2)]
        gt = [ar.take([512], F32) for _ in range(2)]
        p.dma("gpsimd", wg, s5_w_glu[j].rearrange("(k p) n -> p k n", p=128), writes=["wg"])
        ykeys = lambda k: [("yT", k, x) for x in range(3)] + hkeys(k)
        for gi, (t0, w, c) in enumerate(token_groups(True)):
            for d in range(8):
                pa = psb[(2 * d) % 4]
                pb_ = psb[(2 * d + 1) % 4]
                for k in range(8):
                    p.op("tensor", lambda e, pa=pa, k=k, d=d, t0=t0, w=w: e.matmul(pa[:, :w], lhsT=wg[:, k, d * 128:(d + 1) * 128], rhs=hT[:, k, t0:t0 + w],
                                                                               start=(k == 0), stop=(k == 7)), reads=["wg"] + ykeys(k), writes=[PSK[(2 * d) % 4]])
                for k in range(8):
                    p.op("tensor", lambda e, pb_=pb_, k=k, d=d, t0=t0, w=w: e.matmul(pb_[:, :w], lhsT=wg[:, k, 1024 + d * 128:1024 + (d + 1) * 128], rhs=hT[:, k, t0:t0 + w],
                                                                                start=(k == 0), stop=(k == 7)), reads=["wg"] + ykeys(k), writes=[PSK[(2 * d + 1) % 4]])
                s_t = sg[d % 2]
                g_t = gt[d % 2]
                p.op(A, lambda e, pb_=pb_, s_t=s_t, w=w: e.activation(out=s_t[:, :w], in_=pb_[:, :w], func=AF.Sigmoid),
                     reads=[PSK[(2 * d + 1) % 4]], writes=[("sg", d % 2)])
                p.op(V, lambda e, pa=pa, s_t=s_t, g_t=g_t, w=w: e.tensor_tensor(out=g_t[:, :w], in0=pa[:, :w], in1=s_t[:, :w], op=ALU.mult),
                     reads=[PSK[(2 * d) % 4], ("sg", d % 2)], writes=[("gt", d % 2)])
                p.op(V, lambda e, g_t=g_t, d=d, t0=t0, w=w, c=c: e.scalar_tensor_tensor(
                    out=xT[:, d, t0:t0 + w], in0=g_t[:, :w], scalar=mcol(i, 2, d, c), in1=xT[:, d, t0:t0 + w], op0=ALU.mult, op1=ALU.add),
                    reads=[("gt", d % 2), ("modT", i), ("xT", d)], writes=[("xT", d)])
        ar.release()

    def attn_phase(i):
        j = i // 2
        last = i == DEPTH - 1
        V = "vector"
        G = "gpsimd"
        A = "scalar"
        groups = token_groups(True)
        norm_phase(hT, "hT", lambda k, c: gs1[:, i, k, c:c + 1], lambda k, c: mcol(i, 0, k, c), groups,
                   [("gs1", i), ("modT", i)])
        hk = lambda k: [("hT", k, gi) for gi in range(5)]
        hall = [x for k in range(8) for x in hk(k)]
        ar.mark()
        qT = ar.take([8, T], BF)
        kT = ar.take([4, T], BF)
        vS = ar.take([18, 4, 65], BF)
        p.op(G, lambda e: e.memset(kT, 0.0), writes=["kTz"])
        ar.mark()
        cs_c = ar.take([L], BF)
        cs_s = ar.take([L], BF)
        bd = ar.take([128], BF)
        prm = ar.take([128], BF)
        gq = ar.take([2], F32)
        sqb = [ar.take([512], BF) for _ in range(2)]
        rsb = [ar.take([512], F32) for _ in range(2)]
        qnb = [ar.take([512], BF) for _ in range(2)]
        t1b = [ar.take([512], BF) for _ in range(2)]
        t2b = [ar.take([512], BF) for _ in range(2)]
        p.dma("gpsimd", cs_c, rope_c, writes=["cs_c"])
        p.dma("gpsimd", cs_s, rope_s, writes=["cs_s"])
        p.dma("gpsimd", bd, bd_in, writes=["bd"])
        p.dma("gpsimd", prm, prot_in, writes=["prm"])
        p.dma("sync", gq, qkg[:, j, :], writes=["gq"])
        p.op(G, lambda e: e.memset(vS[:, :, :, 64:65], 1.0), writes=["vS1"])
        itc = [0]

        def qk_chunk(wt, wkey, col0, dst, dkey, gidx, dst2=None):
            for gi, (t0, w, c) in enumerate(groups):
                it = itc[0] % 2
                itc[0] += 1
                sq, rs, qn, t1, t2 = sqb[it], rsb[it], qnb[it], t1b[it], t2b[it]
                ksq, krs, kqn, kt1, kt2 = "asq%d" % it, "ars%d" % it, "aqn%d" % it, "at1%d" % it, "at2%d" % it
                bq, bss, brp = it, 2 + 4 * it, 3 + 4 * it
                pq, pss_, prp = psb[bq], psb[bss], psb[brp]
                for k in range(8):
                    p.op("tensor", lambda e, pq=pq, k=k, t0=t0, w=w: e.matmul(pq[:, :w], lhsT=wt[:, k, col0:col0 + 128], rhs=hT[:, k, t0:t0 + w],
                                                                          start=(k == 0), stop=(k == 7)), reads=[wkey, ("hT", k, gi)], writes=[PSK[bq]])
                p.op(A, lambda e, pq=pq, w=w, sq=sq: e.activation(out=sq[:, :w], in_=pq[:, :w], func=AF.Square), reads=[PSK[bq]], writes=[ksq, PSK[bq]])
                p.op("tensor", lambda e, w=w, sq=sq, pss_=pss_: e.matmul(pss_[:, :w], lhsT=bd, rhs=sq[:, :w], start=True, stop=True), reads=["bd", ksq], writes=[PSK[bss]])
                p.op(A, lambda e, w=w, rs=rs, pss_=pss_: e.activation(out=rs[:, :w], in_=pss_[:, :w], func=AF.Sqrt, bias=EPS, scale=1.0), reads=[PSK[bss]], writes=[krs])
                p.op(V, lambda e, w=w, rs=rs: e.reciprocal(out=rs[:, :w], in_=rs[:, :w]), reads=[krs], writes=[krs])
                if c == 1 and dst2 is None:
                    p.op(V, lambda e, pq=pq, t0=t0, w=w, rs=rs: e.scalar_tensor_tensor(out=dst[:, t0:t0 + w], in0=pq[:, :w], scalar=gq[:, gidx:gidx + 1], in1=rs[:, :w],
                                                                                    op0=ALU.mult, op1=ALU.mult), reads=[PSK[bq], krs, "gq"], writes=[(dkey, gi), PSK[bq]])
                elif c == 1:
                    p.op(V, lambda e, pq=pq, t0=t0, w=w, rs=rs: e.scalar_tensor_tensor(out=dst[0:64, t0:t0 + w], in0=pq[0:64, :w], scalar=gq[0:64, gidx:gidx + 1], in1=rs[0:64, :w],
                                                                                    op0=ALU.mult, op1=ALU.mult), reads=[PSK[bq], krs, "gq", "kTz"], writes=[(dkey, gi), PSK[bq]])
                    p.op(V, lambda e, pq=pq, t0=t0, w=w, rs=rs: e.scalar_tensor_tensor(out=dst2[64:128, t0:t0 + w], in0=pq[64:128, :w], scalar=gq[64:128, gidx:gidx + 1], in1=rs[64:128, :w],
                                                                                    op0=ALU.mult, op1=ALU.mult), reads=[PSK[bq], krs, "gq", "kTz"], writes=[(dkey, gi, 1), PSK[bq]])
                else:
                    l0 = t0 - CL
                    p.op(V, lambda e, pq=pq, w=w, rs=rs, qn=qn: e.scalar_tensor_tensor(out=qn[:, :w], in0=pq[:, :w], scalar=gq[:, gidx:gidx + 1], in1=rs[:, :w],
                                                                                    op0=ALU.mult, op1=ALU.mult), reads=[PSK[bq], krs, "gq"], writes=[kqn, PSK[bq]])
                    p.op("tensor", lambda e, w=w, qn=qn, prp=prp: e.matmul(prp[:, :w], lhsT=prm, rhs=qn[:, :w], start=True, stop=True), reads=["prm", kqn], writes=[PSK[brp]])
                    p.op(G, lambda e, w=w, l0=l0, qn=qn, t1=t1: e.tensor_tensor(out=t1[:, :w], in0=qn[:, :w], in1=cs_c[:, l0:l0 + w], op=ALU.mult), reads=[kqn, "cs_c"], writes=[kt1])
                    p.op(V, lambda e, w=w, l0=l0, prp=prp, t2=t2: e.tensor_tensor(out=t2[:, :w], in0=prp[:, :w], in1=cs_s[:, l0:l0 + w], op=ALU.mult), reads=[PSK[brp], "cs_s"], writes=[kt2])
                    if dst2 is None:
                        p.op(V, lambda e, t0=t0, w=w, t1=t1, t2=t2: e.tensor_tensor(out=dst[:, t0:t0 + w], in0=t1[:, :w], in1=t2[:, :w], op=ALU.add), reads=[kt1, kt2], writes=[(dkey, gi)])
                    else:
                        p.op(V, lambda e, t0=t0, w=w, t1=t1, t2=t2: e.tensor_tensor(out=dst[0:64, t0:t0 + w], in0=t1[0:64, :w], in1=t2[0:64, :w], op=ALU.add),
                             reads=[kt1, kt2, "kTz"], writes=[(dkey, gi)])
                        p.op(G, lambda e, t0=t0, w=w, t1=t1, t2=t2: e.tensor_tensor(out=dst2[64:128, t0:t0 + w], in0=t1[64:128, :w], in1=t2[64:128, :w], op=ALU.add),
                             reads=[kt1, kt2, "kTz"], writes=[(dkey, gi, 1)])

        ar.mark()
        wkv = ar.take([8, 512], BF)
        p.dma("gpsimd", wkv, attn_w_qkv[j][:, 1024:1536].rearrange("(k p) n -> p k n", p=128), writes=["wkv"])
        for n in range(2):
            qk_chunk(wkv, "wkv", n * 128, kT[:, 2 * n, :], ("kT", n), 1, dst2=kT[:, 2 * n + 1, :])
        for tt in range(18):
            pv = psb[4 + tt % 2]
            for k in range(8):
                p.op("tensor", lambda e, pv=pv, k=k, tt=tt: e.matmul(pv[:, 0:256], lhsT=hT[:, k, tt * 128:(tt + 1) * 128], rhs=wkv[:, k, 256:512],
                                                                  start=(k == 0), stop=(k == 7)), reads=["wkv"] + hk(k), writes=[PSK[4 + tt % 2]])
            p.op(A, lambda e, pv=pv, tt=tt: e.activation(out=vS[:, tt, :, 0:64], in_=pv[:, 0:256].rearrange("p (h d) -> p h d", h=4), func=AF.Identity),
                 reads=[PSK[4 + tt % 2]], writes=[("vS", tt)])
        ar.release()
        ar.mark()
        wq = ar.take([8, 512], BF)
        for qh in range(2):
            p.dma("gpsimd", wq, attn_w_qkv[j][:, qh * 512:(qh + 1) * 512].rearrange("(k p) n -> p k n", p=128), writes=["wq"])
            for n4 in range(4):
                n = qh * 4 + n4
                qk_chunk(wq, "wq", n4 * 128, qT[:, n, :], ("qT", n), 0)
        ar.release()
        ar.release()
        ar.mark()
        wo = hT.rearrange("p k t -> p (k t)")[:, 0:16 * 1024].rearrange("p (h n) -> p h n", h=16)
        oTg = ar.take([16, 512], BF)
        pT = [ar.take([512], BF) for _ in range(4)]
        bcs = ar.take([512], F32)
        rec = ar.take([512], F32)
        ones_f = ar.take([64], F32)
        p.dma("gpsimd", wo[0:64], attn_w_o[j].rearrange("(h d) n -> d h n", d=64), writes=["wo"] + hall)
        p.op(G, lambda e: e.memset(wo[64:128], 0.0), writes=["wo2"] + hall)
        p.op(V, lambda e: e.memset(oTg[64:128], 0.0), writes=["oTg2"])
        p.op(V, lambda e: e.memset(ones_f, 1.0), writes=["ones_f"])
        vkeys = [("vS", tt) for tt in range(18)] + ["vS1"]
        SCALE = HD ** -0.5
        qgroups = [(gi, t0, w, c) for gi, (t0, w, c) in enumerate(groups) if not (c == 1 and last)]
        scnt = 0
        for (gi, t0, w, c) in qgroups:
            ktiles = range(2) if c == 1 else range(18)
            nkt = len(ktiles)
            pending = []
            for h in range(16):
                kv = h // 4
                half = kv % 2
                lo = 64 * half
                perm_pos = QPERM.index(h)
                qn_, qhalf = perm_pos // 2, perm_pos % 2
                assert qhalf == half
                po = psb[4 + h % 2]
                kts = list(ktiles)

                def score(ti, ps_, bs, lo=lo, kv=kv, qn_=qn_, t0=t0, w=w, gi=gi):
                    kt = kts[ti]
                    p.op("tensor", lambda e: e.matmul(
                        ps_[:, :w], lhsT=kT[:, kv, kt * 128:(kt + 1) * 128], rhs=qT[:, qn_, t0:t0 + w], start=True, stop=True),
                        reads=[(("kT", kv // 2), g_) for g_ in range(5)] + [(("kT", kv // 2), g_, 1) for g_ in range(5)] + ["kTz", (("qT", qn_), gi)], writes=[PSK[bs]])

                LOOK = 3
                slots = []
                for ti in range(min(LOOK, nkt)):
                    bs = scnt % 4
                    scnt += 1
                    slots.append(bs)
                    score(ti, psb[bs], bs)
                for ti in range(nkt):
                    bs = slots[ti]
                    ps_ = psb[bs]
                    pt_ = pT[bs]
                    kt = kts[ti]
                    p.op(A, lambda e, ps_=ps_, pt_=pt_, w=w: e.activation(out=pt_[:, :w], in_=ps_[:, :w], func=AF.Exp, scale=SCALE),
                         reads=[PSK[bs]], writes=[("pT", bs)])
                    if ti == min(8, nkt - 1) and pending:
                        pending.pop(0)()
                    if ti + LOOK < nkt:
                        nb_ = scnt % 4
                        scnt += 1
                        slots.append(nb_)
                        score(ti + LOOK, psb[nb_], nb_)
                    p.op("tensor", lambda e, po=po, pt_=pt_, kt=kt, kv=kv, w=w, ti=ti, nkt=nkt: e.matmul(
                        po[0:65, :w], lhsT=vS[:, kt, kv, :], rhs=pt_[:, :w], start=(ti == 0), stop=(ti == nkt - 1)),
                        reads=vkeys + [("pT", bs)], writes=[PSK[4 + h % 2]])
                def norm_head(po=po, h=h, w=w):
                    p.op(V, lambda e: e.reciprocal(out=rec[64:65, :w], in_=po[64:65, :w]), reads=[PSK[4 + h % 2]], writes=["rec", PSK[4 + h % 2]])
                    p.op("tensor", lambda e: e.matmul(psb[6][0:64, :w], lhsT=ones_f[64:65, 0:64], rhs=rec[64:65, :w], start=True, stop=True),
                         reads=["rec", "ones_f"], writes=[PSK[6]])
                    p.op(V, lambda e: e.tensor_copy(out=bcs[0:64, :w], in_=psb[6][0:64, :w]), reads=[PSK[6]], writes=["bcs", PSK[6]])
                    p.op(V, lambda e: e.tensor_tensor(out=oTg[0:64, h, :w], in0=po[0:64, :w], in1=bcs[0:64, :w], op=ALU.mult),
                         reads=[PSK[4 + h % 2], "bcs"], writes=[("oTg", h), PSK[4 + h % 2]])
                pending.append(norm_head)
            while pending:
                pending.pop(0)()
            for n in range(8):
                pw = psb[7] if n % 2 == 0 else psb[6]
                pwk = PSK[7] if n % 2 == 0 else PSK[6]
                for h in range(16):
                    p.op("tensor", lambda e, pw=pw, h=h, n=n, w=w: e.matmul(pw[:, :w], lhsT=wo[:, h, n * 128:(n + 1) * 128], rhs=oTg[:, h, :w],
                                                                       start=(h == 0), stop=(h == 15)), reads=["wo", "wo2", "oTg2", ("oTg", h)], writes=[pwk])
                p.op(V, lambda e, pw=pw, n=n, t0=t0, w=w, c=c: e.scalar_tensor_tensor(
                    out=xT[:, n, t0:t0 + w], in0=pw[:, :w], scalar=mcol(i, 2, n, c), in1=xT[:, n, t0:t0 + w], op0=ALU.mult, op1=ALU.add),
                    reads=[pwk, ("modT", i), ("xT", n)], writes=[("xT", n)])
        ar.release()
        ar.release()

    hT = ar.take([8, T], BF)
    for i in range(DEPTH) if not opts.get("s5_stop") else []:
        last = i == DEPTH - 1
        if i == 0 or not opts.get("mod_side", True):
            mod_phase(i)
        if opts.get("only_mod"):
            dump(modT[:, 0].rearrange("p a b -> p (a b)"), 96, [("modT", 0)])
            break
        if mixers and i % 2 == 1 and opts.get("attn", True):
            attn_phase(i)
            if opts.get("stop_after") == (i, "mix"):
                break
        if mixers and i % 2 == 0 and opts.get("s5", True):
            s5_phase(i)
            if opts.get("stop_after") == (i, "mix"):
                break
        groups = token_groups(include_ctx=not last)
        norm_phase(hT, "hT", lambda k, c, i=i: gs2[:, i, k, c:c + 1], lambda k, c, i=i: mcol(i, 3, k, c), groups,
                   [("gs2", i), ("modT", i)])
        ffn_phase(i, hT, "hT", groups, side_layer=(i + 1 if (not last and opts.get("mod_side", True)) else None))

    if opts.get("s5_stop"):
        try:
            mod_phase(0)
            s5_phase(0)
        except StopBuild:
            pass
        p.emit()
        return nc, p
    if opts.get("only_mod"):
        p.emit()
        return nc, p
    if opts.get("stop_after"):
        for k in range(8):
            p.dma("sync", outT[k * 128:(k + 1) * 128, :], xT[:, k, CL:T], reads=[("xT", k)])
            p.dma("sync", dbg[:, k * 256:(k + 1) * 256], xT[:, k, 0:CL], reads=[("xT", k)])
        p.emit()
        return nc, p
    ar.mark()
    sq = ar.take([8, 512], BF)
    rstd = ar.take([512], F32)
    ost = [ar.take([8, 512], F32) for _ in range(2)]
    pss = psb[6]
    for gi, (t0, w, c) in enumerate(token_groups(include_ctx=False)):
        o_t = ost[gi % 2]
        for k in range(8):
            p.op("scalar", lambda e, k=k, t0=t0, w=w: e.activation(out=sq[:, k, :w], in_=xT[:, k, t0:t0 + w], func=AF.Square),
                 reads=[("xT", k)], writes=[("fsq", k)])
        for k in range(8):
            p.op("tensor", lambda e, k=k, w=w: e.matmul(pss[:, :w], lhsT=ones_bf, rhs=sq[:, k, :w], start=(k == 0), stop=(k == 7)),
                 reads=[("fsq", k), "ones_bf"], writes=[PSK[6]])
        p.op("scalar", lambda e, w=w: e.activation(out=rstd[:, :w], in_=pss[:, :w], func=AF.Sqrt, bias=EPS, scale=1.0 / D),
             reads=[PSK[6]], writes=["frstd"])
        p.op("vector", lambda e, w=w: e.reciprocal(out=rstd[:, :w], in_=rstd[:, :w]), reads=["frstd"], writes=["frstd"])
        for k in range(8):
            p.op("vector", lambda e, k=k, t0=t0, w=w, o_t=o_t: e.scalar_tensor_tensor(
                out=o_t[:, k, :w], in0=xT[:, k, t0:t0 + w], scalar=gfin_s[:, k:k + 1], in1=rstd[:, :w], op0=ALU.mult, op1=ALU.mult),
                reads=[("xT", k), "frstd", "gfin_s"], writes=[("ost", gi % 2, k)])
            p.dma("sync", outT[k * 128:(k + 1) * 128, t0 - CL:t0 - CL + w], o_t[:, k, :w], reads=[("ost", gi % 2, k)])
    ar.release()
    p.emit()
    return nc, p


def _cols(v):
    return np.ascontiguousarray(np.asarray(v, np.float32).reshape(-1, 128).T)


def make_in_maps(inputs):
    x = np.asarray(inputs["x"], np.float32)
    ctx = np.asarray(inputs["ctx"], np.float32)
    c = np.asarray(inputs["c"], np.float32)
    c_ctx = np.asarray(inputs["c_ctx"], np.float32)
    shared = {
        "adab": np.ascontiguousarray(np.stack([_cols(inputs["ada_b"][i]) for i in range(DEPTH)], axis=1)),
        "gmix": np.ascontiguousarray(np.stack([_cols(inputs["norm_mix_g"][i]) for i in range(DEPTH)], axis=1)),
        "gffn": np.ascontiguousarray(np.stack([_cols(inputs["norm_ffn_g"][i]) for i in range(DEPTH)], axis=1)),
        "gfin": _cols(inputs["final_g"]),
        "ada_w": np.ascontiguousarray(inputs["ada_w"], np.float32),
        "ffn_w1": np.ascontiguousarray(inputs["ffn_w1"], np.float32),
        "ffn_w2": np.ascontiguousarray(inputs["ffn_w2"], np.float32),
        "ident": np.eye(128, dtype=np.float32),
    }
    f32 = lambda a: np.ascontiguousarray(a, dtype=np.float32)
    a_re = np.asarray(inputs["s5_a_re"]); a_im = np.asarray(inputs["s5_a_im"]); ldt = np.asarray(inputs["s5_log_dt"])
    shared["s5ar"] = f32(a_re.transpose(0, 1, 3, 2).reshape(2, 128, 64))
    shared["s5ai"] = f32(a_im.transpose(0, 1, 3, 2).reshape(2, 128, 64))
    shared["s5ldt"] = f32(np.broadcast_to(ldt[:, :, None, :], (2, 2, 64, 64)).reshape(2, 128, 64))
    bre = np.asarray(inputs["s5_b_re"]).transpose(0, 1, 3, 2, 4).reshape(2, 128, 64, 16)
    bim = np.asarray(inputs["s5_b_im"]).transpose(0, 1, 3, 2, 4).reshape(2, 128, 64, 16)
    shared["s5b"] = f32(np.stack([bre, bim], axis=2))
    cre = np.asarray(inputs["s5_c_re"]).transpose(0, 1, 4, 2, 3).reshape(2, 128, 64, 16)
    cim = np.asarray(inputs["s5_c_im"]).transpose(0, 1, 4, 2, 3).reshape(2, 128, 64, 16)
    shared["s5c"] = f32(np.stack([cre, cim], axis=2))
    shared["s5d"] = f32(np.stack([_cols(inputs["s5_d"][j]) for j in range(2)], axis=1))
    ex = np.zeros((128, 35), np.float32)
    ex[:64, 0:17] = np.arange(17); ex[64:, 0:17] = 16 - np.arange(17)
    ex[:64, 17:33] = 15 - np.arange(16); ex[64:, 17:33] = np.arange(16)
    ex[:, 33] = 1.0; ex[:, 34] = 16.0
    shared["expo"] = ex
    gl = np.arange(128) // 16
    hi = np.arange(128) % 16
    parm = np.stack([(gl % 2 == 0), (gl % 2 == 1)], axis=1).astype(np.float32)
    shared["parm"] = f32(parm)
    shared["dmask"] = f32((gl[:, None, None] == np.arange(8)[None, :, None]) * (hi[:, None, None] == np.arange(16)[None, None, :]))
    shared["s5_w_glu"] = f32(inputs["s5_w_glu"])
    wqkv = np.asarray(inputs["attn_w_qkv"], np.float32)
    qcols = np.concatenate([np.arange(h * 64, (h + 1) * 64) for h in QPERM])
    shared["attn_w_qkv"] = f32(np.concatenate([wqkv[:, :, qcols], wqkv[:, :, 1024:]], axis=2))
    shared["attn_w_o"] = f32(inputs["attn_w_o"])
    tpos = np.arange(L)
    rowp = (tpos // 64).astype(np.float64); colp = (tpos % 64).astype(np.float64)
    invf = 10000.0 ** (-np.arange(16, dtype=np.float64) / 16)
    ang = np.zeros((64, L))
    ang[0:16] = invf[:, None] * rowp[None, :]; ang[16:32] = ang[0:16]
    ang[32:48] = invf[:, None] * colp[None, :]; ang[48:64] = ang[32:48]
    shared["rope_c"] = f32(np.tile(np.cos(ang), (2, 1)))
    shared["rope_s"] = f32(np.tile(np.sin(ang), (2, 1)))
    bdm = np.zeros((128, 128), np.float32)
    bdm[:64, :64] = 1.0 / 64; bdm[64:, 64:] = 1.0 / 64
    shared["bd"] = bdm
    pr = np.zeros((128, 128), np.float32)
    for base in (0, 32, 64, 96):
        for d_ in range(16):
            pr[base + d_ + 16, base + d_] = -1.0
            pr[base + d_, base + d_ + 16] = 1.0
    shared["prot"] = pr
    qg = np.asarray(inputs["attn_q_g"], np.float32); kg = np.asarray(inputs["attn_k_g"], np.float32)
    shared["qkg"] = f32(np.stack([np.tile(qg, (1, 2)).T, np.tile(kg, (1, 2)).T], axis=2))
    maps = []
    for b in range(8):
        m = dict(shared)
        m["xin"] = np.ascontiguousarray(np.concatenate([ctx[b], x[b]], axis=0).T)
        m["ccols"] = np.ascontiguousarray(np.stack([_cols(c[b]), _cols(c_ctx)], axis=2))
        maps.append(m)
    return maps


def kernel(**inputs):
    nc, _ = build()
    maps = make_in_maps(inputs)
    res = run_bass_kernel_spmd(nc, maps, core_ids=list(range(8)))
    out = np.stack([np.ascontiguousarray(res.results[b]["outT"].T) for b in range(8)], axis=0)
    return out.astype(np.float32)
```

```python
import math
from contextlib import ExitStack
import numpy as np
import concourse.bass as bass
import concourse.mybir as mybir
from concourse.bass_utils import run_bass_kernel_spmd

F32 = mybir.dt.float32
BF = mybir.dt.bfloat16
AF = mybir.ActivationFunctionType
ALU = mybir.AluOpType

D = 1024
DEPTH = 4
L = 2048
CL = 256
T = L + CL
NG = 64
NP = 64
NH = 16
Q = 16
NCH = T // Q
NCC = CL // Q
NCL = L // Q
HD = 64
NHEAD = 16
NKV = 4
DFF = 4096
EPS = 1e-6

QPERM = [0, 4, 1, 5, 2, 6, 3, 7, 8, 12, 9, 13, 10, 14, 11, 15]
COMPUTE = ("tensor", "vector", "scalar", "gpsimd")
NSLOT = 8


class Op:
    __slots__ = ("eng", "fn", "reads", "writes", "dma", "idx", "deps", "waits",
                 "needs_inc", "ctr", "slot", "slot_use", "gidx", "force")

    def __init__(self, eng, fn, reads, writes, dma):
        self.eng, self.fn, self.reads, self.writes, self.dma = eng, fn, reads, writes, dma
        self.deps = []
        self.waits = []
        self.needs_inc = False
        self.ctr = 0
        self.slot = None
        self.slot_use = 0


class Prog:
    def __init__(self, nc, strict=True):
        self.nc = nc
        self.strict = strict
        self.ops = []
        self.stack = ExitStack()
        self.last_w = {}
        self.readers = {}
        self.bar_tile = None
        self.last_bar = None
        self.bar_from = 0
        self.fence = False
        self.last_pe = None

    def sb(self, name, shape, dtype):
        return self.stack.enter_context(self.nc.sbuf_tensor(name, list(shape), dtype))

    def ps(self, name, shape, dtype):
        return self.stack.enter_context(self.nc.psum_tensor(name, list(shape), dtype))

    def op(self, eng, fn, reads=(), writes=(), dma=False):
        o = Op(eng, fn, tuple(reads), tuple(writes), dma)
        o.gidx = len(self.ops)
        deps = set()
        for k in o.reads:
            w = self.last_w.get(k)
            if w is not None:
                deps.add(w)
        for k in o.writes:
            w = self.last_w.get(k)
            if w is not None:
                deps.add(w)
            for r in self.readers.get(k, ()):
                deps.add(r)
        deps.discard(o.gidx)
        if self.last_bar is not None:
            deps.add(self.last_bar)
        o.force = None
        if eng == "tensor":
            if self.fence and self.last_pe is not None:
                deps.add(self.last_pe)
                o.force = self.last_pe
            self.fence = False
            self.last_pe = o.gidx
        o.deps = sorted(deps)
        for k in o.reads:
            self.readers.setdefault(k, []).append(o.gidx)
        for k in o.writes:
            self.last_w[k] = o.gidx
            self.readers[k] = []
        self.ops.append(o)
        return o

    def pe_fence(self):
        self.fence = True

    def barrier(self):
        if self.bar_tile is None:
            self.bar_tile = self.sb("bar_tile", [128, 8], F32)
        o = Op("vector", lambda e: e.memset(self.bar_tile[:, 0:1], 0.0), (), (), False)
        o.force = None
        o.gidx = len(self.ops)
        lastc = {}
        deps = set()
        for q in self.ops[self.bar_from:]:
            if q.dma:
                deps.add(q.gidx)
            else:
                lastc[q.eng] = q.gidx
        deps.update(lastc.values())
        if self.last_bar is not None:
            deps.add(self.last_bar)
        o.deps = sorted(deps)
        self.ops.append(o)
        self.last_bar = o.gidx
        self.bar_from = len(self.ops)
        return o

    def dma(self, eng, out, in_, reads=(), writes=(), **kw):
        return self.op(eng, lambda e: e.dma_start(out=out, in_=in_, **kw), reads, writes, dma=True)

    def emit(self):
        nc = self.nc
        ops = self.ops
        engs = ("tensor", "vector", "scalar", "gpsimd", "sync")
        per = {e: [] for e in engs}
        for o in ops:
            o.idx = len(per[o.eng])
            per[o.eng].append(o)
        dcount = {e: 0 for e in engs}
        for o in ops:
            if o.dma:
                o.slot = dcount[o.eng] % NSLOT
                o.slot_use = dcount[o.eng] // NSLOT + 1
                dcount[o.eng] += 1
        known = {e: {f: -1 for f in COMPUTE} for e in engs}
        kdma = {e: {} for e in engs}
        snap = [None] * len(ops)
        for o in ops:
            E = o.eng
            kn = known[E]
            for d in reversed(o.deps):
                pr = ops[d]
                if pr.dma:
                    key = (pr.eng, pr.slot)
                    if kdma[E].get(key, 0) >= pr.slot_use:
                        continue
                    kdma[E][key] = pr.slot_use
                    o.waits.append(("dma", pr.eng, pr.slot, pr.slot_use))
                    psn = snap[d]
                    for f in COMPUTE:
                        if psn[f] > kn[f]:
                            kn[f] = psn[f]
                else:
                    Fe = pr.eng
                    if Fe == E and not o.dma and (Fe == "tensor" or not self.strict) and getattr(o, "force", None) != d:
                        continue
                    if kn[Fe] >= pr.idx:
                        continue
                    o.waits.append(("eng", d))
                    pr.needs_inc = True
                    psn = snap[d]
                    for f in COMPUTE:
                        if psn[f] > kn[f]:
                            kn[f] = psn[f]
                    if pr.idx > kn[Fe]:
                        kn[Fe] = pr.idx
            if o.dma and o.slot_use > 1:
                key = (o.eng, o.slot)
                if kdma[E].get(key, 0) < o.slot_use - 1:
                    kdma[E][key] = o.slot_use - 1
                    o.waits.append(("dma", o.eng, o.slot, o.slot_use - 1))
            snap[o.gidx] = dict(kn)
        for e in COMPUTE:
            c = 0
            for o in per[e]:
                if o.needs_inc and not o.dma:
                    c += 1
                    o.ctr = c
        st = self.stack
        esem = {e: st.enter_context(nc.semaphore("c_" + e)) for e in COMPUTE}
        dsem = {}
        for e in engs:
            for s in range(min(NSLOT, dcount[e])):
                dsem[(e, s)] = st.enter_context(nc.semaphore("d_%s_%d" % (e, s)))
        last_use = {}
        for o in ops:
            if o.dma:
                last_use[(o.eng, o.slot)] = o.slot_use

        def run(e, ename):
            for o in per[ename]:
                for w in o.waits:
                    if w[0] == "dma":
                        e.wait_ge(dsem[(w[1], w[2])], 16 * w[3])
                    else:
                        pr = ops[w[1]]
                        e.wait_ge(esem[pr.eng], pr.ctr)
                ins = o.fn(e)
                if o.dma:
                    ins.then_inc(dsem[(o.eng, o.slot)], 16)
                elif o.needs_inc:
                    ins.then_inc(esem[o.eng], 1)
            for (qe, s), u in last_use.items():
                if qe == ename:
                    e.wait_ge(dsem[(qe, s)], 16 * u)

        with nc.Block() as block:
            for ename in engs:
                if per[ename]:
                    getattr(block, ename)(lambda e, ename=ename: run(e, ename))
        st.close()
        self.stats = {e: len(per[e]) for e in engs}


class StopBuild(Exception):
    pass


class Arena:
    def __init__(self, p, name, words):
        self.t = p.sb(name, [128, words], F32)
        self.p = p
        self.words = words
        self.off = 0
        self.marks = []

    def take(self, shape, dtype):
        n = 1
        for s in shape:
            n *= s
        w = n if dtype == F32 else (n + 1) // 2
        assert self.off + w <= self.words, ("arena overflow", self.off, w, self.words)
        ap = self.t[:, self.off:self.off + w]
        self.off += w
        if dtype != F32:
            ap = ap.bitcast(dtype)[:, 0:n]
        if len(shape) == 2:
            ap = ap.rearrange("p (a b) -> p a b", a=shape[0])
        elif len(shape) == 3:
            ap = ap.rearrange("p (a b c) -> p a b c", a=shape[0], b=shape[1])
        elif len(shape) == 4:
            ap = ap.rearrange("p (a b c d) -> p a b c d", a=shape[0], b=shape[1], c=shape[2])
        return ap

    def mark(self):
        self.marks.append(self.off)

    def release(self):
        self.off = self.marks.pop()
        self.p.barrier()


def token_groups(include_ctx=True):
    g = []
    if include_ctx:
        g.append((0, CL, 1))
    for i in range(L // 512):
        g.append((CL + i * 512, 512, 0))
    return g


def build(opts=None):
    opts = opts or {}
    mixers = opts.get("mixers", True)
    nc = bass.Bass("TRN2", target_bir_lowering=False)
    dr = {}

    def din(name, shape, dtype=F32):
        dr[name] = nc.dram_tensor(name, list(shape), dtype, kind="ExternalInput").ap()
        return dr[name]

    xin = din("xin", [D, T])
    ccols = din("ccols", [128, 8, 2])
    adab = din("adab", [128, DEPTH, 48])
    gmix = din("gmix", [128, DEPTH, 8])
    gffn = din("gffn", [128, DEPTH, 8])
    gfin = din("gfin", [128, 8])
    ada_w = din("ada_w", [DEPTH, D, 6 * D])
    ffn_w1 = din("ffn_w1", [DEPTH, D, DFF])
    ffn_w2 = din("ffn_w2", [DEPTH, DFF, D])
    ident_in = din("ident", [128, 128])
    s5ar = din("s5ar", [2, 128, 64])
    s5ai = din("s5ai", [2, 128, 64])
    s5ldt = din("s5ldt", [2, 128, 64])
    s5b = din("s5b", [2, 128, 2, 64, 16])
    s5c = din("s5c", [2, 128, 2, 64, 16])
    s5d_in = din("s5d", [128, 2, 8])
    expo = din("expo", [128, 35])
    parm_in = din("parm", [128, 2])
    dmask_in = din("dmask", [128, 8, 16])
    s5_w_glu = din("s5_w_glu", [2, D, 2 * D])
    attn_w_qkv = din("attn_w_qkv", [2, D, 1536])
    attn_w_o = din("attn_w_o", [2, D, D])
    rope_c = din("rope_c", [128, L])
    rope_s = din("rope_s", [128, L])
    bd_in = din("bd", [128, 128])
    prot_in = din("prot", [128, 128])
    qkg = din("qkg", [128, 2, 2])
    outT = nc.dram_tensor("outT", [D, L], F32, kind="ExternalOutput").ap()
    dbg = nc.dram_tensor("dbg", [128, 20480], F32, kind="ExternalOutput").ap() if opts.get("debug") else None
    dbg_off = [0]

    def dump(ap2d, n, keys):
        if dbg is None:
            return
        p.dma("sync", dbg[:, dbg_off[0]:dbg_off[0] + n], ap2d, reads=keys)
        print("dump at", dbg_off[0], n, keys)
        dbg_off[0] += n

    p = Prog(nc)
    ar = Arena(p, "arena", 53200)
    psb = [p.ps("psb%d" % i, [128, 512], F32) for i in range(8)]
    PSK = ["psb%d" % i for i in range(8)]

    xT = ar.take([8, T], F32)
    modT = ar.take([DEPTH, 48, 2], F32)
    adab_s = ar.take([DEPTH, 48], F32)
    gmix_s = ar.take([DEPTH, 8], F32)
    gffn_s = ar.take([DEPTH, 8], F32)
    gfin_s = ar.take([8], F32)
    gs1 = ar.take([DEPTH, 8, 2], F32)
    gs2 = ar.take([DEPTH, 8, 2], F32)
    cc_s = ar.take([8, 2], F32)
    sc_bf = ar.take([8, 2], BF)
    ones_bf = ar.take([128], BF)
    ident_f = ar.take([128], F32)
    ident_bf = ar.take([128], BF)

    s5d_s = ar.take([2, 8], F32)
    parm_s = ar.take([2], F32)
    dmask_s = ar.take([8, 16], F32)
    p.dma("sync", s5d_s, s5d_in, writes=["s5d_s"])
    p.dma("sync", parm_s, parm_in, writes=["parm_s"])
    p.dma("sync", dmask_s, dmask_in, writes=["dmask_s"])
    for k in range(8):
        p.dma("sync", xT[:, k, :], xin[k * 128:(k + 1) * 128, :], writes=[("xT", k)])
    p.dma("sync", cc_s, ccols, writes=["cc_s"])
    p.dma("sync", adab_s, adab, writes=["adab_s"])
    p.dma("sync", gmix_s, gmix, writes=["gmix_s"])
    p.dma("sync", gffn_s, gffn, writes=["gffn_s"])
    p.dma("sync", gfin_s, gfin, writes=["gfin_s"])
    p.dma("sync", ident_f, ident_in, writes=["ident_f"])
    p.op("vector", lambda e: e.memset(ones_bf, 1.0), writes=["ones_bf"])
    p.op("vector", lambda e: e.tensor_copy(out=ident_bf, in_=ident_f), reads=["ident_f"], writes=["ident_bf"])
    p.op("scalar", lambda e: e.activation(out=sc_bf, in_=cc_s, func=AF.Silu), reads=["cc_s"], writes=["sc_bf"])

    def mod_phase(i):
        ar.mark()
        wA = [ar.take([8, 1024], BF) for _ in range(2)]
        pm = psb[7]
        for piece in range(6):
            wa = wA[piece % 2]
            wk = ("wA", piece % 2)
            p.dma("gpsimd", wa, ada_w[i][:, piece * 1024:(piece + 1) * 1024].rearrange("(k p) n -> p k n", p=128),
                  writes=[wk])
            for n in range(8):
                j = piece * 8 + n
                for k in range(8):
                    p.op("tensor", lambda e, wa=wa, k=k, n=n, j=j: e.matmul(
                        pm[:, 2 * j:2 * j + 2], lhsT=wa[:, k, n * 128:(n + 1) * 128], rhs=sc_bf[:, k, :],
                        start=(k == 0), stop=(k == 7)), reads=[wk, "sc_bf"], writes=[PSK[7]])
        p.op("vector", lambda e: e.tensor_tensor(
            out=modT[:, i], in0=pm[:, 0:96].rearrange("p (j c) -> p j c", c=2),
            in1=adab_s[:, i].unsqueeze(2).broadcast_to([128, 48, 2]), op=ALU.add),
            reads=[PSK[7], "adab_s"], writes=[("modT", i)])
        for (dst, dk, g_s, gk, which) in ((gs1, "gs1", gmix_s, "gmix_s", 1), (gs2, "gs2", gffn_s, "gffn_s", 4)):
            p.op("vector", lambda e, dst=dst, g_s=g_s, which=which: e.scalar_tensor_tensor(
                out=dst[:, i], in0=modT[:, i, which * 8:(which + 1) * 8, :], scalar=1.0,
                in1=g_s[:, i].unsqueeze(2).broadcast_to([128, 8, 2]), op0=ALU.add, op1=ALU.mult),
                reads=[("modT", i), gk], writes=[(dk, i)])
        ar.release()

    def mod_finish(i):
        pm = psb[7]
        p.op("vector", lambda e: e.tensor_tensor(
            out=modT[:, i], in0=pm[:, 0:96].rearrange("p (j c) -> p j c", c=2),
            in1=adab_s[:, i].unsqueeze(2).broadcast_to([128, 48, 2]), op=ALU.add),
            reads=[PSK[7], "adab_s"], writes=[("modT", i)])
        for (dst, dk, g_s, gk, which) in ((gs1, "gs1", gmix_s, "gmix_s", 1), (gs2, "gs2", gffn_s, "gffn_s", 4)):
            p.op("vector", lambda e, dst=dst, g_s=g_s, which=which: e.scalar_tensor_tensor(
                out=dst[:, i], in0=modT[:, i, which * 8:(which + 1) * 8, :], scalar=1.0,
                in1=g_s[:, i].unsqueeze(2).broadcast_to([128, 8, 2]), op0=ALU.add, op1=ALU.mult),
                reads=[("modT", i), gk], writes=[(dk, i)])

    def mod_side(i, bufs):
        pm = psb[7]
        pieces = []
        for jn in range(48):
            def piece(jn=jn):
                wa = bufs[jn % len(bufs)]
                wk = ("wAs", jn % len(bufs))
                p.dma("gpsimd", wa, ada_w[i][:, jn * 128:(jn + 1) * 128].rearrange("(k p) n -> p k n", p=128), writes=[wk])
                for k in range(8):
                    p.op("tensor", lambda e, wa=wa, k=k: e.matmul(pm[:, 2 * jn:2 * jn + 2], lhsT=wa[:, k, :], rhs=sc_bf[:, k, :],
                                                               start=(k == 0), stop=(k == 7)), reads=[wk, "sc_bf"], writes=[PSK[7]])
            pieces.append(piece)
        return pieces

    def mcol(i, which, k, c):
        return modT[:, i, which * 8 + k, c:c + 1]

    def norm_phase(hT, hkey, gcol, shcol, groups, rkeys):
        ar.mark()
        sq = ar.take([8, 512], BF)
        rstd = ar.take([512], F32)
        tmp = [ar.take([512], F32) for _ in range(2)]
        pss = psb[6]
        for gi, (t0, w, c) in enumerate(groups):
            for k in range(8):
                p.op("scalar", lambda e, k=k, t0=t0, w=w: e.activation(out=sq[:, k, :w], in_=xT[:, k, t0:t0 + w], func=AF.Square),
                     reads=[("xT", k)], writes=[("sq", k)])
            for k in range(8):
                p.op("tensor", lambda e, k=k, w=w: e.matmul(pss[:, :w], lhsT=ones_bf, rhs=sq[:, k, :w], start=(k == 0), stop=(k == 7)),
                     reads=[("sq", k), "ones_bf"], writes=[PSK[6]])
            p.op("scalar", lambda e, w=w: e.activation(out=rstd[:, :w], in_=pss[:, :w], func=AF.Sqrt, bias=EPS, scale=1.0 / D),
                 reads=[PSK[6]], writes=["rstd"])
            p.op("vector", lambda e, w=w: e.reciprocal(out=rstd[:, :w], in_=rstd[:, :w]), reads=["rstd"], writes=["rstd"])
            for k in range(8):
                tm = tmp[k % 2]
                p.op("vector", lambda e, k=k, t0=t0, w=w, c=c, tm=tm: e.scalar_tensor_tensor(
                    out=tm[:, :w], in0=xT[:, k, t0:t0 + w], scalar=gcol(k, c), in1=rstd[:, :w], op0=ALU.mult, op1=ALU.mult),
                    reads=[("xT", k), "rstd"] + rkeys, writes=[("ntmp", k % 2)])
                if shcol is None:
                    p.op("scalar", lambda e, k=k, t0=t0, w=w, tm=tm: e.activation(out=hT[:, k, t0:t0 + w], in_=tm[:, :w], func=AF.Identity),
                         reads=[("ntmp", k % 2)], writes=[(hkey, k, gi)])
                else:
                    p.op("scalar", lambda e, k=k, t0=t0, w=w, c=c, tm=tm: e.activation(
                        out=hT[:, k, t0:t0 + w], in_=tm[:, :w], func=AF.Identity, bias=shcol(k, c)),
                        reads=[("ntmp", k % 2)] + rkeys, writes=[(hkey, k, gi)])
        ar.release()

    def ffn_phase(i, hT, hkey, groups, side_layer=None):
        ar.mark()
        side = []
        if side_layer is not None:
            side = mod_side(side_layer, [ar.take([8, 128], BF) for _ in range(4)])
        nblk = 0
        w1q = [ar.take([8, 1024], BF) for _ in range(2)]
        w2q = [ar.take([8, 1024], BF) for _ in range(2)]
        aT = [ar.take([8, 512], BF) for _ in range(2)]
        rt = [ar.take([512], BF) for _ in range(2)]
        cnt = 0
        for q in range(4):
            b = q % 2
            p.dma("gpsimd", w1q[b], ffn_w1[i][:, q * 1024:(q + 1) * 1024].rearrange("(k p) n -> p k n", p=128), writes=[("w1q", b)])
            p.dma("gpsimd", w2q[b], ffn_w2[i][q * 1024:(q + 1) * 1024, :].rearrange("(f p) n -> p f n", p=128), writes=[("w2q", b)])
            for gi, (t0, w, c) in enumerate(groups):
                ab = cnt % 2
                cnt += 1
                a_t = aT[ab]
                for f in range(8):
                    ps = psb[f % 2]
                    for k in range(8):
                        p.op("tensor", lambda e, ps=ps, b=b, k=k, f=f, t0=t0, w=w: e.matmul(
                            ps[:, :w], lhsT=w1q[b][:, k, f * 128:(f + 1) * 128], rhs=hT[:, k, t0:t0 + w],
                            start=(k == 0), stop=(k == 7)), reads=[("w1q", b), (hkey, k, gi)], writes=[PSK[f % 2]])
                    r_t = rt[f % 2]
                    p.op("scalar", lambda e, ps=ps, r_t=r_t, w=w: e.activation(out=r_t[:, :w], in_=ps[:, :w], func=AF.Relu),
                         reads=[PSK[f % 2]], writes=[("rt", f % 2)])
                    p.op("vector", lambda e, r_t=r_t, a_t=a_t, f=f, w=w: e.tensor_tensor(out=a_t[:, f, :w], in0=r_t[:, :w], in1=r_t[:, :w], op=ALU.mult),
                         reads=[("rt", f % 2)], writes=[("aT", ab, f)])
                    nblk += 1
                    if side and nblk % 3 == 0:
                        side.pop(0)()
                for d in range(8):
                    ps2 = psb[2 + d % 2]
                    for f in range(8):
                        p.op("tensor", lambda e, ps2=ps2, b=b, f=f, d=d, a_t=a_t, w=w: e.matmul(
                            ps2[:, :w], lhsT=w2q[b][:, f, d * 128:(d + 1) * 128], rhs=a_t[:, f, :w],
                            start=(f == 0), stop=(f == 7)), reads=[("w2q", b), ("aT", ab, f)], writes=[PSK[2 + d % 2]])
                    p.op("vector", lambda e, ps2=ps2, d=d, t0=t0, w=w, c=c: e.scalar_tensor_tensor(
                        out=xT[:, d, t0:t0 + w], in0=ps2[:, :w], scalar=mcol(i, 5, d, c), in1=xT[:, d, t0:t0 + w],
                        op0=ALU.mult, op1=ALU.add), reads=[PSK[2 + d % 2], ("modT", i), ("xT", d)], writes=[("xT", d)])
        while side:
            side.pop(0)()
        if side_layer is not None:
            mod_finish(side_layer)
        ar.release()

    TWO_PI = 2.0 * math.pi
    MAGIC = 12582912.0
    CW1 = 6.28125
    CW2 = 0.0019350051879882812
    CW3 = TWO_PI - CW1 - CW2
    PI_LO = 3.1415925

    def s5_phase(i):
        j = i // 2
        V = "vector"
        G = "gpsimd"
        A = "scalar"
        ar.mark()
        PCr = ar.take([64, 17], F32)
        PCi = ar.take([64, 17], F32)
        BBb = ar.take([2, 64, 16], BF)
        A1v = ar.take([2, 64], F32)
        A2v = ar.take([2, 64], F32)
        NGH = 64
        VH = ar.take([2, NGH, NCH + 1], BF)
        ar.mark()
        PBr = ar.take([64, 16], F32)
        PBi = ar.take([64, 16], F32)
        ar.mark()
        ar_s = ar.take([64], F32); ai_s = ar.take([64], F32); ldt_s = ar.take([64], F32)
        ex_s = ar.take([35], F32)
        p.dma("sync", ar_s, s5ar[j], writes=["ar_s"])
        p.dma("sync", ai_s, s5ai[j], writes=["ai_s"])
        p.dma("sync", ldt_s, s5ldt[j], writes=["ldt_s"])
        p.dma("sync", ex_s, expo, writes=["ex_s"])
        dt = ar.take([64], F32); dar = ar.take([64], F32); th = ar.take([64], F32)
        p.op(A, lambda e: e.activation(out=dt, in_=ldt_s, func=AF.Exp), reads=["ldt_s"], writes=["dt"])
        p.op(V, lambda e: e.tensor_tensor(out=dar, in0=dt, in1=ar_s, op=ALU.mult), reads=["dt", "ar_s"], writes=["dar"])
        p.op(V, lambda e: e.tensor_tensor(out=th, in0=dt, in1=ai_s, op=ALU.mult), reads=["dt", "ai_s"], writes=["th"])
        GH = 32
        b_s = ar.take([2, GH, 16], F32)
        lm = ar.take([GH, 35], F32); ang = ar.take([GH, 35], F32); kk = ar.take([GH, 35], F32); rc = ar.take([GH, 35], F32)
        den = ar.take([GH], F32); t0_ = ar.take([GH], F32); nr = ar.take([GH], F32); fr = ar.take([GH], F32); fi = ar.take([GH], F32)
        tb1 = ar.take([GH, 16], F32); tb2 = ar.take([GH, 16], F32)
        for g2h in range(2):
            gs_ = slice(GH * g2h, GH * g2h + GH)
            p.dma("sync", b_s, s5b[j][:, :, gs_, :], writes=["b_s"])
            exb = ex_s.unsqueeze(1).broadcast_to([128, GH, 35])
            p.op(V, lambda e, gs_=gs_, exb=exb: e.tensor_tensor(out=lm, in0=dar[:, gs_].unsqueeze(2).broadcast_to([128, GH, 35]), in1=exb, op=ALU.mult),
                 reads=["dar", "ex_s"], writes=["lm"])
            p.op(V, lambda e, gs_=gs_, exb=exb: e.tensor_tensor(out=ang, in0=th[:, gs_].unsqueeze(2).broadcast_to([128, GH, 35]), in1=exb, op=ALU.mult),
                 reads=["th", "ex_s"], writes=["ang"])
            p.op(A, lambda e: e.activation(out=lm, in_=lm, func=AF.Exp), reads=["lm"], writes=["lm"])
            p.op(V, lambda e: e.tensor_scalar(out=kk, in0=ang, scalar1=1.0 / TWO_PI, scalar2=MAGIC, op0=ALU.mult, op1=ALU.add), reads=["ang"], writes=["kk"])
            p.op(V, lambda e: e.tensor_scalar(out=kk, in0=kk, scalar1=-MAGIC, scalar2=None, op0=ALU.add), reads=["kk"], writes=["kk"])
            p.op(V, lambda e: e.scalar_tensor_tensor(out=ang, in0=kk, scalar=-CW1, in1=ang, op0=ALU.mult, op1=ALU.add), reads=["kk", "ang"], writes=["ang"])
            p.op(V, lambda e: e.scalar_tensor_tensor(out=ang, in0=kk, scalar=-CW2, in1=ang, op0=ALU.mult, op1=ALU.add), reads=["kk", "ang"], writes=["ang"])
            p.op(V, lambda e: e.scalar_tensor_tensor(out=ang, in0=kk, scalar=-CW3, in1=ang, op0=ALU.mult, op1=ALU.add), reads=["kk", "ang"], writes=["ang"])
            p.op(V, lambda e: e.tensor_scalar(out=ang, in0=ang, scalar1=PI_LO, scalar2=-PI_LO, op0=ALU.min, op1=ALU.max), reads=["ang"], writes=["ang"])
            p.op(V, lambda e: e.tensor_scalar(out=kk, in0=ang, scalar1=math.pi / 2, scalar2=-TWO_PI, op0=ALU.is_gt, op1=ALU.mult), reads=["ang"], writes=["kk"])
            p.op(V, lambda e: e.scalar_tensor_tensor(out=rc, in0=ang, scalar=math.pi / 2, in1=kk, op0=ALU.add, op1=ALU.add), reads=["ang", "kk"], writes=["rc"])
            p.op(V, lambda e: e.tensor_scalar(out=rc, in0=rc, scalar1=PI_LO, scalar2=-PI_LO, op0=ALU.min, op1=ALU.max), reads=["rc"], writes=["rc"])
            p.op(A, lambda e: e.activation(out=ang, in_=ang, func=AF.Sin), reads=["ang"], writes=["ang"])
            p.op(A, lambda e: e.activation(out=rc, in_=rc, func=AF.Sin), reads=["rc"], writes=["rc"])
            p.op(V, lambda e: e.tensor_tensor(out=rc, in0=lm, in1=rc, op=ALU.mult), reads=["lm", "rc"], writes=["rc"])
            p.op(V, lambda e: e.tensor_tensor(out=ang, in0=lm, in1=ang, op=ALU.mult), reads=["lm", "ang"], writes=["ang"])
            p.op(V, lambda e, gs_=gs_: e.tensor_copy(out=PCr[:, gs_, :], in_=rc[:, :, 0:17]), reads=["rc"], writes=["PCr"])
            p.op(V, lambda e, gs_=gs_: e.tensor_copy(out=PCi[:, gs_, :], in_=ang[:, :, 0:17]), reads=["ang"], writes=["PCi"])
            p.op(V, lambda e, gs_=gs_: e.tensor_copy(out=PBr[:, gs_, :], in_=rc[:, :, 17:33]), reads=["rc"], writes=["PBr"])
            p.op(V, lambda e, gs_=gs_: e.tensor_copy(out=PBi[:, gs_, :], in_=ang[:, :, 17:33]), reads=["ang"], writes=["PBi"])
            a1r = rc[:, :, 33]
            a1i = ang[:, :, 33]
            ars = ar_s[:, gs_]
            ais = ai_s[:, gs_]
            p.op(V, lambda e, ars=ars: e.tensor_tensor(out=den, in0=ars, in1=ars, op=ALU.mult), reads=["ar_s"], writes=["den"])
            p.op(V, lambda e, ais=ais: e.tensor_tensor(out=t0_, in0=ais, in1=ais, op=ALU.mult), reads=["ai_s"], writes=["t0_"])
            p.op(V, lambda e: e.tensor_tensor(out=den, in0=den, in1=t0_, op=ALU.add), reads=["den", "t0_"], writes=["den"])
            p.op(V, lambda e: e.reciprocal(out=den, in_=den), reads=["den"], writes=["den"])
            p.op(V, lambda e, a1r=a1r: e.tensor_scalar(out=nr, in0=a1r, scalar1=-1.0, scalar2=None, op0=ALU.add), reads=["rc"], writes=["nr"])
            p.op(V, lambda e, ars=ars: e.tensor_tensor(out=fr, in0=nr, in1=ars, op=ALU.mult), reads=["nr", "ar_s"], writes=["fr"])
            p.op(V, lambda e, a1i=a1i, ais=ais: e.tensor_tensor(out=t0_, in0=a1i, in1=ais, op=ALU.mult), reads=["ang", "ai_s", "den"], writes=["t0_"])
            p.op(V, lambda e: e.tensor_tensor(out=fr, in0=fr, in1=t0_, op=ALU.add), reads=["fr", "t0_"], writes=["fr"])
            p.op(V, lambda e: e.tensor_tensor(out=fr, in0=fr, in1=den, op=ALU.mult), reads=["fr", "den"], writes=["fr"])
            p.op(V, lambda e, a1i=a1i, ars=ars: e.tensor_tensor(out=fi, in0=a1i, in1=ars, op=ALU.mult), reads=["ang", "ar_s"], writes=["fi"])
            p.op(V, lambda e, ais=ais: e.tensor_tensor(out=t0_, in0=nr, in1=ais, op=ALU.mult), reads=["nr", "ai_s", "fr"], writes=["t0_"])
            p.op(V, lambda e: e.tensor_tensor(out=fi, in0=fi, in1=t0_, op=ALU.subtract), reads=["fi", "t0_"], writes=["fi"])
            p.op(V, lambda e: e.tensor_tensor(out=fi, in0=fi, in1=den, op=ALU.mult), reads=["fi", "den"], writes=["fi"])
            frb = fr.unsqueeze(2).broadcast_to([128, GH, 16])
            fib = fi.unsqueeze(2).broadcast_to([128, GH, 16])
            p.op(V, lambda e, frb=frb: e.tensor_tensor(out=tb1, in0=b_s[:, 0], in1=frb, op=ALU.mult), reads=["b_s", "fr"], writes=["tb1"])
            p.op(V, lambda e, fib=fib: e.tensor_tensor(out=tb2, in0=b_s[:, 1], in1=fib, op=ALU.mult), reads=["b_s", "fi"], writes=["tb2"])
            p.op(V, lambda e, gs_=gs_: e.tensor_tensor(out=BBb[:, 0, gs_, :], in0=tb1, in1=tb2, op=ALU.subtract), reads=["tb1", "tb2"], writes=["BBb"])
            p.op(V, lambda e, frb=frb: e.tensor_tensor(out=tb1, in0=b_s[:, 1], in1=frb, op=ALU.mult), reads=["b_s", "fr", "BBb"], writes=["tb1"])
            p.op(V, lambda e, fib=fib: e.tensor_tensor(out=tb2, in0=b_s[:, 0], in1=fib, op=ALU.mult), reads=["b_s", "fi", "BBb"], writes=["tb2"])
            p.op(V, lambda e, gs_=gs_: e.tensor_tensor(out=BBb[:, 1, gs_, :], in0=tb1, in1=tb2, op=ALU.add), reads=["tb1", "tb2"], writes=["BBb"])
            p.op(V, lambda e, gs_=gs_: e.tensor_copy(out=A1v[:, 0, gs_], in_=rc[:, :, 34]), reads=["rc"], writes=["A1v"])
            p.op(V, lambda e, gs_=gs_: e.tensor_copy(out=A1v[:, 1, gs_], in_=rc[:, :, 34]), reads=["rc", "A1v"], writes=["A1v"])
            p.op(V, lambda e, gs_=gs_: e.tensor_scalar(out=A2v[:, 0, gs_], in0=ang[:, :, 34], scalar1=-1.0, scalar2=None, op0=ALU.mult), reads=["ang"], writes=["A2v"])
            p.op(V, lambda e, gs_=gs_: e.tensor_copy(out=A2v[:, 1, gs_], in_=ang[:, :, 34]), reads=["ang", "A2v"], writes=["A2v"])
        ar.release()

        norm_phase(hT, "hT", lambda k, c: gs1[:, i, k, c:c + 1], lambda k, c: mcol(i, 0, k, c), token_groups(True),
                   [("gs1", i), ("modT", i)])
        hkeys = lambda k: [("hT", k, gi) for gi in range(5)]

        for half in range(1):
            ks = range(8)
            p.op(G, lambda e: e.memset(VH[0:64, :, :, 0:1], 0.0), writes=["VH"])
            p.op(G, lambda e: e.memset(VH[64:128, :, :, NCH:NCH + 1], 0.0), reads=["VH"], writes=["VH"])
            ar.mark()
            ABb = [ar.take([2, 16, 8, 16], BF) for _ in range(2)]
            ABt = ar.take([2, 16, 2, 128], BF)
            vt1 = ar.take([16, 2, 16], F32)
            vt2 = ar.take([16, 2, 16], F32)
            vcnt = 0
            def gen_ab(k):
                AB_ = ABb[k % 2]
                abk = "AB%d" % (k % 2)
                for hh in range(4):
                    gsl = slice(8 * k + 2 * hh, 8 * k + 2 * hh + 2)
                    pbr = PBr[:, gsl, :].rearrange("p g s -> p s g").unsqueeze(3).broadcast_to([128, 16, 2, 16])
                    pbi = PBi[:, gsl, :].rearrange("p g s -> p s g").unsqueeze(3).broadcast_to([128, 16, 2, 16])
                    bbr = BBb[:, 0, gsl, :].unsqueeze(1).broadcast_to([128, 16, 2, 16])
                    bbi = BBb[:, 1, gsl, :].unsqueeze(1).broadcast_to([128, 16, 2, 16])
                    abr = AB_[:, 0, :, 2 * hh:2 * hh + 2, :]
                    abi = AB_[:, 1, :, 2 * hh:2 * hh + 2, :]
                    p.op(G, lambda e, pbr=pbr, bbr=bbr: e.tensor_tensor(out=vt1, in0=pbr, in1=bbr, op=ALU.mult), reads=["PBr", "BBb"], writes=["vt1"])
                    p.op(G, lambda e, pbi=pbi, bbi=bbi: e.tensor_tensor(out=vt2, in0=pbi, in1=bbi, op=ALU.mult), reads=["PBi", "BBb"], writes=["vt2"])
                    p.op(G, lambda e, abr=abr: e.tensor_tensor(out=abr, in0=vt1, in1=vt2, op=ALU.subtract), reads=["vt1", "vt2"], writes=[abk])
                    p.op(G, lambda e, pbr=pbr, bbi=bbi: e.tensor_tensor(out=vt1, in0=pbr, in1=bbi, op=ALU.mult), reads=["PBr", "BBb"], writes=["vt1"])
                    p.op(G, lambda e, pbi=pbi, bbr=bbr: e.tensor_tensor(out=vt2, in0=pbi, in1=bbr, op=ALU.mult), reads=["PBi", "BBb"], writes=["vt2"])
                    p.op(G, lambda e, abi=abi: e.tensor_tensor(out=abi, in0=vt1, in1=vt2, op=ALU.add), reads=["vt1", "vt2"], writes=[abk])

            gen_ab(0)
            for k in ks:
                AB = ABb[k % 2]
                ABK = "AB%d" % (k % 2)
                if k + 1 < 8:
                    gen_ab(k + 1)
                for ri in range(2):
                    for sb_ in range(2):
                        bank = 6 + (2 * ri + sb_) % 2
                        pbf = psb[bank].bitcast(BF)
                        p.pe_fence()
                        for s8 in range(8):
                            s = sb_ * 8 + s8
                            p.op("tensor", lambda e, pbf=pbf, ri=ri, s=s, s8=s8, AB=AB: e.transpose(
                                pbf[:, s8 * 128:(s8 + 1) * 128], AB[:, ri, s, :, :].rearrange("p g h -> p (g h)"), ident_bf),
                                reads=[ABK, "ident_bf"], writes=[PSK[bank]])
                        for g2 in range(2):
                            eng = A if g2 == 0 else V
                            src = pbf[:, 0:1024].rearrange("p (s q) -> p s q", s=8)
                            dst = ABt[:, g2, sb_ * 8:sb_ * 8 + 8, ri, :]
                            if g2 == 0:
                                p.op(A, lambda e, src=src, dst=dst, g2=g2: e.activation(out=dst, in_=src, func=AF.Identity, scale=parm_s[:, g2:g2 + 1]),
                                     reads=[PSK[bank], "parm_s"], writes=["ABt", PSK[bank]])
                            else:
                                p.op(V, lambda e, src=src, dst=dst, g2=g2: e.tensor_scalar(out=dst, in0=src, scalar1=parm_s[:, g2:g2 + 1], scalar2=None, op0=ALU.mult),
                                     reads=[PSK[bank], "parm_s"], writes=["ABt", PSK[bank]])
                for g2 in range(2):
                    for ri in range(2):
                        p.pe_fence()
                        for s in range(16):
                            for q in range(4):
                                p.op("tensor", lambda e, q=q, g2=g2, s=s, ri=ri, k=k: e.matmul(
                                    psb[q][:, 0:NCH], lhsT=ABt[32 * q:32 * q + 32, g2, s, ri, :], rhs=hT[32 * q:32 * q + 32, k, s:T:Q],
                                    start=(s == 0), stop=(s == 15), tile_position=(32 * q, 0)),
                                    reads=["ABt"] + hkeys(k), writes=[PSK[q]])
                        for q in range(4):
                            gl = 2 * q + g2
                            gh = 8 * k + gl
                            ps = psb[q]
                            p.op(A, lambda e, ps=ps, ri=ri, gh=gh: e.activation(out=VH[0:64, ri, gh, 1:NCH + 1], in_=ps[0:64, 0:NCH], func=AF.Identity),
                                 reads=[PSK[q]], writes=[("VH", ri, gh, 0)])
                            p.op(V, lambda e, ps=ps, ri=ri, gh=gh: e.tensor_copy(out=VH[64:128, ri, gh, 0:NCL], in_=ps[64:128, NCC:NCH]),
                                 reads=[PSK[q]], writes=[("VH", ri, gh, 1)])
                            p.op(V, lambda e, ps=ps, ri=ri, gh=gh: e.tensor_copy(out=VH[64:128, ri, gh, NCL:NCH], in_=ps[64:128, 0:NCC]),
                                 reads=[PSK[q]], writes=[("VH", ri, gh, 2)])
            ar.release()
            ar.release()
            vh_all = ["VH"] + [("VH", ri, gh, x) for ri in range(2) for gh in range(NGH) for x in range(3)]
            if opts.get("s5_stop") == 2:
                vdb = ar.take([2, 4, NCH + 1], F32)
                p.op(V, lambda e: e.tensor_copy(out=vdb, in_=VH[:, :, 0:4, :]), reads=vh_all, writes=["vdb"])
                dump(vdb.rearrange("p a g c -> p (a g c)"), 2 * 4 * (NCH + 1), ["vdb"])
                raise StopBuild()
            ar.mark()
            Hc = [ar.take([2, NGH], F32) for _ in range(2)]
            sP1 = ar.take([2, NGH], F32)
            sP2 = ar.take([2, NGH], F32)
            a1h = A1v
            a2h = A2v
            first = True
            SCAN_ENG = {"f": V, "r": V}
            for st in range(NCH - 1):
                cur = Hc[st % 2]
                nxt = Hc[(st + 1) % 2]
                chains = ((0, 64, st + 1, "f"), (64, 128, NCH - 1 - st, "r"))
                rk = vh_all if first else []
                if st == 0:
                    for (lo, hi_, col, tag) in chains:
                        vcol = VH[lo:hi_, :, :, col]
                        p.op(SCAN_ENG[tag], lambda e, nxt=nxt, lo=lo, hi_=hi_, vcol=vcol: e.tensor_copy(out=nxt[lo:hi_], in_=vcol),
                             reads=rk, writes=[("Hc", (st + 1) % 2, tag)])
                    first = False
                    continue
                for stage in range(6):
                    for (lo, hi_, col, tag) in chains:
                        eng = SCAN_ENG[tag]
                        vcol = VH[lo:hi_, :, :, col]
                        if stage == 0:
                            p.op(eng, lambda e, cur=cur, lo=lo, hi_=hi_, a1h=a1h: e.tensor_tensor(out=sP1[lo:hi_], in0=cur[lo:hi_], in1=a1h[lo:hi_], op=ALU.mult),
                                 reads=[("Hc", st % 2, tag), "A1v"], writes=[("sP1", tag)])
                        elif stage == 1:
                            p.op(G, lambda e, cur=cur, lo=lo, hi_=hi_, a2h=a2h: e.tensor_tensor(out=sP2[lo:hi_], in0=cur[lo:hi_, ::-1, :], in1=a2h[lo:hi_], op=ALU.mult),
                                 reads=[("Hc", st % 2, tag), "A2v"], writes=[("sP2", tag, 0), ("sP2", tag, 1)])
                        elif stage == 2:
                            continue
                        elif stage == 3:
                            p.op(eng, lambda e, lo=lo, hi_=hi_: e.tensor_tensor(out=sP1[lo:hi_], in0=sP1[lo:hi_], in1=sP2[lo:hi_], op=ALU.add),
                                 reads=[("sP1", tag), ("sP2", tag, 0), ("sP2", tag, 1)], writes=[("sP1", tag)])
                        elif stage == 4:
                            p.op(eng, lambda e, nxt=nxt, lo=lo, hi_=hi_, vcol=vcol: e.tensor_tensor(out=nxt[lo:hi_], in0=sP1[lo:hi_], in1=vcol, op=ALU.add),
                                 reads=[("sP1", tag)], writes=[("Hc", (st + 1) % 2, tag)])
                        else:
                            p.op(A, lambda e, nxt=nxt, lo=lo, hi_=hi_, vcol=vcol: e.activation(out=vcol, in_=nxt[lo:hi_], func=AF.Identity),
                                 reads=[("Hc", (st + 1) % 2, tag)], writes=[("VHc", tag)])
            ar.release()
            vh_done = [("VHc", "f"), ("VHc", "r")] + vh_all
            if opts.get("s5_stop") == 3:
                vdb = ar.take([2, 4, NCH + 1], F32)
                p.op(V, lambda e: e.tensor_copy(out=vdb, in_=VH[:, :, 0:4, :]), reads=vh_done, writes=["vdb"])
                dump(vdb.rearrange("p a g c -> p (a g c)"), 2 * 4 * (NCH + 1), ["vdb"])
                raise StopBuild()
            ar.mark()
            c_kb = [ar.take([2, 8, 16], F32) for _ in range(2)]
            ncr_kb = [ar.take([8, 16], F32) for _ in range(2)]
            Cqb = [ar.take([2, 8, 17, 16], BF) for _ in range(2)]
            BBp = ar.take([2, 8, 128], BF)
            Kc = ar.take([8, 31, 16], BF)
            Yl = ar.take([16, 8, 16], BF)
            Yc = Yl
            ct1 = ar.take([2, 17, 16], F32)
            ct2 = ar.take([2, 17, 16], F32)
            kt = ar.take([16], F32)
            kt2 = ar.take([16], F32)
            ycar = [ar.take([512], F32)] * 2
            ycnt = 0

            def gen_cq(k):
                kb = k % 2
                c_k = c_kb[kb]
                ncr_k = ncr_kb[kb]
                Cq_ = Cqb[kb]
                ck, nk, qk = "c_k%d" % kb, "ncr_k%d" % kb, "Cq%d" % kb
                p.dma("sync", c_k, s5c[j][:, :, 8 * k:8 * k + 8, :], writes=[ck])
                p.op(G, lambda e: e.tensor_scalar(out=ncr_k, in0=c_k[:, 0], scalar1=-1.0, scalar2=None, op0=ALU.mult), reads=[ck], writes=[nk])
                for hh in range(4):
                    gsl = slice(8 * k + 2 * hh, 8 * k + 2 * hh + 2)
                    lsl = slice(2 * hh, 2 * hh + 2)
                    pcr = PCr[:, gsl, :].unsqueeze(3).broadcast_to([128, 2, 17, 16])
                    pci = PCi[:, gsl, :].unsqueeze(3).broadcast_to([128, 2, 17, 16])
                    cr = c_k[:, 0, lsl, :].unsqueeze(2).broadcast_to([128, 2, 17, 16])
                    ci = c_k[:, 1, lsl, :].unsqueeze(2).broadcast_to([128, 2, 17, 16])
                    ncr = ncr_k[:, lsl, :].unsqueeze(2).broadcast_to([128, 2, 17, 16])
                    p.op(G, lambda e, cr=cr, pcr=pcr: e.tensor_tensor(out=ct1, in0=cr, in1=pcr, op=ALU.mult), reads=[ck, "PCr"], writes=["ct1"])
                    p.op(G, lambda e, ci=ci, pci=pci: e.tensor_tensor(out=ct2, in0=ci, in1=pci, op=ALU.mult), reads=[ck, "PCi"], writes=["ct2"])
                    p.op(G, lambda e, lsl=lsl, Cq_=Cq_: e.tensor_tensor(out=Cq_[:, 0, lsl], in0=ct1, in1=ct2, op=ALU.subtract), reads=["ct1", "ct2"], writes=[qk])
                    p.op(G, lambda e, ncr=ncr, pci=pci: e.tensor_tensor(out=ct1, in0=ncr, in1=pci, op=ALU.mult), reads=[nk, "PCi"], writes=["ct1"])
                    p.op(G, lambda e, ci=ci, pcr=pcr: e.tensor_tensor(out=ct2, in0=ci, in1=pcr, op=ALU.mult), reads=[ck, "PCr"], writes=["ct2"])
                    p.op(G, lambda e, lsl=lsl, Cq_=Cq_: e.tensor_tensor(out=Cq_[:, 1, lsl], in0=ct1, in1=ct2, op=ALU.subtract), reads=["ct1", "ct2"], writes=[qk])

            for k in ks:
                kb = k % 2
                Cq = Cqb[kb]
                CQK = "Cq%d" % kb
                if k == ks[0]:
                    gen_cq(k)
                p.op(G, lambda e: e.memset(BBp, 0.0), writes=["BBp"])
                for gl in range(8):
                    p.op(G, lambda e, gl=gl, k=k: e.tensor_copy(out=BBp[:, :, gl, gl * 16:(gl + 1) * 16], in_=BBb[:, :, 8 * k + gl, :]),
                         reads=["BBb", "BBp"], writes=["BBp"])
                if k + 1 < 8:
                    gen_cq(k + 1)
                if opts.get("s5_tsub") == 1:
                    raise StopBuild()
                for gl in range(8):
                    bf_, br_ = (4, 5) if gl % 2 == 0 else (6, 7)
                    for dh, bank in ((0, bf_), (1, br_)):
                        lo = 64 * dh
                        p.pe_fence()
                        for ri in range(2):
                            p.op("tensor", lambda e, bank=bank, lo=lo, gl=gl, ri=ri, Cq=Cq: e.matmul(
                                psb[bank][:, 0:272], lhsT=BBp[lo:lo + 64, ri, gl, :], rhs=Cq[lo:lo + 64, ri, gl].rearrange("p j h -> p (j h)"),
                                start=(ri == 0), stop=(ri == 1)), reads=["BBp", CQK], writes=[PSK[bank]])
                    pf = psb[bf_]
                    pr_ = psb[br_]
                    p.op(A, lambda e, gl=gl, pr_=pr_: e.activation(out=Kc[:, gl, 0:15, :].rearrange("p l h -> p (l h)"), in_=pr_[:, 16:256], func=AF.Identity),
                         reads=[PSK[br_]], writes=[("Kc", gl, 0), PSK[br_]])
                    p.op(A, lambda e, gl=gl, pf=pf: e.activation(out=Kc[:, gl, 16:31, :].rearrange("p l h -> p (l h)"), in_=pf[:, 16:256], func=AF.Identity),
                         reads=[PSK[bf_]], writes=[("Kc", gl, 1), PSK[bf_]])
                    p.op(V, lambda e, gl=gl, k=k: e.tensor_scalar(out=kt2, in0=dmask_s[:, gl, :], scalar1=s5d_s[:, j, k:k + 1], scalar2=None, op0=ALU.mult),
                         reads=["dmask_s", "s5d_s"], writes=["kt2"])
                    p.op(V, lambda e, pf=pf: e.tensor_tensor(out=kt, in0=pf[:, 0:16], in1=kt2, op=ALU.add), reads=[PSK[bf_], "kt2"], writes=["kt", PSK[bf_]])
                    p.op(V, lambda e, gl=gl, pr_=pr_: e.tensor_tensor(out=Kc[:, gl, 15, :], in0=pr_[:, 256:272], in1=kt, op=ALU.add),
                         reads=["kt", PSK[br_]], writes=[("Kc", gl, 2), PSK[br_]])
                for (rows, tok0, fcol, rcol, is_lat) in ((NCC, 0, 0, NCL + 1, False), (NCL, CL, NCC, 1, True)):
                    hk_ = [("hT", k, gi_) for gi_ in range(1, 5)] if is_lat else [("hT", k, 0)]
                    for gp in range(4):
                        kkeys = [("Kc", gl, x) for gl in (2 * gp, 2 * gp + 1) for x in range(3)]
                        bank = ycnt % 2
                        bankb = 2 + ycnt % 2
                        ycnt += 1
                        py = psb[bank]
                        pyb = psb[bankb]
                        p.pe_fence()
                        for s in range(16):
                            p.op("tensor", lambda e, py=py, rows=rows, tok0=tok0, gp=gp, s=s, k=k: e.matmul(
                                py[0:rows, 0:512], lhsT=hT[:, k, tok0 + s:tok0 + rows * Q:Q],
                                rhs=Kc[:, 2 * gp:2 * gp + 2, 15 - s:31 - s, :].rearrange("p g l h -> p g (l h)"),
                                start=(s == 0), stop=(s == 15)), reads=kkeys + hk_, writes=[PSK[bank]])
                        for g2 in range(2):
                            gl = 2 * gp + g2
                            gh = 8 * k + gl
                            p.pe_fence()
                            for ri in range(2):
                                p.op("tensor", lambda e, pyb=pyb, rows=rows, fcol=fcol, ri=ri, gh=gh, gl=gl, g2=g2, Cq=Cq: e.matmul(
                                    pyb[0:rows, 256 * g2:256 * g2 + 256], lhsT=VH[0:64, ri, gh, fcol:fcol + rows], rhs=Cq[0:64, ri, gl, 1:17, :].rearrange("p j h -> p (j h)"),
                                    start=(ri == 0), stop=False), reads=vh_done + [CQK], writes=[PSK[bankb]])
                            p.pe_fence()
                            for ri in range(2):
                                p.op("tensor", lambda e, pyb=pyb, rows=rows, rcol=rcol, ri=ri, gh=gh, gl=gl, g2=g2, Cq=Cq: e.matmul(
                                    pyb[0:rows, 256 * g2:256 * g2 + 256], lhsT=VH[64:128, ri, gh, rcol:rcol + rows], rhs=Cq[64:128, ri, gl, 0:16, :].rearrange("p j h -> p (j h)"),
                                    start=False, stop=(ri == 1)), reads=vh_done + [CQK], writes=[PSK[bankb]])
                        ycs = ycar[0]
                        p.op(A, lambda e, pyb=pyb, rows=rows, ycs=ycs: e.activation(out=ycs[0:rows, :], in_=pyb[0:rows, 0:512], func=AF.Identity),
                             reads=[PSK[bankb]], writes=["ycar"])
                        p.op(V, lambda e, py=py, rows=rows, gp=gp, ycs=ycs: e.tensor_tensor(
                            out=Yl[0:rows, :, 2 * gp:2 * gp + 2, :], in0=py[0:rows, 0:512].rearrange("p (g t h) -> p t g h", g=2, t=16),
                            in1=ycs[0:rows, :].rearrange("p (g t h) -> p t g h", g=2, t=16), op=ALU.add),
                            reads=[PSK[bank], "ycar"], writes=[("Y", 2 * gp), ("Y", 2 * gp + 1)])
                    ylk = [("Y", gl) for gl in range(8)]
                    if is_lat:
                        for tb in range(2):
                            bank = 6 + tb
                            pbf = psb[bank].bitcast(BF)
                            p.pe_fence()
                            for t8 in range(8):
                                t = tb * 8 + t8
                                p.op("tensor", lambda e, pbf=pbf, t=t, t8=t8: e.transpose(pbf[:, t8 * 128:(t8 + 1) * 128], Yl[:, t, :, :].rearrange("p g h -> p (g h)"), ident_bf),
                                     reads=ylk + ["ident_bf"], writes=[PSK[bank]])
                            dst = hT[:, k, CL:T].rearrange("p (c t) -> p t c", t=Q)[:, tb * 8:tb * 8 + 8, :]
                            p.op(A, lambda e, pbf=pbf, dst=dst: e.activation(out=dst, in_=pbf[:, 0:1024].rearrange("p (t c) -> p t c", t=8), func=AF.Gelu_apprx_tanh),
                                 reads=[PSK[bank]] + hk_, writes=hk_ + [("yT", k, tb)])
                    else:
                        bank = 6
                        pbf = psb[bank].bitcast(BF)
                        p.pe_fence()
                        for t in range(16):
                            p.op("tensor", lambda e, pbf=pbf, t=t: e.transpose(pbf[:, t * 16:(t + 1) * 16], Yl[0:NCC, t, :, :].rearrange("p g h -> p (g h)"), ident_bf[0:NCC, 0:NCC]),
                                 reads=ylk + ["ident_bf"], writes=[PSK[bank]])
                        dst = hT[:, k, 0:CL].rearrange("p (c t) -> p t c", t=Q)
                        p.op(A, lambda e, pbf=pbf, dst=dst: e.activation(out=dst, in_=pbf[:, 0:256].rearrange("p (t c) -> p t c", t=16), func=AF.Gelu_apprx_tanh),
                             reads=[PSK[bank]] + hk_, writes=hk_ + [("yT", k, 2)])
            if opts.get("s5_stop") == 5:
                ydb = ar.take([2, 1024], F32)
                p.op(V, lambda e: e.tensor_copy(out=ydb[:, 0, :], in_=hT[:, 4, 0:1024]), reads=[("yT", 4, x) for x in range(3)], writes=["ydb"])
                p.op(V, lambda e: e.tensor_copy(out=ydb[:, 1, :], in_=hT[:, 7, 1280:2304]), reads=[("yT", 7, x) for x in range(3)] + ["ydb"], writes=["ydb"])
                dump(ydb.rearrange("p a t -> p (a t)"), 2048, ["ydb"])
                raise StopBuild()
            if opts.get("s5_stop") == 4:
                ydb = ar.take([2, 1024], F32)
                p.op(V, lambda e: e.tensor_copy(out=ydb[:, 0, :], in_=hT[:, 0, 0:1024]), reads=[("yT", 0, x) for x in range(3)], writes=["ydb"])
                p.op(V, lambda e: e.tensor_copy(out=ydb[:, 1, :], in_=hT[:, 3, 1280:2304]), reads=[("yT", 3, x) for x in range(3)] + ["ydb"], writes=["ydb"])
                dump(ydb.rearrange("p a t -> p (a t)"), 2048, ["ydb"])
                raise StopBuild()
            ar.release()
        ar.release()
        if opts.get("s5_stop") == 6:
            ydb = ar.take([2304], F32)
            for k in range(8):
                p.op(V, lambda e, k=k: e.tensor_copy(out=ydb, in_=hT[:, k, :]), reads=[("yT", k, x) for x in range(3)] + hkeys(k), writes=["ydb"])
                dump(ydb, 2304, ["ydb"])
            raise StopBuild()
        ar.mark()
        wg = ar.take([8, 2048], BF)
        sg = [ar.take([512], F32) for _ in range(2)]
        gt = [ar.take([512], F32) for _ in range(2)]
        p.dma("gpsimd", wg, s5_w_glu[j].rearrange("(k p) n -> p k n", p=128), writes=["wg"])
        ykeys = lambda k: [("yT", k, x) for x in range(3)] + hkeys(k)
        for gi, (t0, w, c) in enumerate(token_groups(True)):
            for d in range(8):
                pa = psb[(2 * d) % 4]
                pb_ = psb[(2 * d + 1) % 4]
                for k in range(8):
                    p.op("tensor", lambda e, pa=pa, k=k, d=d, t0=t0, w=w: e.matmul(pa[:, :w], lhsT=wg[:, k, d * 128:(d + 1) * 128], rhs=hT[:, k, t0:t0 + w],
                                                                               start=(k == 0), stop=(k == 7)), reads=["wg"] + ykeys(k), writes=[PSK[(2 * d) % 4]])
                for k in range(8):
                    p.op("tensor", lambda e, pb_=pb_, k=k, d=d, t0=t0, w=w: e.matmul(pb_[:, :w], lhsT=wg[:, k, 1024 + d * 128:1024 + (d + 1) * 128], rhs=hT[:, k, t0:t0 + w],
                                                                                start=(k == 0), stop=(k == 7)), reads=["wg"] + ykeys(k), writes=[PSK[(2 * d + 1) % 4]])
                s_t = sg[d % 2]
                g_t = gt[d % 2]
                p.op(A, lambda e, pb_=pb_, s_t=s_t, w=w: e.activation(out=s_t[:, :w], in_=pb_[:, :w], func=AF.Sigmoid),
                     reads=[PSK[(2 * d + 1) % 4]], writes=[("sg", d % 2)])
                p.op(V, lambda e, pa=pa, s_t=s_t, g_t=g_t, w=w: e.tensor_tensor(out=g_t[:, :w], in0=pa[:, :w], in1=s_t[:, :w], op=ALU.mult),
                     reads=[PSK[(2 * d) % 4], ("sg", d % 2)], writes=[("gt", d % 2)])
                p.op(V, lambda e, g_t=g_t, d=d, t0=t0, w=w, c=c: e.scalar_tensor_tensor(
                    out=xT[:, d, t0:t0 + w], in0=g_t[:, :w], scalar=mcol(i, 2, d, c), in1=xT[:, d, t0:t0 + w], op0=ALU.mult, op1=ALU.add),
                    reads=[("gt", d % 2), ("modT", i), ("xT", d)], writes=[("xT", d)])
        ar.release()

    def attn_phase(i):
        j = i // 2
        last = i == DEPTH - 1
        V = "vector"
        G = "gpsimd"
        A = "scalar"
        groups = token_groups(True)
        norm_phase(hT, "hT", lambda k, c: gs1[:, i, k, c:c + 1], lambda k, c: mcol(i, 0, k, c), groups,
                   [("gs1", i), ("modT", i)])
        hk = lambda k: [("hT", k, gi) for gi in range(5)]
        hall = [x for k in range(8) for x in hk(k)]
        ar.mark()
        qT = ar.take([8, T], BF)
        kT = ar.take([4, T], BF)
        vS = ar.take([18, 4, 65], BF)
        p.op(G, lambda e: e.memset(kT, 0.0), writes=["kTz"])
        ar.mark()
        cs_c = ar.take([L], BF)
        cs_s = ar.take([L], BF)
        bd = ar.take([128], BF)
        prm = ar.take([128], BF)
        gq = ar.take([2], F32)
        sqb = [ar.take([512], BF) for _ in range(2)]
        rsb = [ar.take([512], F32) for _ in range(2)]
        qnb = [ar.take([512], BF) for _ in range(2)]
        t1b = [ar.take([512], BF) for _ in range(2)]
        t2b = [ar.take([512], BF) for _ in range(2)]
        p.dma("gpsimd", cs_c, rope_c, writes=["cs_c"])
        p.dma("gpsimd", cs_s, rope_s, writes=["cs_s"])
        p.dma("gpsimd", bd, bd_in, writes=["bd"])
        p.dma("gpsimd", prm, prot_in, writes=["prm"])
        p.dma("sync", gq, qkg[:, j, :], writes=["gq"])
        p.op(G, lambda e: e.memset(vS[:, :, :, 64:65], 1.0), writes=["vS1"])
        itc = [0]

        def qk_chunk(wt, wkey, col0, dst, dkey, gidx, dst2=None):
            for gi, (t0, w, c) in enumerate(groups):
                it = itc[0] % 2
                itc[0] += 1
                sq, rs, qn, t1, t2 = sqb[it], rsb[it], qnb[it], t1b[it], t2b[it]
                ksq, krs, kqn, kt1, kt2 = "asq%d" % it, "ars%d" % it, "aqn%d" % it, "at1%d" % it, "at2%d" % it
                bq, bss, brp = it, 2 + 4 * it, 3 + 4 * it
                pq, pss_, prp = psb[bq], psb[bss], psb[brp]
                for k in range(8):
                    p.op("tensor", lambda e, pq=pq, k=k, t0=t0, w=w: e.matmul(pq[:, :w], lhsT=wt[:, k, col0:col0 + 128], rhs=hT[:, k, t0:t0 + w],
                                                                          start=(k == 0), stop=(k == 7)), reads=[wkey, ("hT", k, gi)], writes=[PSK[bq]])
                p.op(A, lambda e, pq=pq, w=w, sq=sq: e.activation(out=sq[:, :w], in_=pq[:, :w], func=AF.Square), reads=[PSK[bq]], writes=[ksq, PSK[bq]])
                p.op("tensor", lambda e, w=w, sq=sq, pss_=pss_: e.matmul(pss_[:, :w], lhsT=bd, rhs=sq[:, :w], start=True, stop=True), reads=["bd", ksq], writes=[PSK[bss]])
                p.op(A, lambda e, w=w, rs=rs, pss_=pss_: e.activation(out=rs[:, :w], in_=pss_[:, :w], func=AF.Sqrt, bias=EPS, scale=1.0), reads=[PSK[bss]], writes=[krs])
                p.op(V, lambda e, w=w, rs=rs: e.reciprocal(out=rs[:, :w], in_=rs[:, :w]), reads=[krs], writes=[krs])
                if c == 1 and dst2 is None:
                    p.op(V, lambda e, pq=pq, t0=t0, w=w, rs=rs: e.scalar_tensor_tensor(out=dst[:, t0:t0 + w], in0=pq[:, :w], scalar=gq[:, gidx:gidx + 1], in1=rs[:, :w],
                                                                                    op0=ALU.mult, op1=ALU.mult), reads=[PSK[bq], krs, "gq"], writes=[(dkey, gi), PSK[bq]])
                elif c == 1:
                    p.op(V, lambda e, pq=pq, t0=t0, w=w, rs=rs: e.scalar_tensor_tensor(out=dst[0:64, t0:t0 + w], in0=pq[0:64, :w], scalar=gq[0:64, gidx:gidx + 1], in1=rs[0:64, :w],
                                                                                    op0=ALU.mult, op1=ALU.mult), reads=[PSK[bq], krs, "gq", "kTz"], writes=[(dkey, gi), PSK[bq]])
                    p.op(V, lambda e, pq=pq, t0=t0, w=w, rs=rs: e.scalar_tensor_tensor(out=dst2[64:128, t0:t0 + w], in0=pq[64:128, :w], scalar=gq[64:128, gidx:gidx + 1], in1=rs[64:128, :w],
                                                                                    op0=ALU.mult, op1=ALU.mult), reads=[PSK[bq], krs, "gq", "kTz"], writes=[(dkey, gi, 1), PSK[bq]])
                else:
                    l0 = t0 - CL
                    p.op(V, lambda e, pq=pq, w=w, rs=rs, qn=qn: e.scalar_tensor_tensor(out=qn[:, :w], in0=pq[:, :w], scalar=gq[:, gidx:gidx + 1], in1=rs[:, :w],
                                                                                    op0=ALU.mult, op1=ALU.mult), reads=[PSK[bq], krs, "gq"], writes=[kqn, PSK[bq]])
                    p.op("tensor", lambda e, w=w, qn=qn, prp=prp: e.matmul(prp[:, :w], lhsT=prm, rhs=qn[:, :w], start=True, stop=True), reads=["prm", kqn], writes=[PSK[brp]])
                    p.op(G, lambda e, w=w, l0=l0, qn=qn, t1=t1: e.tensor_tensor(out=t1[:, :w], in0=qn[:, :w], in1=cs_c[:, l0:l0 + w], op=ALU.mult), reads=[kqn, "cs_c"], writes=[kt1])
                    p.op(V, lambda e, w=w, l0=l0, prp=prp, t2=t2: e.tensor_tensor(out=t2[:, :w], in0=prp[:, :w], in1=cs_s[:, l0:l0 + w], op=ALU.mult), reads=[PSK[brp], "cs_s"], writes=[kt2])
                    if dst2 is None:
                        p.op(G, lambda e, t0=t0, w=w, t1=t1, t2=t2: e.tensor_tensor(out=dst[:, t0:t0 + w], in0=t1[:, :w], in1=t2[:, :w], op=ALU.add), reads=[kt1, kt2], writes=[(dkey, gi)])
                    else:
                        p.op(V, lambda e, t0=t0, w=w, t1=t1, t2=t2: e.tensor_tensor(out=dst[0:64, t0:t0 + w], in0=t1[0:64, :w], in1=t2[0:64, :w], op=ALU.add),
                             reads=[kt1, kt2, "kTz"], writes=[(dkey, gi)])
                        p.op(G, lambda e, t0=t0, w=w, t1=t1, t2=t2: e.tensor_tensor(out=dst2[64:128, t0:t0 + w], in0=t1[64:128, :w], in1=t2[64:128, :w], op=ALU.add),
                             reads=[kt1, kt2, "kTz"], writes=[(dkey, gi, 1)])

        ar.mark()
        wkv = ar.take([8, 512], BF)
        p.dma("gpsimd", wkv, attn_w_qkv[j][:, 1024:1536].rearrange("(k p) n -> p k n", p=128), writes=["wkv"])
        for n in range(2):
            qk_chunk(wkv, "wkv", n * 128, kT[:, 2 * n, :], ("kT", n), 1, dst2=kT[:, 2 * n + 1, :])
        for tt in range(18):
            pv = psb[4 + tt % 2]
            for k in range(8):
                p.op("tensor", lambda e, pv=pv, k=k, tt=tt: e.matmul(pv[:, 0:256], lhsT=hT[:, k, tt * 128:(tt + 1) * 128], rhs=wkv[:, k, 256:512],
                                                                  start=(k == 0), stop=(k == 7)), reads=["wkv"] + hk(k), writes=[PSK[4 + tt % 2]])
            p.op(A, lambda e, pv=pv, tt=tt: e.activation(out=vS[:, tt, :, 0:64], in_=pv[:, 0:256].rearrange("p (h d) -> p h d", h=4), func=AF.Identity),
                 reads=[PSK[4 + tt % 2]], writes=[("vS", tt)])
        ar.release()
        ar.mark()
        wq = ar.take([8, 512], BF)
        for qh in range(2):
            p.dma("gpsimd", wq, attn_w_qkv[j][:, qh * 512:(qh + 1) * 512].rearrange("(k p) n -> p k n", p=128), writes=["wq"])
            for n4 in range(4):
                n = qh * 4 + n4
                qk_chunk(wq, "wq", n4 * 128, qT[:, n, :], ("qT", n), 0)
        ar.release()
        ar.release()
        ar.mark()
        wo = hT.rearrange("p k t -> p (k t)")[:, 0:16 * 1024].rearrange("p (h n) -> p h n", h=16)
        oTg = ar.take([16, 512], BF)
        pT = [ar.take([512], BF) for _ in range(4)]
        bcs = ar.take([512], F32)
        rec = ar.take([512], F32)
        ones_f = ar.take([64], F32)
        p.dma("gpsimd", wo[0:64], attn_w_o[j].rearrange("(h d) n -> d h n", d=64), writes=["wo"] + hall)
        p.op(G, lambda e: e.memset(wo[64:128], 0.0), writes=["wo2"] + hall)
        p.op(V, lambda e: e.memset(oTg[64:128], 0.0), writes=["oTg2"])
        p.op(V, lambda e: e.memset(ones_f, 1.0), writes=["ones_f"])
        vkeys = [("vS", tt) for tt in range(18)] + ["vS1"]
        SCALE = HD ** -0.5
        qgroups = [(gi, t0, w, c) for gi, (t0, w, c) in enumerate(groups) if not (c == 1 and last)]
        scnt = 0
        for (gi, t0, w, c) in qgroups:
            ktiles = range(2) if c == 1 else range(18)
            nkt = len(ktiles)
            pending = []
            for h in range(16):
                kv = h // 4
                half = kv % 2
                lo = 64 * half
                perm_pos = QPERM.index(h)
                qn_, qhalf = perm_pos // 2, perm_pos % 2
                assert qhalf == half
                po = psb[4 + h % 2]
                kts = list(ktiles)

                def score(ti, ps_, bs, lo=lo, kv=kv, qn_=qn_, t0=t0, w=w, gi=gi):
                    kt = kts[ti]
                    p.op("tensor", lambda e: e.matmul(
                        ps_[:, :w], lhsT=kT[:, kv, kt * 128:(kt + 1) * 128], rhs=qT[:, qn_, t0:t0 + w], start=True, stop=True),
                        reads=[(("kT", kv // 2), g_) for g_ in range(5)] + [(("kT", kv // 2), g_, 1) for g_ in range(5)] + ["kTz", (("qT", qn_), gi)], writes=[PSK[bs]])

                LOOK = 3
                slots = []
                for ti in range(min(LOOK, nkt)):
                    bs = scnt % 4
                    scnt += 1
                    slots.append(bs)
                    score(ti, psb[bs], bs)
                for ti in range(nkt):
                    bs = slots[ti]
                    ps_ = psb[bs]
                    pt_ = pT[bs]
                    kt = kts[ti]
                    p.op(A, lambda e, ps_=ps_, pt_=pt_, w=w: e.activation(out=pt_[:, :w], in_=ps_[:, :w], func=AF.Exp, scale=SCALE),
                         reads=[PSK[bs]], writes=[("pT", bs)])
                    if ti == min(8, nkt - 1) and pending:
                        pending.pop(0)()
                    if ti + LOOK < nkt:
                        nb_ = scnt % 4
                        scnt += 1
                        slots.append(nb_)
                        score(ti + LOOK, psb[nb_], nb_)
                    p.op("tensor", lambda e, po=po, pt_=pt_, kt=kt, kv=kv, w=w, ti=ti, nkt=nkt: e.matmul(
                        po[0:65, :w], lhsT=vS[:, kt, kv, :], rhs=pt_[:, :w], start=(ti == 0), stop=(ti == nkt - 1)),
                        reads=vkeys + [("pT", bs)], writes=[PSK[4 + h % 2]])
                def norm_head(po=po, h=h, w=w):
                    p.op(V, lambda e: e.reciprocal(out=rec[64:65, :w], in_=po[64:65, :w]), reads=[PSK[4 + h % 2]], writes=["rec", PSK[4 + h % 2]])
                    p.op("tensor", lambda e: e.matmul(psb[6][0:64, :w], lhsT=ones_f[64:65, 0:64], rhs=rec[64:65, :w], start=True, stop=True),
                         reads=["rec", "ones_f"], writes=[PSK[6]])
                    p.op(V, lambda e: e.tensor_copy(out=bcs[0:64, :w], in_=psb[6][0:64, :w]), reads=[PSK[6]], writes=["bcs", PSK[6]])
                    p.op(V, lambda e: e.tensor_tensor(out=oTg[0:64, h, :w], in0=po[0:64, :w], in1=bcs[0:64, :w], op=ALU.mult),
                         reads=[PSK[4 + h % 2], "bcs"], writes=[("oTg", h), PSK[4 + h % 2]])
                pending.append(norm_head)
            while pending:
                pending.pop(0)()
            for n in range(8):
                pw = psb[7] if n % 2 == 0 else psb[6]
                pwk = PSK[7] if n % 2 == 0 else PSK[6]
                for h in range(16):
                    p.op("tensor", lambda e, pw=pw, h=h, n=n, w=w: e.matmul(pw[:, :w], lhsT=wo[:, h, n * 128:(n + 1) * 128], rhs=oTg[:, h, :w],
                                                                       start=(h == 0), stop=(h == 15)), reads=["wo", "wo2", "oTg2", ("oTg", h)], writes=[pwk])
                p.op(V, lambda e, pw=pw, n=n, t0=t0, w=w, c=c: e.scalar_tensor_tensor(
                    out=xT[:, n, t0:t0 + w], in0=pw[:, :w], scalar=mcol(i, 2, n, c), in1=xT[:, n, t0:t0 + w], op0=ALU.mult, op1=ALU.add),
                    reads=[pwk, ("modT", i), ("xT", n)], writes=[("xT", n)])
        ar.release()
        ar.release()

    hT = ar.take([8, T], BF)
    for i in range(DEPTH) if not opts.get("s5_stop") else []:
        last = i == DEPTH - 1
        if i == 0 or not opts.get("mod_side", True):
            mod_phase(i)
        if opts.get("only_mod"):
            dump(modT[:, 0].rearrange("p a b -> p (a b)"), 96, [("modT", 0)])
            break
        if mixers and i % 2 == 1 and opts.get("attn", True):
            attn_phase(i)
            if opts.get("stop_after") == (i, "mix"):
                break
        if mixers and i % 2 == 0 and opts.get("s5", True):
            s5_phase(i)
            if opts.get("stop_after") == (i, "mix"):
                break
        groups = token_groups(include_ctx=not last)
        norm_phase(hT, "hT", lambda k, c, i=i: gs2[:, i, k, c:c + 1], lambda k, c, i=i: mcol(i, 3, k, c), groups,
                   [("gs2", i), ("modT", i)])
        ffn_phase(i, hT, "hT", groups, side_layer=(i + 1 if (not last and opts.get("mod_side", True)) else None))

    if opts.get("s5_stop"):
        try:
            mod_phase(0)
            s5_phase(0)
        except StopBuild:
            pass
        p.emit()
        return nc, p
    if opts.get("only_mod"):
        p.emit()
        return nc, p
    if opts.get("stop_after"):
        for k in range(8):
            p.dma("sync", outT[k * 128:(k + 1) * 128, :], xT[:, k, CL:T], reads=[("xT", k)])
            p.dma("sync", dbg[:, k * 256:(k + 1) * 256], xT[:, k, 0:CL], reads=[("xT", k)])
        p.emit()
        return nc, p
    ar.mark()
    sq = ar.take([8, 512], BF)
    rstd = ar.take([512], F32)
    ost = [ar.take([8, 512], F32) for _ in range(2)]
    pss = psb[6]
    for gi, (t0, w, c) in enumerate(token_groups(include_ctx=False)):
        o_t = ost[gi % 2]
        for k in range(8):
            p.op("scalar", lambda e, k=k, t0=t0, w=w: e.activation(out=sq[:, k, :w], in_=xT[:, k, t0:t0 + w], func=AF.Square),
                 reads=[("xT", k)], writes=[("fsq", k)])
        for k in range(8):
            p.op("tensor", lambda e, k=k, w=w: e.matmul(pss[:, :w], lhsT=ones_bf, rhs=sq[:, k, :w], start=(k == 0), stop=(k == 7)),
                 reads=[("fsq", k), "ones_bf"], writes=[PSK[6]])
        p.op("scalar", lambda e, w=w: e.activation(out=rstd[:, :w], in_=pss[:, :w], func=AF.Sqrt, bias=EPS, scale=1.0 / D),
             reads=[PSK[6]], writes=["frstd"])
        p.op("vector", lambda e, w=w: e.reciprocal(out=rstd[:, :w], in_=rstd[:, :w]), reads=["frstd"], writes=["frstd"])
        for k in range(8):
            p.op("vector", lambda e, k=k, t0=t0, w=w, o_t=o_t: e.scalar_tensor_tensor(
                out=o_t[:, k, :w], in0=xT[:, k, t0:t0 + w], scalar=gfin_s[:, k:k + 1], in1=rstd[:, :w], op0=ALU.mult, op1=ALU.mult),
                reads=[("xT", k), "frstd", "gfin_s"], writes=[("ost", gi % 2, k)])
            p.dma("sync", outT[k * 128:(k + 1) * 128, t0 - CL:t0 - CL + w], o_t[:, k, :w], reads=[("ost", gi % 2, k)])
    ar.release()
    p.emit()
    return nc, p


def _cols(v):
    return np.ascontiguousarray(np.asarray(v, np.float32).reshape(-1, 128).T)


def make_in_maps(inputs):
    x = np.asarray(inputs["x"], np.float32)
    ctx = np.asarray(inputs["ctx"], np.float32)
    c = np.asarray(inputs["c"], np.float32)
    c_ctx = np.asarray(inputs["c_ctx"], np.float32)
    shared = {
        "adab": np.ascontiguousarray(np.stack([_cols(inputs["ada_b"][i]) for i in range(DEPTH)], axis=1)),
        "gmix": np.ascontiguousarray(np.stack([_cols(inputs["norm_mix_g"][i]) for i in range(DEPTH)], axis=1)),
        "gffn": np.ascontiguousarray(np.stack([_cols(inputs["norm_ffn_g"][i]) for i in range(DEPTH)], axis=1)),
        "gfin": _cols(inputs["final_g"]),
        "ada_w": np.ascontiguousarray(inputs["ada_w"], np.float32),
        "ffn_w1": np.ascontiguousarray(inputs["ffn_w1"], np.float32),
        "ffn_w2": np.ascontiguousarray(inputs["ffn_w2"], np.float32),
        "ident": np.eye(128, dtype=np.float32),
    }
    f32 = lambda a: np.ascontiguousarray(a, dtype=np.float32)
    a_re = np.asarray(inputs["s5_a_re"]); a_im = np.asarray(inputs["s5_a_im"]); ldt = np.asarray(inputs["s5_log_dt"])
    shared["s5ar"] = f32(a_re.transpose(0, 1, 3, 2).reshape(2, 128, 64))
    shared["s5ai"] = f32(a_im.transpose(0, 1, 3, 2).reshape(2, 128, 64))
    shared["s5ldt"] = f32(np.broadcast_to(ldt[:, :, None, :], (2, 2, 64, 64)).reshape(2, 128, 64))
    bre = np.asarray(inputs["s5_b_re"]).transpose(0, 1, 3, 2, 4).reshape(2, 128, 64, 16)
    bim = np.asarray(inputs["s5_b_im"]).transpose(0, 1, 3, 2, 4).reshape(2, 128, 64, 16)
    shared["s5b"] = f32(np.stack([bre, bim], axis=2))
    cre = np.asarray(inputs["s5_c_re"]).transpose(0, 1, 4, 2, 3).reshape(2, 128, 64, 16)
    cim = np.asarray(inputs["s5_c_im"]).transpose(0, 1, 4, 2, 3).reshape(2, 128, 64, 16)
    shared["s5c"] = f32(np.stack([cre, cim], axis=2))
    shared["s5d"] = f32(np.stack([_cols(inputs["s5_d"][j]) for j in range(2)], axis=1))
    ex = np.zeros((128, 35), np.float32)
    ex[:64, 0:17] = np.arange(17); ex[64:, 0:17] = 16 - np.arange(17)
    ex[:64, 17:33] = 15 - np.arange(16); ex[64:, 17:33] = np.arange(16)
    ex[:, 33] = 1.0; ex[:, 34] = 16.0
    shared["expo"] = ex
    gl = np.arange(128) // 16
    hi = np.arange(128) % 16
    parm = np.stack([(gl % 2 == 0), (gl % 2 == 1)], axis=1).astype(np.float32)
    shared["parm"] = f32(parm)
    shared["dmask"] = f32((gl[:, None, None] == np.arange(8)[None, :, None]) * (hi[:, None, None] == np.arange(16)[None, None, :]))
    shared["s5_w_glu"] = f32(inputs["s5_w_glu"])
    wqkv = np.asarray(inputs["attn_w_qkv"], np.float32)
    qcols = np.concatenate([np.arange(h * 64, (h + 1) * 64) for h in QPERM])
    shared["attn_w_qkv"] = f32(np.concatenate([wqkv[:, :, qcols], wqkv[:, :, 1024:]], axis=2))
    shared["attn_w_o"] = f32(inputs["attn_w_o"])
    tpos = np.arange(L)
    rowp = (tpos // 64).astype(np.float64); colp = (tpos % 64).astype(np.float64)
    invf = 10000.0 ** (-np.arange(16, dtype=np.float64) / 16)
    ang = np.zeros((64, L))
    ang[0:16] = invf[:, None] * rowp[None, :]; ang[16:32] = ang[0:16]
    ang[32:48] = invf[:, None] * colp[None, :]; ang[48:64] = ang[32:48]
    shared["rope_c"] = f32(np.tile(np.cos(ang), (2, 1)))
    shared["rope_s"] = f32(np.tile(np.sin(ang), (2, 1)))
    bdm = np.zeros((128, 128), np.float32)
    bdm[:64, :64] = 1.0 / 64; bdm[64:, 64:] = 1.0 / 64
    shared["bd"] = bdm
    pr = np.zeros((128, 128), np.float32)
    for base in (0, 32, 64, 96):
        for d_ in range(16):
            pr[base + d_ + 16, base + d_] = -1.0
            pr[base + d_, base + d_ + 16] = 1.0
    shared["prot"] = pr
    qg = np.asarray(inputs["attn_q_g"], np.float32); kg = np.asarray(inputs["attn_k_g"], np.float32)
    shared["qkg"] = f32(np.stack([np.tile(qg, (1, 2)).T, np.tile(kg, (1, 2)).T], axis=2))
    maps = []
    for b in range(8):
        m = dict(shared)
        m["xin"] = np.ascontiguousarray(np.concatenate([ctx[b], x[b]], axis=0).T)
        m["ccols"] = np.ascontiguousarray(np.stack([_cols(c[b]), _cols(c_ctx)], axis=2))
        maps.append(m)
    return maps


def kernel(**inputs):
    nc, _ = build()
    maps = make_in_maps(inputs)
    res = run_bass_kernel_spmd(nc, maps, core_ids=list(range(8)))
    out = np.stack([np.ascontiguousarray(res.results[b]["outT"].T) for b in range(8)], axis=0)
    return out.astype(np.float32)
```

```python
import math
from contextlib import ExitStack
import numpy as np
import concourse.bass as bass
import concourse.mybir as mybir
from concourse.bass_utils import run_bass_kernel_spmd

F32 = mybir.dt.float32
BF = mybir.dt.bfloat16
AF = mybir.ActivationFunctionType
ALU = mybir.AluOpType

D = 1024
DEPTH = 4
L = 2048
CL = 256
T = L + CL
NG = 64
NP = 64
NH = 16
Q = 16
NCH = T // Q
NCC = CL // Q
NCL = L // Q
HD = 64
NHEAD = 16
NKV = 4
DFF = 4096
EPS = 1e-6

QPERM = [0, 4, 1, 5, 2, 6, 3, 7, 8, 12, 9, 13, 10, 14, 11, 15]
COMPUTE = ("tensor", "vector", "scalar", "gpsimd")
NSLOT = 8


class Op:
    __slots__ = ("eng", "fn", "reads", "writes", "dma", "idx", "deps", "waits",
                 "needs_inc", "ctr", "slot", "slot_use", "gidx", "force")

    def __init__(self, eng, fn, reads, writes, dma):
        self.eng, self.fn, self.reads, self.writes, self.dma = eng, fn, reads, writes, dma
        self.deps = []
        self.waits = []
        self.needs_inc = False
        self.ctr = 0
        self.slot = None
        self.slot_use = 0


class Prog:
    def __init__(self, nc, strict=True):
        self.nc = nc
        self.strict = strict
        self.ops = []
        self.stack = ExitStack()
        self.last_w = {}
        self.readers = {}
        self.bar_tile = None
        self.last_bar = None
        self.bar_from = 0
        self.fence = False
        self.last_pe = None

    def sb(self, name, shape, dtype):
        return self.stack.enter_context(self.nc.sbuf_tensor(name, list(shape), dtype))

    def ps(self, name, shape, dtype):
        return self.stack.enter_context(self.nc.psum_tensor(name, list(shape), dtype))

    def op(self, eng, fn, reads=(), writes=(), dma=False):
        o = Op(eng, fn, tuple(reads), tuple(writes), dma)
        o.gidx = len(self.ops)
        deps = set()
        for k in o.reads:
            w = self.last_w.get(k)
            if w is not None:
                deps.add(w)
        for k in o.writes:
            w = self.last_w.get(k)
            if w is not None:
                deps.add(w)
            for r in self.readers.get(k, ()):
                deps.add(r)
        deps.discard(o.gidx)
        if self.last_bar is not None:
            deps.add(self.last_bar)
        o.force = None
        if eng == "tensor":
            if self.fence and self.last_pe is not None:
                deps.add(self.last_pe)
                o.force = self.last_pe
            self.fence = False
            self.last_pe = o.gidx
        o.deps = sorted(deps)
        for k in o.reads:
            self.readers.setdefault(k, []).append(o.gidx)
        for k in o.writes:
            self.last_w[k] = o.gidx
            self.readers[k] = []
        self.ops.append(o)
        return o

    def pe_fence(self):
        self.fence = True

    def barrier(self):
        if self.bar_tile is None:
            self.bar_tile = self.sb("bar_tile", [128, 8], F32)
        o = Op("vector", lambda e: e.memset(self.bar_tile[:, 0:1], 0.0), (), (), False)
        o.force = None
        o.gidx = len(self.ops)
        lastc = {}
        deps = set()
        for q in self.ops[self.bar_from:]:
            if q.dma:
                deps.add(q.gidx)
            else:
                lastc[q.eng] = q.gidx
        deps.update(lastc.values())
        if self.last_bar is not None:
            deps.add(self.last_bar)
        o.deps = sorted(deps)
        self.ops.append(o)
        self.last_bar = o.gidx
        self.bar_from = len(self.ops)
        return o

    def dma(self, eng, out, in_, reads=(), writes=(), **kw):
        return self.op(eng, lambda e: e.dma_start(out=out, in_=in_, **kw), reads, writes, dma=True)

    def emit(self):
        nc = self.nc
        ops = self.ops
        engs = ("tensor", "vector", "scalar", "gpsimd", "sync")
        per = {e: [] for e in engs}
        for o in ops:
            o.idx = len(per[o.eng])
            per[o.eng].append(o)
        dcount = {e: 0 for e in engs}
        for o in ops:
            if o.dma:
                o.slot = dcount[o.eng] % NSLOT
                o.slot_use = dcount[o.eng] // NSLOT + 1
                dcount[o.eng] += 1
        known = {e: {f: -1 for f in COMPUTE} for e in engs}
        kdma = {e: {} for e in engs}
        snap = [None] * len(ops)
        for o in ops:
            E = o.eng
            kn = known[E]
            for d in reversed(o.deps):
                pr = ops[d]
                if pr.dma:
                    key = (pr.eng, pr.slot)
                    if kdma[E].get(key, 0) >= pr.slot_use:
                        continue
                    kdma[E][key] = pr.slot_use
                    o.waits.append(("dma", pr.eng, pr.slot, pr.slot_use))
                    psn = snap[d]
                    for f in COMPUTE:
                        if psn[f] > kn[f]:
                            kn[f] = psn[f]
                else:
                    Fe = pr.eng
                    if Fe == E and not o.dma and (Fe == "tensor" or not self.strict) and getattr(o, "force", None) != d:
                        continue
                    if kn[Fe] >= pr.idx:
                        continue
                    o.waits.append(("eng", d))
                    pr.needs_inc = True
                    psn = snap[d]
                    for f in COMPUTE:
                        if psn[f] > kn[f]:
                            kn[f] = psn[f]
                    if pr.idx > kn[Fe]:
                        kn[Fe] = pr.idx
            if o.dma and o.slot_use > 1:
                key = (o.eng, o.slot)
                if kdma[E].get(key, 0) < o.slot_use - 1:
                    kdma[E][key] = o.slot_use - 1
                    o.waits.append(("dma", o.eng, o.slot, o.slot_use - 1))
            snap[o.gidx] = dict(kn)
        for e in COMPUTE:
            c = 0
            for o in per[e]:
                if o.needs_inc and not o.dma:
                    c += 1
                    o.ctr = c
        st = self.stack
        esem = {e: st.enter_context(nc.semaphore("c_" + e)) for e in COMPUTE}
        dsem = {}
        for e in engs:
            for s in range(min(NSLOT, dcount[e])):
                dsem[(e, s)] = st.enter_context(nc.semaphore("d_%s_%d" % (e, s)))
        last_use = {}
        for o in ops:
            if o.dma:
                last_use[(o.eng, o.slot)] = o.slot_use

        def run(e, ename):
            for o in per[ename]:
                for w in o.waits:
                    if w[0] == "dma":
                        e.wait_ge(dsem[(w[1], w[2])], 16 * w[3])
                    else:
                        pr = ops[w[1]]
                        e.wait_ge(esem[pr.eng], pr.ctr)
                ins = o.fn(e)
                if o.dma:
                    ins.then_inc(dsem[(o.eng, o.slot)], 16)
                elif o.needs_inc:
                    ins.then_inc(esem[o.eng], 1)
            for (qe, s), u in last_use.items():
                if qe == ename:
                    e.wait_ge(dsem[(qe, s)], 16 * u)

        with nc.Block() as block:
            for ename in engs:
                if per[ename]:
                    getattr(block, ename)(lambda e, ename=ename: run(e, ename))
        st.close()
        self.stats = {e: len(per[e]) for e in engs}


class StopBuild(Exception):
    pass


class Arena:
    def __init__(self, p, name, words):
        self.t = p.sb(name, [128, words], F32)
        self.p = p
        self.words = words
        self.off = 0
        self.marks = []

    def take(self, shape, dtype):
        n = 1
        for s in shape:
            n *= s
        w = n if dtype == F32 else (n + 1) // 2
        assert self.off + w <= self.words, ("arena overflow", self.off, w, self.words)
        ap = self.t[:, self.off:self.off + w]
        self.off += w
        if dtype != F32:
            ap = ap.bitcast(dtype)[:, 0:n]
        if len(shape) == 2:
            ap = ap.rearrange("p (a b) -> p a b", a=shape[0])
        elif len(shape) == 3:
            ap = ap.rearrange("p (a b c) -> p a b c", a=shape[0], b=shape[1])
        elif len(shape) == 4:
            ap = ap.rearrange("p (a b c d) -> p a b c d", a=shape[0], b=shape[1], c=shape[2])
        return ap

    def mark(self):
        self.marks.append(self.off)

    def release(self):
        self.off = self.marks.pop()
        self.p.barrier()


def token_groups(include_ctx=True):
    g = []
    if include_ctx:
        g.append((0, CL, 1))
    for i in range(L // 512):
        g.append((CL + i * 512, 512, 0))
    return g


def build(opts=None):
    opts = opts or {}
    mixers = opts.get("mixers", True)
    nc = bass.Bass("TRN2", target_bir_lowering=False)
    dr = {}

    def din(name, shape, dtype=F32):
        dr[name] = nc.dram_tensor(name, list(shape), dtype, kind="ExternalInput").ap()
        return dr[name]

    xin = din("xin", [D, T])
    ccols = din("ccols", [128, 8, 2])
    adab = din("adab", [128, DEPTH, 48])
    gmix = din("gmix", [128, DEPTH, 8])
    gffn = din("gffn", [128, DEPTH, 8])
    gfin = din("gfin", [128, 8])
    ada_w = din("ada_w", [DEPTH, D, 6 * D])
    ffn_w1 = din("ffn_w1", [DEPTH, D, DFF])
    ffn_w2 = din("ffn_w2", [DEPTH, DFF, D])
    ident_in = din("ident", [128, 128])
    s5ar = din("s5ar", [2, 128, 64])
    s5ai = din("s5ai", [2, 128, 64])
    s5ldt = din("s5ldt", [2, 128, 64])
    s5b = din("s5b", [2, 128, 2, 64, 16])
    s5c = din("s5c", [2, 128, 2, 64, 16])
    s5d_in = din("s5d", [128, 2, 8])
    expo = din("expo", [128, 35])
    parm_in = din("parm", [128, 2])
    dmask_in = din("dmask", [128, 8, 16])
    s5_w_glu = din("s5_w_glu", [2, D, 2 * D])
    attn_w_qkv = din("attn_w_qkv", [2, D, 1536])
    attn_w_o = din("attn_w_o", [2, D, D])
    rope_c = din("rope_c", [128, L])
    rope_s = din("rope_s", [128, L])
    bd_in = din("bd", [128, 128])
    prot_in = din("prot", [128, 128])
    qkg = din("qkg", [128, 2, 2])
    outT = nc.dram_tensor("outT", [D, L], F32, kind="ExternalOutput").ap()
    dbg = nc.dram_tensor("dbg", [128, 20480], F32, kind="ExternalOutput").ap() if opts.get("debug") else None
    dbg_off = [0]

    def dump(ap2d, n, keys):
        if dbg is None:
            return
        p.dma("sync", dbg[:, dbg_off[0]:dbg_off[0] + n], ap2d, reads=keys)
        print("dump at", dbg_off[0], n, keys)
        dbg_off[0] += n

    p = Prog(nc)
    ar = Arena(p, "arena", 53200)
    psb = [p.ps("psb%d" % i, [128, 512], F32) for i in range(8)]
    PSK = ["psb%d" % i for i in range(8)]

    xT = ar.take([8, T], F32)
    modT = ar.take([DEPTH, 48, 2], F32)
    adab_s = ar.take([DEPTH, 48], F32)
    gmix_s = ar.take([DEPTH, 8], F32)
    gffn_s = ar.take([DEPTH, 8], F32)
    gfin_s = ar.take([8], F32)
    gs1 = ar.take([DEPTH, 8, 2], F32)
    gs2 = ar.take([DEPTH, 8, 2], F32)
    cc_s = ar.take([8, 2], F32)
    sc_bf = ar.take([8, 2], BF)
    ones_bf = ar.take([128], BF)
    ident_f = ar.take([128], F32)
    ident_bf = ar.take([128], BF)

    s5d_s = ar.take([2, 8], F32)
    parm_s = ar.take([2], F32)
    dmask_s = ar.take([8, 16], F32)
    p.dma("sync", s5d_s, s5d_in, writes=["s5d_s"])
    p.dma("sync", parm_s, parm_in, writes=["parm_s"])
    p.dma("sync", dmask_s, dmask_in, writes=["dmask_s"])
    for k in range(8):
        p.dma("sync", xT[:, k, :], xin[k * 128:(k + 1) * 128, :], writes=[("xT", k)])
    p.dma("sync", cc_s, ccols, writes=["cc_s"])
    p.dma("sync", adab_s, adab, writes=["adab_s"])
    p.dma("sync", gmix_s, gmix, writes=["gmix_s"])
    p.dma("sync", gffn_s, gffn, writes=["gffn_s"])
    p.dma("sync", gfin_s, gfin, writes=["gfin_s"])
    p.dma("sync", ident_f, ident_in, writes=["ident_f"])
    p.op("vector", lambda e: e.memset(ones_bf, 1.0), writes=["ones_bf"])
    p.op("vector", lambda e: e.tensor_copy(out=ident_bf, in_=ident_f), reads=["ident_f"], writes=["ident_bf"])
    p.op("scalar", lambda e: e.activation(out=sc_bf, in_=cc_s, func=AF.Silu), reads=["cc_s"], writes=["sc_bf"])

    def mod_phase(i):
        ar.mark()
        wA = [ar.take([8, 1024], BF) for _ in range(2)]
        pm = psb[7]
        for piece in range(6):
            wa = wA[piece % 2]
            wk = ("wA", piece % 2)
            p.dma("gpsimd", wa, ada_w[i][:, piece * 1024:(piece + 1) * 1024].rearrange("(k p) n -> p k n", p=128),
                  writes=[wk])
            for n in range(8):
                j = piece * 8 + n
                for k in range(8):
                    p.op("tensor", lambda e, wa=wa, k=k, n=n, j=j: e.matmul(
                        pm[:, 2 * j:2 * j + 2], lhsT=wa[:, k, n * 128:(n + 1) * 128], rhs=sc_bf[:, k, :],
                        start=(k == 0), stop=(k == 7)), reads=[wk, "sc_bf"], writes=[PSK[7]])
        p.op("vector", lambda e: e.tensor_tensor(
            out=modT[:, i], in0=pm[:, 0:96].rearrange("p (j c) -> p j c", c=2),
            in1=adab_s[:, i].unsqueeze(2).broadcast_to([128, 48, 2]), op=ALU.add),
            reads=[PSK[7], "adab_s"], writes=[("modT", i)])
        for (dst, dk, g_s, gk, which) in ((gs1, "gs1", gmix_s, "gmix_s", 1), (gs2, "gs2", gffn_s, "gffn_s", 4)):
            p.op("vector", lambda e, dst=dst, g_s=g_s, which=which: e.scalar_tensor_tensor(
                out=dst[:, i], in0=modT[:, i, which * 8:(which + 1) * 8, :], scalar=1.0,
                in1=g_s[:, i].unsqueeze(2).broadcast_to([128, 8, 2]), op0=ALU.add, op1=ALU.mult),
                reads=[("modT", i), gk], writes=[(dk, i)])
        ar.release()

    def mod_finish(i):
        pm = psb[7]
        p.op("vector", lambda e: e.tensor_tensor(
            out=modT[:, i], in0=pm[:, 0:96].rearrange("p (j c) -> p j c", c=2),
            in1=adab_s[:, i].unsqueeze(2).broadcast_to([128, 48, 2]), op=ALU.add),
            reads=[PSK[7], "adab_s"], writes=[("modT", i)])
        for (dst, dk, g_s, gk, which) in ((gs1, "gs1", gmix_s, "gmix_s", 1), (gs2, "gs2", gffn_s, "gffn_s", 4)):
            p.op("vector", lambda e, dst=dst, g_s=g_s, which=which: e.scalar_tensor_tensor(
                out=dst[:, i], in0=modT[:, i, which * 8:(which + 1) * 8, :], scalar=1.0,
                in1=g_s[:, i].unsqueeze(2).broadcast_to([128, 8, 2]), op0=ALU.add, op1=ALU.mult),
                reads=[("modT", i), gk], writes=[(dk, i)])

    def mod_side(i, bufs):
        pm = psb[7]
        pieces = []
        for jn in range(48):
            def piece(jn=jn):
                wa = bufs[jn % len(bufs)]
                wk = ("wAs", jn % len(bufs))
                p.dma("gpsimd", wa, ada_w[i][:, jn * 128:(jn + 1) * 128].rearrange("(k p) n -> p k n", p=128), writes=[wk])
                for k in range(8):
                    p.op("tensor", lambda e, wa=wa, k=k: e.matmul(pm[:, 2 * jn:2 * jn + 2], lhsT=wa[:, k, :], rhs=sc_bf[:, k, :],
                                                               start=(k == 0), stop=(k == 7)), reads=[wk, "sc_bf"], writes=[PSK[7]])
            pieces.append(piece)
        return pieces

    def mcol(i, which, k, c):
        return modT[:, i, which * 8 + k, c:c + 1]

    def norm_phase(hT, hkey, gcol, shcol, groups, rkeys):
        ar.mark()
        sq = ar.take([8, 512], BF)
        rstd = ar.take([512], F32)
        tmp = [ar.take([512], F32) for _ in range(2)]
        pss = psb[6]
        for gi, (t0, w, c) in enumerate(groups):
            for k in range(8):
                p.op("scalar", lambda e, k=k, t0=t0, w=w: e.activation(out=sq[:, k, :w], in_=xT[:, k, t0:t0 + w], func=AF.Square),
                     reads=[("xT", k)], writes=[("sq", k)])
            for k in range(8):
                p.op("tensor", lambda e, k=k, w=w: e.matmul(pss[:, :w], lhsT=ones_bf, rhs=sq[:, k, :w], start=(k == 0), stop=(k == 7)),
                     reads=[("sq", k), "ones_bf"], writes=[PSK[6]])
            p.op("scalar", lambda e, w=w: e.activation(out=rstd[:, :w], in_=pss[:, :w], func=AF.Sqrt, bias=EPS, scale=1.0 / D),
                 reads=[PSK[6]], writes=["rstd"])
            p.op("vector", lambda e, w=w: e.reciprocal(out=rstd[:, :w], in_=rstd[:, :w]), reads=["rstd"], writes=["rstd"])
            for k in range(8):
                tm = tmp[k % 2]
                p.op("vector", lambda e, k=k, t0=t0, w=w, c=c, tm=tm: e.scalar_tensor_tensor(
                    out=tm[:, :w], in0=xT[:, k, t0:t0 + w], scalar=gcol(k, c), in1=rstd[:, :w], op0=ALU.mult, op1=ALU.mult),
                    reads=[("xT", k), "rstd"] + rkeys, writes=[("ntmp", k % 2)])
                if shcol is None:
                    p.op("scalar", lambda e, k=k, t0=t0, w=w, tm=tm: e.activation(out=hT[:, k, t0:t0 + w], in_=tm[:, :w], func=AF.Identity),
                         reads=[("ntmp", k % 2)], writes=[(hkey, k, gi)])
                else:
                    p.op("scalar", lambda e, k=k, t0=t0, w=w, c=c, tm=tm: e.activation(
                        out=hT[:, k, t0:t0 + w], in_=tm[:, :w], func=AF.Identity, bias=shcol(k, c)),
                        reads=[("ntmp", k % 2)] + rkeys, writes=[(hkey, k, gi)])
        ar.release()

    def ffn_phase(i, hT, hkey, groups, side_layer=None):
        ar.mark()
        side = []
        if side_layer is not None:
            side = mod_side(side_layer, [ar.take([8, 128], BF) for _ in range(4)])
        nblk = 0
        w1q = [ar.take([8, 1024], BF) for _ in range(2)]
        w2q = [ar.take([8, 1024], BF) for _ in range(2)]
        aT = [ar.take([8, 512], BF) for _ in range(2)]
        rt = [ar.take([512], BF) for _ in range(2)]
        cnt = 0
        for q in range(4):
            b = q % 2
            p.dma("gpsimd", w1q[b], ffn_w1[i][:, q * 1024:(q + 1) * 1024].rearrange("(k p) n -> p k n", p=128), writes=[("w1q", b)])
            p.dma("gpsimd", w2q[b], ffn_w2[i][q * 1024:(q + 1) * 1024, :].rearrange("(f p) n -> p f n", p=128), writes=[("w2q", b)])
            for gi, (t0, w, c) in enumerate(groups):
                ab = cnt % 2
                cnt += 1
                a_t = aT[ab]
                for f in range(8):
                    ps = psb[f % 2]
                    for k in range(8):
                        p.op("tensor", lambda e, ps=ps, b=b, k=k, f=f, t0=t0, w=w: e.matmul(
                            ps[:, :w], lhsT=w1q[b][:, k, f * 128:(f + 1) * 128], rhs=hT[:, k, t0:t0 + w],
                            start=(k == 0), stop=(k == 7)), reads=[("w1q", b), (hkey, k, gi)], writes=[PSK[f % 2]])
                    r_t = rt[f % 2]
                    p.op("scalar", lambda e, ps=ps, r_t=r_t, w=w: e.activation(out=r_t[:, :w], in_=ps[:, :w], func=AF.Relu),
                         reads=[PSK[f % 2]], writes=[("rt", f % 2)])
                    p.op("vector", lambda e, r_t=r_t, a_t=a_t, f=f, w=w: e.tensor_tensor(out=a_t[:, f, :w], in0=r_t[:, :w], in1=r_t[:, :w], op=ALU.mult),
                         reads=[("rt", f % 2)], writes=[("aT", ab, f)])
                    nblk += 1
                    if side and nblk % 3 == 0:
                        side.pop(0)()
                for d in range(8):
                    ps2 = psb[2 + d % 2]
                    for f in range(8):
                        p.op("tensor", lambda e, ps2=ps2, b=b, f=f, d=d, a_t=a_t, w=w: e.matmul(
                            ps2[:, :w], lhsT=w2q[b][:, f, d * 128:(d + 1) * 128], rhs=a_t[:, f, :w],
                            start=(f == 0), stop=(f == 7)), reads=[("w2q", b), ("aT", ab, f)], writes=[PSK[2 + d % 2]])
                    p.op("vector", lambda e, ps2=ps2, d=d, t0=t0, w=w, c=c: e.scalar_tensor_tensor(
                        out=xT[:, d, t0:t0 + w], in0=ps2[:, :w], scalar=mcol(i, 5, d, c), in1=xT[:, d, t0:t0 + w],
                        op0=ALU.mult, op1=ALU.add), reads=[PSK[2 + d % 2], ("modT", i), ("xT", d)], writes=[("xT", d)])
        while side:
            side.pop(0)()
        if side_layer is not None:
            mod_finish(side_layer)
        ar.release()

    TWO_PI = 2.0 * math.pi
    MAGIC = 12582912.0
    CW1 = 6.28125
    CW2 = 0.0019350051879882812
    CW3 = TWO_PI - CW1 - CW2
    PI_LO = 3.1415925

    def s5_phase(i):
        j = i // 2
        V = "vector"
        G = "gpsimd"
        A = "scalar"
        ar.mark()
        PCr = ar.take([64, 17], F32)
        PCi = ar.take([64, 17], F32)
        BBb = ar.take([2, 64, 16], BF)
        A1v = ar.take([2, 64], F32)
        A2v = ar.take([2, 64], F32)
        NGH = 64
        VH = ar.take([2, NGH, NCH + 1], BF)
        ar.mark()
        PBr = ar.take([64, 16], F32)
        PBi = ar.take([64, 16], F32)
        ar.mark()
        ar_s = ar.take([64], F32); ai_s = ar.take([64], F32); ldt_s = ar.take([64], F32)
        ex_s = ar.take([35], F32)
        p.dma("sync", ar_s, s5ar[j], writes=["ar_s"])
        p.dma("sync", ai_s, s5ai[j], writes=["ai_s"])
        p.dma("sync", ldt_s, s5ldt[j], writes=["ldt_s"])
        p.dma("sync", ex_s, expo, writes=["ex_s"])
        dt = ar.take([64], F32); dar = ar.take([64], F32); th = ar.take([64], F32)
        p.op(A, lambda e: e.activation(out=dt, in_=ldt_s, func=AF.Exp), reads=["ldt_s"], writes=["dt"])
        p.op(V, lambda e: e.tensor_tensor(out=dar, in0=dt, in1=ar_s, op=ALU.mult), reads=["dt", "ar_s"], writes=["dar"])
        p.op(V, lambda e: e.tensor_tensor(out=th, in0=dt, in1=ai_s, op=ALU.mult), reads=["dt", "ai_s"], writes=["th"])
        GH = 32
        b_s = ar.take([2, GH, 16], F32)
        lm = ar.take([GH, 35], F32); ang = ar.take([GH, 35], F32); kk = ar.take([GH, 35], F32); rc = ar.take([GH, 35], F32)
        den = ar.take([GH], F32); t0_ = ar.take([GH], F32); nr = ar.take([GH], F32); fr = ar.take([GH], F32); fi = ar.take([GH], F32)
        tb1 = ar.take([GH, 16], F32); tb2 = ar.take([GH, 16], F32)
        for g2h in range(2):
            gs_ = slice(GH * g2h, GH * g2h + GH)
            p.dma("sync", b_s, s5b[j][:, :, gs_, :], writes=["b_s"])
            exb = ex_s.unsqueeze(1).broadcast_to([128, GH, 35])
            p.op(V, lambda e, gs_=gs_, exb=exb: e.tensor_tensor(out=lm, in0=dar[:, gs_].unsqueeze(2).broadcast_to([128, GH, 35]), in1=exb, op=ALU.mult),
                 reads=["dar", "ex_s"], writes=["lm"])
            p.op(V, lambda e, gs_=gs_, exb=exb: e.tensor_tensor(out=ang, in0=th[:, gs_].unsqueeze(2).broadcast_to([128, GH, 35]), in1=exb, op=ALU.mult),
                 reads=["th", "ex_s"], writes=["ang"])
            p.op(A, lambda e: e.activation(out=lm, in_=lm, func=AF.Exp), reads=["lm"], writes=["lm"])
            p.op(V, lambda e: e.tensor_scalar(out=kk, in0=ang, scalar1=1.0 / TWO_PI, scalar2=MAGIC, op0=ALU.mult, op1=ALU.add), reads=["ang"], writes=["kk"])
            p.op(V, lambda e: e.tensor_scalar(out=kk, in0=kk, scalar1=-MAGIC, scalar2=None, op0=ALU.add), reads=["kk"], writes=["kk"])
            p.op(V, lambda e: e.scalar_tensor_tensor(out=ang, in0=kk, scalar=-CW1, in1=ang, op0=ALU.mult, op1=ALU.add), reads=["kk", "ang"], writes=["ang"])
            p.op(V, lambda e: e.scalar_tensor_tensor(out=ang, in0=kk, scalar=-CW2, in1=ang, op0=ALU.mult, op1=ALU.add), reads=["kk", "ang"], writes=["ang"])
            p.op(V, lambda e: e.scalar_tensor_tensor(out=ang, in0=kk, scalar=-CW3, in1=ang, op0=ALU.mult, op1=ALU.add), reads=["kk", "ang"], writes=["ang"])
            p.op(V, lambda e: e.tensor_scalar(out=ang, in0=ang, scalar1=PI_LO, scalar2=-PI_LO, op0=ALU.min, op1=ALU.max), reads=["ang"], writes=["ang"])
            p.op(V, lambda e: e.tensor_scalar(out=kk, in0=ang, scalar1=math.pi / 2, scalar2=-TWO_PI, op0=ALU.is_gt, op1=ALU.mult), reads=["ang"], writes=["kk"])
            p.op(V, lambda e: e.scalar_tensor_tensor(out=rc, in0=ang, scalar=math.pi / 2, in1=kk, op0=ALU.add, op1=ALU.add), reads=["ang", "kk"], writes=["rc"])
            p.op(V, lambda e: e.tensor_scalar(out=rc, in0=rc, scalar1=PI_LO, scalar2=-PI_LO, op0=ALU.min, op1=ALU.max), reads=["rc"], writes=["rc"])
            p.op(A, lambda e: e.activation(out=ang, in_=ang, func=AF.Sin), reads=["ang"], writes=["ang"])
            p.op(A, lambda e: e.activation(out=rc, in_=rc, func=AF.Sin), reads=["rc"], writes=["rc"])
            p.op(V, lambda e: e.tensor_tensor(out=rc, in0=lm, in1=rc, op=ALU.mult), reads=["lm", "rc"], writes=["rc"])
            p.op(V, lambda e: e.tensor_tensor(out=ang, in0=lm, in1=ang, op=ALU.mult), reads=["lm", "ang"], writes=["ang"])
            p.op(V, lambda e, gs_=gs_: e.tensor_copy(out=PCr[:, gs_, :], in_=rc[:, :, 0:17]), reads=["rc"], writes=["PCr"])
            p.op(V, lambda e, gs_=gs_: e.tensor_copy(out=PCi[:, gs_, :], in_=ang[:, :, 0:17]), reads=["ang"], writes=["PCi"])
            p.op(V, lambda e, gs_=gs_: e.tensor_copy(out=PBr[:, gs_, :], in_=rc[:, :, 17:33]), reads=["rc"], writes=["PBr"])
            p.op(V, lambda e, gs_=gs_: e.tensor_copy(out=PBi[:, gs_, :], in_=ang[:, :, 17:33]), reads=["ang"], writes=["PBi"])
            a1r = rc[:, :, 33]
            a1i = ang[:, :, 33]
            ars = ar_s[:, gs_]
            ais = ai_s[:, gs_]
            p.op(V, lambda e, ars=ars: e.tensor_tensor(out=den, in0=ars, in1=ars, op=ALU.mult), reads=["ar_s"], writes=["den"])
            p.op(V, lambda e, ais=ais: e.tensor_tensor(out=t0_, in0=ais, in1=ais, op=ALU.mult), reads=["ai_s"], writes=["t0_"])
            p.op(V, lambda e: e.tensor_tensor(out=den, in0=den, in1=t0_, op=ALU.add), reads=["den", "t0_"], writes=["den"])
            p.op(V, lambda e: e.reciprocal(out=den, in_=den), reads=["den"], writes=["den"])
            p.op(V, lambda e, a1r=a1r: e.tensor_scalar(out=nr, in0=a1r, scalar1=-1.0, scalar2=None, op0=ALU.add), reads=["rc"], writes=["nr"])
            p.op(V, lambda e, ars=ars: e.tensor_tensor(out=fr, in0=nr, in1=ars, op=ALU.mult), reads=["nr", "ar_s"], writes=["fr"])
            p.op(V, lambda e, a1i=a1i, ais=ais: e.tensor_tensor(out=t0_, in0=a1i, in1=ais, op=ALU.mult), reads=["ang", "ai_s", "den"], writes=["t0_"])
            p.op(V, lambda e: e.tensor_tensor(out=fr, in0=fr, in1=t0_, op=ALU.add), reads=["fr", "t0_"], writes=["fr"])
            p.op(V, lambda e: e.tensor_tensor(out=fr, in0=fr, in1=den, op=ALU.mult), reads=["fr", "den"], writes=["fr"])
            p.op(V, lambda e, a1i=a1i, ars=ars: e.tensor_tensor(out=fi, in0=a1i, in1=ars, op=ALU.mult), reads=["ang", "ar_s"], writes=["fi"])
            p.op(V, lambda e, ais=ais: e.tensor_tensor(out=t0_, in0=nr, in1=ais, op=ALU.mult), reads=["nr", "ai_s", "fr"], writes=["t0_"])
            p.op(V, lambda e: e.tensor_tensor(out=fi, in0=fi, in1=t0_, op=ALU.subtract), reads=["fi", "t0_"], writes=["fi"])
            p.op(V, lambda e: e.tensor_tensor(out=fi, in0=fi, in1=den, op=ALU.mult), reads=["fi", "den"], writes=["fi"])
            frb = fr.unsqueeze(2).broadcast_to([128, GH, 16])
            fib = fi.unsqueeze(2).broadcast_to([128, GH, 16])
            p.op(V, lambda e, frb=frb: e.tensor_tensor(out=tb1, in0=b_s[:, 0], in1=frb, op=ALU.mult), reads=["b_s", "fr"], writes=["tb1"])
            p.op(V, lambda e, fib=fib: e.tensor_tensor(out=tb2, in0=b_s[:, 1], in1=fib, op=ALU.mult), reads=["b_s", "fi"], writes=["tb2"])
            p.op(V, lambda e, gs_=gs_: e.tensor_tensor(out=BBb[:, 0, gs_, :], in0=tb1, in1=tb2, op=ALU.subtract), reads=["tb1", "tb2"], writes=["BBb"])
            p.op(V, lambda e, frb=frb: e.tensor_tensor(out=tb1, in0=b_s[:, 1], in1=frb, op=ALU.mult), reads=["b_s", "fr", "BBb"], writes=["tb1"])
            p.op(V, lambda e, fib=fib: e.tensor_tensor(out=tb2, in0=b_s[:, 0], in1=fib, op=ALU.mult), reads=["b_s", "fi", "BBb"], writes=["tb2"])
            p.op(V, lambda e, gs_=gs_: e.tensor_tensor(out=BBb[:, 1, gs_, :], in0=tb1, in1=tb2, op=ALU.add), reads=["tb1", "tb2"], writes=["BBb"])
            p.op(V, lambda e, gs_=gs_: e.tensor_copy(out=A1v[:, 0, gs_], in_=rc[:, :, 34]), reads=["rc"], writes=["A1v"])
            p.op(V, lambda e, gs_=gs_: e.tensor_copy(out=A1v[:, 1, gs_], in_=rc[:, :, 34]), reads=["rc", "A1v"], writes=["A1v"])
            p.op(V, lambda e, gs_=gs_: e.tensor_scalar(out=A2v[:, 0, gs_], in0=ang[:, :, 34], scalar1=-1.0, scalar2=None, op0=ALU.mult), reads=["ang"], writes=["A2v"])
            p.op(V, lambda e, gs_=gs_: e.tensor_copy(out=A2v[:, 1, gs_], in_=ang[:, :, 34]), reads=["ang", "A2v"], writes=["A2v"])
        ar.release()

        norm_phase(hT, "hT", lambda k, c: gs1[:, i, k, c:c + 1], lambda k, c: mcol(i, 0, k, c), token_groups(True),
                   [("gs1", i), ("modT", i)])
        hkeys = lambda k: [("hT", k, gi) for gi in range(5)]

        for half in range(1):
            ks = range(8)
            p.op(G, lambda e: e.memset(VH[0:64, :, :, 0:1], 0.0), writes=["VH"])
            p.op(G, lambda e: e.memset(VH[64:128, :, :, NCH:NCH + 1], 0.0), reads=["VH"], writes=["VH"])
            ar.mark()
            ABb = [ar.take([2, 16, 8, 16], BF) for _ in range(2)]
            ABt = ar.take([2, 16, 2, 128], BF)
            vt1 = ar.take([16, 2, 16], F32)
            vt2 = ar.take([16, 2, 16], F32)
            vcnt = 0
            def gen_ab(k):
                AB_ = ABb[k % 2]
                abk = "AB%d" % (k % 2)
                for hh in range(4):
                    gsl = slice(8 * k + 2 * hh, 8 * k + 2 * hh + 2)
                    pbr = PBr[:, gsl, :].rearrange("p g s -> p s g").unsqueeze(3).broadcast_to([128, 16, 2, 16])
                    pbi = PBi[:, gsl, :].rearrange("p g s -> p s g").unsqueeze(3).broadcast_to([128, 16, 2, 16])
                    bbr = BBb[:, 0, gsl, :].unsqueeze(1).broadcast_to([128, 16, 2, 16])
                    bbi = BBb[:, 1, gsl, :].unsqueeze(1).broadcast_to([128, 16, 2, 16])
                    abr = AB_[:, 0, :, 2 * hh:2 * hh + 2, :]
                    abi = AB_[:, 1, :, 2 * hh:2 * hh + 2, :]
                    p.op(G, lambda e, pbr=pbr, bbr=bbr: e.tensor_tensor(out=vt1, in0=pbr, in1=bbr, op=ALU.mult), reads=["PBr", "BBb"], writes=["vt1"])
                    p.op(G, lambda e, pbi=pbi, bbi=bbi: e.tensor_tensor(out=vt2, in0=pbi, in1=bbi, op=ALU.mult), reads=["PBi", "BBb"], writes=["vt2"])
                    p.op(G, lambda e, abr=abr: e.tensor_tensor(out=abr, in0=vt1, in1=vt2, op=ALU.subtract), reads=["vt1", "vt2"], writes=[abk])
                    p.op(G, lambda e, pbr=pbr, bbi=bbi: e.tensor_tensor(out=vt1, in0=pbr, in1=bbi, op=ALU.mult), reads=["PBr", "BBb"], writes=["vt1"])
                    p.op(G, lambda e, pbi=pbi, bbr=bbr: e.tensor_tensor(out=vt2, in0=pbi, in1=bbr, op=ALU.mult), reads=["PBi", "BBb"], writes=["vt2"])
                    p.op(G, lambda e, abi=abi: e.tensor_tensor(out=abi, in0=vt1, in1=vt2, op=ALU.add), reads=["vt1", "vt2"], writes=[abk])

            gen_ab(0)
            for k in ks:
                AB = ABb[k % 2]
                ABK = "AB%d" % (k % 2)
                if k + 1 < 8:
                    gen_ab(k + 1)
                for ri in range(2):
                    for sb_ in range(2):
                        bank = 6 + (2 * ri + sb_) % 2
                        pbf = psb[bank].bitcast(BF)
                        p.pe_fence()
                        for s8 in range(8):
                            s = sb_ * 8 + s8
                            p.op("tensor", lambda e, pbf=pbf, ri=ri, s=s, s8=s8, AB=AB: e.transpose(
                                pbf[:, s8 * 128:(s8 + 1) * 128], AB[:, ri, s, :, :].rearrange("p g h -> p (g h)"), ident_bf),
                                reads=[ABK, "ident_bf"], writes=[PSK[bank]])
                        for g2 in range(2):
                            eng = A if g2 == 0 else V
                            src = pbf[:, 0:1024].rearrange("p (s q) -> p s q", s=8)
                            dst = ABt[:, g2, sb_ * 8:sb_ * 8 + 8, ri, :]
                            if g2 == 0:
                                p.op(A, lambda e, src=src, dst=dst, g2=g2: e.activation(out=dst, in_=src, func=AF.Identity, scale=parm_s[:, g2:g2 + 1]),
                                     reads=[PSK[bank], "parm_s"], writes=["ABt", PSK[bank]])
                            else:
                                p.op(V, lambda e, src=src, dst=dst, g2=g2: e.tensor_scalar(out=dst, in0=src, scalar1=parm_s[:, g2:g2 + 1], scalar2=None, op0=ALU.mult),
                                     reads=[PSK[bank], "parm_s"], writes=["ABt", PSK[bank]])
                for g2 in range(2):
                    for ri in range(2):
                        p.pe_fence()
                        for s in range(16):
                            for q in range(4):
                                p.op("tensor", lambda e, q=q, g2=g2, s=s, ri=ri, k=k: e.matmul(
                                    psb[q][:, 0:NCH], lhsT=ABt[32 * q:32 * q + 32, g2, s, ri, :], rhs=hT[32 * q:32 * q + 32, k, s:T:Q],
                                    start=(s == 0), stop=(s == 15), tile_position=(32 * q, 0)),
                                    reads=["ABt"] + hkeys(k), writes=[PSK[q]])
                        for q in range(4):
                            gl = 2 * q + g2
                            gh = 8 * k + gl
                            ps = psb[q]
                            p.op(A, lambda e, ps=ps, ri=ri, gh=gh: e.activation(out=VH[0:64, ri, gh, 1:NCH + 1], in_=ps[0:64, 0:NCH], func=AF.Identity),
                                 reads=[PSK[q]], writes=[("VH", ri, gh, 0)])
                            p.op(V, lambda e, ps=ps, ri=ri, gh=gh: e.tensor_copy(out=VH[64:128, ri, gh, 0:NCL], in_=ps[64:128, NCC:NCH]),
                                 reads=[PSK[q]], writes=[("VH", ri, gh, 1)])
                            p.op(V, lambda e, ps=ps, ri=ri, gh=gh: e.tensor_copy(out=VH[64:128, ri, gh, NCL:NCH], in_=ps[64:128, 0:NCC]),
                                 reads=[PSK[q]], writes=[("VH", ri, gh, 2)])
            ar.release()
            ar.release()
            vh_all = ["VH"] + [("VH", ri, gh, x) for ri in range(2) for gh in range(NGH) for x in range(3)]
            if opts.get("s5_stop") == 2:
                vdb = ar.take([2, 4, NCH + 1], F32)
                p.op(V, lambda e: e.tensor_copy(out=vdb, in_=VH[:, :, 0:4, :]), reads=vh_all, writes=["vdb"])
                dump(vdb.rearrange("p a g c -> p (a g c)"), 2 * 4 * (NCH + 1), ["vdb"])
                raise StopBuild()
            ar.mark()
            Hc = [ar.take([2, NGH], F32) for _ in range(2)]
            sP1 = ar.take([2, NGH], F32)
            sP2 = ar.take([2, NGH], F32)
            a1h = A1v
            a2h = A2v
            first = True
            SCAN_ENG = {"f": V, "r": V}
            for st in range(NCH - 1):
                cur = Hc[st % 2]
                nxt = Hc[(st + 1) % 2]
                chains = ((0, 64, st + 1, "f"), (64, 128, NCH - 1 - st, "r"))
                rk = vh_all if first else []
                if st == 0:
                    for (lo, hi_, col, tag) in chains:
                        vcol = VH[lo:hi_, :, :, col]
                        p.op(SCAN_ENG[tag], lambda e, nxt=nxt, lo=lo, hi_=hi_, vcol=vcol: e.tensor_copy(out=nxt[lo:hi_], in_=vcol),
                             reads=rk, writes=[("Hc", (st + 1) % 2, tag)])
                    first = False
                    continue
                for stage in range(6):
                    for (lo, hi_, col, tag) in chains:
                        eng = SCAN_ENG[tag]
                        vcol = VH[lo:hi_, :, :, col]
                        if stage == 0:
                            p.op(eng, lambda e, cur=cur, lo=lo, hi_=hi_, a1h=a1h: e.tensor_tensor(out=sP1[lo:hi_], in0=cur[lo:hi_], in1=a1h[lo:hi_], op=ALU.mult),
                                 reads=[("Hc", st % 2, tag), "A1v"], writes=[("sP1", tag)])
                        elif stage == 1:
                            p.op(G, lambda e, cur=cur, lo=lo, hi_=hi_, a2h=a2h: e.tensor_tensor(out=sP2[lo:hi_], in0=cur[lo:hi_, ::-1, :], in1=a2h[lo:hi_], op=ALU.mult),
                                 reads=[("Hc", st % 2, tag), "A2v"], writes=[("sP2", tag, 0), ("sP2", tag, 1)])
                        elif stage == 2:
                            continue
                        elif stage == 3:
                            p.op(eng, lambda e, lo=lo, hi_=hi_: e.tensor_tensor(out=sP1[lo:hi_], in0=sP1[lo:hi_], in1=sP2[lo:hi_], op=ALU.add),
                                 reads=[("sP1", tag), ("sP2", tag, 0), ("sP2", tag, 1)], writes=[("sP1", tag)])
                        elif stage == 4:
                            p.op(eng, lambda e, nxt=nxt, lo=lo, hi_=hi_, vcol=vcol: e.tensor_tensor(out=nxt[lo:hi_], in0=sP1[lo:hi_], in1=vcol, op=ALU.add),
                                 reads=[("sP1", tag)], writes=[("Hc", (st + 1) % 2, tag)])
                        else:
                            p.op(A, lambda e, nxt=nxt, lo=lo, hi_=hi_, vcol=vcol: e.activation(out=vcol, in_=nxt[lo:hi_], func=AF.Identity),
                                 reads=[("Hc", (st + 1) % 2, tag)], writes=[("VHc", tag)])
            ar.release()
            vh_done = [("VHc", "f"), ("VHc", "r")] + vh_all
            if opts.get("s5_stop") == 3:
                vdb = ar.take([2, 4, NCH + 1], F32)
                p.op(V, lambda e: e.tensor_copy(out=vdb, in_=VH[:, :, 0:4, :]), reads=vh_done, writes=["vdb"])
                dump(vdb.rearrange("p a g c -> p (a g c)"), 2 * 4 * (NCH + 1), ["vdb"])
                raise StopBuild()
            ar.mark()
            c_kb = [ar.take([2, 8, 16], F32) for _ in range(2)]
            ncr_kb = [ar.take([8, 16], F32) for _ in range(2)]
            Cqb = [ar.take([2, 8, 17, 16], BF) for _ in range(2)]
            BBp = ar.take([2, 8, 128], BF)
            Kc = ar.take([8, 31, 16], BF)
            Yl = ar.take([16, 8, 16], BF)
            Yc = Yl
            ct1 = ar.take([2, 17, 16], F32)
            ct2 = ar.take([2, 17, 16], F32)
            kt = ar.take([16], F32)
            kt2 = ar.take([16], F32)
            ycar = [ar.take([512], F32)] * 2
            ycnt = 0

            def gen_cq(k):
                kb = k % 2
                c_k = c_kb[kb]
                ncr_k = ncr_kb[kb]
                Cq_ = Cqb[kb]
                ck, nk, qk = "c_k%d" % kb, "ncr_k%d" % kb, "Cq%d" % kb
                p.dma("sync", c_k, s5c[j][:, :, 8 * k:8 * k + 8, :], writes=[ck])
                p.op(G, lambda e: e.tensor_scalar(out=ncr_k, in0=c_k[:, 0], scalar1=-1.0, scalar2=None, op0=ALU.mult), reads=[ck], writes=[nk])
                for hh in range(4):
                    gsl = slice(8 * k + 2 * hh, 8 * k + 2 * hh + 2)
                    lsl = slice(2 * hh, 2 * hh + 2)
                    pcr = PCr[:, gsl, :].unsqueeze(3).broadcast_to([128, 2, 17, 16])
                    pci = PCi[:, gsl, :].unsqueeze(3).broadcast_to([128, 2, 17, 16])
                    cr = c_k[:, 0, lsl, :].unsqueeze(2).broadcast_to([128, 2, 17, 16])
                    ci = c_k[:, 1, lsl, :].unsqueeze(2).broadcast_to([128, 2, 17, 16])
                    ncr = ncr_k[:, lsl, :].unsqueeze(2).broadcast_to([128, 2, 17, 16])
                    p.op(G, lambda e, cr=cr, pcr=pcr: e.tensor_tensor(out=ct1, in0=cr, in1=pcr, op=ALU.mult), reads=[ck, "PCr"], writes=["ct1"])
                    p.op(G, lambda e, ci=ci, pci=pci: e.tensor_tensor(out=ct2, in0=ci, in1=pci, op=ALU.mult), reads=[ck, "PCi"], writes=["ct2"])
                    p.op(G, lambda e, lsl=lsl, Cq_=Cq_: e.tensor_tensor(out=Cq_[:, 0, lsl], in0=ct1, in1=ct2, op=ALU.subtract), reads=["ct1", "ct2"], writes=[qk])
                    p.op(G, lambda e, ncr=ncr, pci=pci: e.tensor_tensor(out=ct1, in0=ncr, in1=pci, op=ALU.mult), reads=[nk, "PCi"], writes=["ct1"])
                    p.op(G, lambda e, ci=ci, pcr=pcr: e.tensor_tensor(out=ct2, in0=ci, in1=pcr, op=ALU.mult), reads=[ck, "PCr"], writes=["ct2"])
                    p.op(G, lambda e, lsl=lsl, Cq_=Cq_: e.tensor_tensor(out=Cq_[:, 1, lsl], in0=ct1, in1=ct2, op=ALU.subtract), reads=["ct1", "ct2"], writes=[qk])

            for k in ks:
                kb = k % 2
                Cq = Cqb[kb]
                CQK = "Cq%d" % kb
                if k == ks[0]:
                    gen_cq(k)
                p.op(G, lambda e: e.memset(BBp, 0.0), writes=["BBp"])
                for gl in range(8):
                    p.op(G, lambda e, gl=gl, k=k: e.tensor_copy(out=BBp[:, :, gl, gl * 16:(gl + 1) * 16], in_=BBb[:, :, 8 * k + gl, :]),
                         reads=["BBb", "BBp"], writes=["BBp"])
                if k + 1 < 8:
                    gen_cq(k + 1)
                if opts.get("s5_tsub") == 1:
                    raise StopBuild()
                for gl in range(8):
                    bf_, br_ = (4, 5) if gl % 2 == 0 else (6, 7)
                    for dh, bank in ((0, bf_), (1, br_)):
                        lo = 64 * dh
                        p.pe_fence()
                        for ri in range(2):
                            p.op("tensor", lambda e, bank=bank, lo=lo, gl=gl, ri=ri, Cq=Cq: e.matmul(
                                psb[bank][:, 0:272], lhsT=BBp[lo:lo + 64, ri, gl, :], rhs=Cq[lo:lo + 64, ri, gl].rearrange("p j h -> p (j h)"),
                                start=(ri == 0), stop=(ri == 1)), reads=["BBp", CQK], writes=[PSK[bank]])
                    pf = psb[bf_]
                    pr_ = psb[br_]
                    p.op(A, lambda e, gl=gl, pr_=pr_: e.activation(out=Kc[:, gl, 0:15, :].rearrange("p l h -> p (l h)"), in_=pr_[:, 16:256], func=AF.Identity),
                         reads=[PSK[br_]], writes=[("Kc", gl, 0), PSK[br_]])
                    p.op(A, lambda e, gl=gl, pf=pf: e.activation(out=Kc[:, gl, 16:31, :].rearrange("p l h -> p (l h)"), in_=pf[:, 16:256], func=AF.Identity),
                         reads=[PSK[bf_]], writes=[("Kc", gl, 1), PSK[bf_]])
                    p.op(V, lambda e, gl=gl, k=k: e.tensor_scalar(out=kt2, in0=dmask_s[:, gl, :], scalar1=s5d_s[:, j, k:k + 1], scalar2=None, op0=ALU.mult),
                         reads=["dmask_s", "s5d_s"], writes=["kt2"])
                    p.op(V, lambda e, pf=pf: e.tensor_tensor(out=kt, in0=pf[:, 0:16], in1=kt2, op=ALU.add), reads=[PSK[bf_], "kt2"], writes=["kt", PSK[bf_]])
                    p.op(V, lambda e, gl=gl, pr_=pr_: e.tensor_tensor(out=Kc[:, gl, 15, :], in0=pr_[:, 256:272], in1=kt, op=ALU.add),
                         reads=["kt", PSK[br_]], writes=[("Kc", gl, 2), PSK[br_]])
                for (rows, tok0, fcol, rcol, is_lat) in ((NCC, 0, 0, NCL + 1, False), (NCL, CL, NCC, 1, True)):
                    hk_ = [("hT", k, gi_) for gi_ in range(1, 5)] if is_lat else [("hT", k, 0)]
                    for gp in range(4):
                        kkeys = [("Kc", gl, x) for gl in (2 * gp, 2 * gp + 1) for x in range(3)]
                        bank = ycnt % 2
                        bankb = 2 + ycnt % 2
                        ycnt += 1
                        py = psb[bank]
                        pyb = psb[bankb]
                        p.pe_fence()
                        for s in range(16):
                            p.op("tensor", lambda e, py=py, rows=rows, tok0=tok0, gp=gp, s=s, k=k: e.matmul(
                                py[0:rows, 0:512], lhsT=hT[:, k, tok0 + s:tok0 + rows * Q:Q],
                                rhs=Kc[:, 2 * gp:2 * gp + 2, 15 - s:31 - s, :].rearrange("p g l h -> p g (l h)"),
                                start=(s == 0), stop=(s == 15)), reads=kkeys + hk_, writes=[PSK[bank]])
                        for g2 in range(2):
                            gl = 2 * gp + g2
                            gh = 8 * k + gl
                            p.pe_fence()
                            for ri in range(2):
                                p.op("tensor", lambda e, pyb=pyb, rows=rows, fcol=fcol, ri=ri, gh=gh, gl=gl, g2=g2, Cq=Cq: e.matmul(
                                    pyb[0:rows, 256 * g2:256 * g2 + 256], lhsT=VH[0:64, ri, gh, fcol:fcol + rows], rhs=Cq[0:64, ri, gl, 1:17, :].rearrange("p j h -> p (j h)"),
                                    start=(ri == 0), stop=False), reads=vh_done + [CQK], writes=[PSK[bankb]])
                            p.pe_fence()
                            for ri in range(2):
                                p.op("tensor", lambda e, pyb=pyb, rows=rows, rcol=rcol, ri=ri, gh=gh, gl=gl, g2=g2, Cq=Cq: e.matmul(
                                    pyb[0:rows, 256 * g2:256 * g2 + 256], lhsT=VH[64:128, ri, gh, rcol:rcol + rows], rhs=Cq[64:128, ri, gl, 0:16, :].rearrange("p j h -> p (j h)"),
                                    start=False, stop=(ri == 1)), reads=vh_done + [CQK], writes=[PSK[bankb]])
                        ycs = ycar[0]
                        p.op(A, lambda e, pyb=pyb, rows=rows, ycs=ycs: e.activation(out=ycs[0:rows, :], in_=pyb[0:rows, 0:512], func=AF.Identity),
                             reads=[PSK[bankb]], writes=["ycar"])
                        p.op(V, lambda e, py=py, rows=rows, gp=gp, ycs=ycs: e.tensor_tensor(
                            out=Yl[0:rows, :, 2 * gp:2 * gp + 2, :], in0=py[0:rows, 0:512].rearrange("p (g t h) -> p t g h", g=2, t=16),
                            in1=ycs[0:rows, :].rearrange("p (g t h) -> p t g h", g=2, t=16), op=ALU.add),
                            reads=[PSK[bank], "ycar"], writes=[("Y", 2 * gp), ("Y", 2 * gp + 1)])
                    ylk = [("Y", gl) for gl in range(8)]
                    if is_lat:
                        for tb in range(2):
                            bank = 6 + tb
                            pbf = psb[bank].bitcast(BF)
                            p.pe_fence()
                            for t8 in range(8):
                                t = tb * 8 + t8
                                p.op("tensor", lambda e, pbf=pbf, t=t, t8=t8: e.transpose(pbf[:, t8 * 128:(t8 + 1) * 128], Yl[:, t, :, :].rearrange("p g h -> p (g h)"), ident_bf),
                                     reads=ylk + ["ident_bf"], writes=[PSK[bank]])
                            dst = hT[:, k, CL:T].rearrange("p (c t) -> p t c", t=Q)[:, tb * 8:tb * 8 + 8, :]
                            p.op(A, lambda e, pbf=pbf, dst=dst: e.activation(out=dst, in_=pbf[:, 0:1024].rearrange("p (t c) -> p t c", t=8), func=AF.Gelu_apprx_tanh),
                                 reads=[PSK[bank]] + hk_, writes=hk_ + [("yT", k, tb)])
                    else:
                        bank = 6
                        pbf = psb[bank].bitcast(BF)
                        p.pe_fence()
                        for t in range(16):
                            p.op("tensor", lambda e, pbf=pbf, t=t: e.transpose(pbf[:, t * 16:(t + 1) * 16], Yl[0:NCC, t, :, :].rearrange("p g h -> p (g h)"), ident_bf[0:NCC, 0:NCC]),
                                 reads=ylk + ["ident_bf"], writes=[PSK[bank]])
                        dst = hT[:, k, 0:CL].rearrange("p (c t) -> p t c", t=Q)
                        p.op(A, lambda e, pbf=pbf, dst=dst: e.activation(out=dst, in_=pbf[:, 0:256].rearrange("p (t c) -> p t c", t=16), func=AF.Gelu_apprx_tanh),
                             reads=[PSK[bank]] + hk_, writes=hk_ + [("yT", k, 2)])
            if opts.get("s5_stop") == 5:
                ydb = ar.take([2, 1024], F32)
                p.op(V, lambda e: e.tensor_copy(out=ydb[:, 0, :], in_=hT[:, 4, 0:1024]), reads=[("yT", 4, x) for x in range(3)], writes=["ydb"])
                p.op(V, lambda e: e.tensor_copy(out=ydb[:, 1, :], in_=hT[:, 7, 1280:2304]), reads=[("yT", 7, x) for x in range(3)] + ["ydb"], writes=["ydb"])
                dump(ydb.rearrange("p a t -> p (a t)"), 2048, ["ydb"])
                raise StopBuild()
            if opts.get("s5_stop") == 4:
                ydb = ar.take([2, 1024], F32)
                p.op(V, lambda e: e.tensor_copy(out=ydb[:, 0, :], in_=hT[:, 0, 0:1024]), reads=[("yT", 0, x) for x in range(3)], writes=["ydb"])
                p.op(V, lambda e: e.tensor_copy(out=ydb[:, 1, :], in_=hT[:, 3, 1280:2304]), reads=[("yT", 3, x) for x in range(3)] + ["ydb"], writes=["ydb"])
                dump(ydb.rearrange("p a t -> p (a t)"), 2048, ["ydb"])
                raise StopBuild()
            ar.release()
        ar.release()
        if opts.get("s5_stop") == 6:
            ydb = ar.take([2304], F32)
            for k in range(8):
                p.op(V, lambda e, k=k: e.tensor_copy(out=ydb, in_=hT[:, k, :]), reads=[("yT", k, x) for x in range(3)] + hkeys(k), writes=["ydb"])
                dump(ydb, 2304, ["ydb"])
            raise StopBuild()
        ar.mark()
        wg = ar.take([8, 2048], BF)
        sg = [ar.take([512], F32) for _ in range(4)]
        gt = [ar.take([512], F32) for _ in range(4)]
        p.dma("gpsimd", wg, s5_w_glu[j].rearrange("(k p) n -> p k n", p=128), writes=["wg"])
        ykeys = lambda k: [("yT", k, x) for x in range(3)] + hkeys(k)
        for gi, (t0, w, c) in enumerate(token_groups(True)):
            for d in range(8):
                ba_ = 2 * (d % 4)
                bb_ = 2 * (d % 4) + 1
                pa = psb[ba_]
                pb_ = psb[bb_]
                for k in range(8):
                    p.op("tensor", lambda e, pa=pa, k=k, d=d, t0=t0, w=w: e.matmul(pa[:, :w], lhsT=wg[:, k, d * 128:(d + 1) * 128], rhs=hT[:, k, t0:t0 + w],
                                                                               start=(k == 0), stop=(k == 7)), reads=["wg"] + ykeys(k), writes=[PSK[ba_]])
                for k in range(8):
                    p.op("tensor", lambda e, pb_=pb_, k=k, d=d, t0=t0, w=w: e.matmul(pb_[:, :w], lhsT=wg[:, k, 1024 + d * 128:1024 + (d + 1) * 128], rhs=hT[:, k, t0:t0 + w],
                                                                                start=(k == 0), stop=(k == 7)), reads=["wg"] + ykeys(k), writes=[PSK[bb_]])
                s_t = sg[d % 4]
                g_t = gt[d % 4]
                p.op(A, lambda e, pb_=pb_, s_t=s_t, w=w: e.activation(out=s_t[:, :w], in_=pb_[:, :w], func=AF.Sigmoid),
                     reads=[PSK[bb_]], writes=[("sg", d % 4)])
                p.op(V, lambda e, pa=pa, s_t=s_t, g_t=g_t, w=w: e.tensor_tensor(out=g_t[:, :w], in0=pa[:, :w], in1=s_t[:, :w], op=ALU.mult),
                     reads=[PSK[ba_], ("sg", d % 4)], writes=[("gt", d % 4)])
                p.op(V, lambda e, g_t=g_t, d=d, t0=t0, w=w, c=c: e.scalar_tensor_tensor(
                    out=xT[:, d, t0:t0 + w], in0=g_t[:, :w], scalar=mcol(i, 2, d, c), in1=xT[:, d, t0:t0 + w], op0=ALU.mult, op1=ALU.add),
                    reads=[("gt", d % 4), ("modT", i), ("xT", d)], writes=[("xT", d)])
        ar.release()

    def attn_phase(i):
        j = i // 2
        last = i == DEPTH - 1
        V = "vector"
        G = "gpsimd"
        A = "scalar"
        groups = token_groups(True)
        norm_phase(hT, "hT", lambda k, c: gs1[:, i, k, c:c + 1], lambda k, c: mcol(i, 0, k, c), groups,
                   [("gs1", i), ("modT", i)])
        hk = lambda k: [("hT", k, gi) for gi in range(5)]
        hall = [x for k in range(8) for x in hk(k)]
        ar.mark()
        qT = ar.take([8, T], BF)
        kT = ar.take([4, T], BF)
        vS = ar.take([18, 4, 65], BF)
        p.op(G, lambda e: e.memset(kT, 0.0), writes=["kTz"])
        ar.mark()
        cs_c = ar.take([L], BF)
        cs_s = ar.take([L], BF)
        bd = ar.take([128], BF)
        prm = ar.take([128], BF)
        gq = ar.take([2], F32)
        sqb = [ar.take([512], BF) for _ in range(2)]
        rsb = [ar.take([512], F32) for _ in range(2)]
        qnb = [ar.take([512], BF) for _ in range(2)]
        t1b = [ar.take([512], BF) for _ in range(2)]
        t2b = [ar.take([512], BF) for _ in range(2)]
        p.dma("gpsimd", cs_c, rope_c, writes=["cs_c"])
        p.dma("gpsimd", cs_s, rope_s, writes=["cs_s"])
        p.dma("gpsimd", bd, bd_in, writes=["bd"])
        p.dma("gpsimd", prm, prot_in, writes=["prm"])
        p.dma("sync", gq, qkg[:, j, :], writes=["gq"])
        p.op(G, lambda e: e.memset(vS[:, :, :, 64:65], 1.0), writes=["vS1"])
        itc = [0]

        def qk_chunk(wt, wkey, col0, dst, dkey, gidx, dst2=None):
            for gi, (t0, w, c) in enumerate(groups):
                it = itc[0] % 2
                itc[0] += 1
                sq, rs, qn, t1, t2 = sqb[it], rsb[it], qnb[it], t1b[it], t2b[it]
                ksq, krs, kqn, kt1, kt2 = "asq%d" % it, "ars%d" % it, "aqn%d" % it, "at1%d" % it, "at2%d" % it
                bq, bss, brp = it, 2 + 4 * it, 3 + 4 * it
                pq, pss_, prp = psb[bq], psb[bss], psb[brp]
                for k in range(8):
                    p.op("tensor", lambda e, pq=pq, k=k, t0=t0, w=w: e.matmul(pq[:, :w], lhsT=wt[:, k, col0:col0 + 128], rhs=hT[:, k, t0:t0 + w],
                                                                          start=(k == 0), stop=(k == 7)), reads=[wkey, ("hT", k, gi)], writes=[PSK[bq]])
                p.op(A, lambda e, pq=pq, w=w, sq=sq: e.activation(out=sq[:, :w], in_=pq[:, :w], func=AF.Square), reads=[PSK[bq]], writes=[ksq, PSK[bq]])
                p.op("tensor", lambda e, w=w, sq=sq, pss_=pss_: e.matmul(pss_[:, :w], lhsT=bd, rhs=sq[:, :w], start=True, stop=True), reads=["bd", ksq], writes=[PSK[bss]])
                p.op(A, lambda e, w=w, rs=rs, pss_=pss_: e.activation(out=rs[:, :w], in_=pss_[:, :w], func=AF.Sqrt, bias=EPS, scale=1.0), reads=[PSK[bss]], writes=[krs])
                p.op(V, lambda e, w=w, rs=rs: e.reciprocal(out=rs[:, :w], in_=rs[:, :w]), reads=[krs], writes=[krs])
                if c == 1 and dst2 is None:
                    p.op(V, lambda e, pq=pq, t0=t0, w=w, rs=rs: e.scalar_tensor_tensor(out=dst[:, t0:t0 + w], in0=pq[:, :w], scalar=gq[:, gidx:gidx + 1], in1=rs[:, :w],
                                                                                    op0=ALU.mult, op1=ALU.mult), reads=[PSK[bq], krs, "gq"], writes=[(dkey, gi), PSK[bq]])
                elif c == 1:
                    p.op(V, lambda e, pq=pq, t0=t0, w=w, rs=rs: e.scalar_tensor_tensor(out=dst[0:64, t0:t0 + w], in0=pq[0:64, :w], scalar=gq[0:64, gidx:gidx + 1], in1=rs[0:64, :w],
                                                                                    op0=ALU.mult, op1=ALU.mult), reads=[PSK[bq], krs, "gq", "kTz"], writes=[(dkey, gi), PSK[bq]])
                    p.op(V, lambda e, pq=pq, t0=t0, w=w, rs=rs: e.scalar_tensor_tensor(out=dst2[64:128, t0:t0 + w], in0=pq[64:128, :w], scalar=gq[64:128, gidx:gidx + 1], in1=rs[64:128, :w],
                                                                                    op0=ALU.mult, op1=ALU.mult), reads=[PSK[bq], krs, "gq", "kTz"], writes=[(dkey, gi, 1), PSK[bq]])
                else:
                    l0 = t0 - CL
                    p.op(V, lambda e, pq=pq, w=w, rs=rs, qn=qn: e.scalar_tensor_tensor(out=qn[:, :w], in0=pq[:, :w], scalar=gq[:, gidx:gidx + 1], in1=rs[:, :w],
                                                                                    op0=ALU.mult, op1=ALU.mult), reads=[PSK[bq], krs, "gq"], writes=[kqn, PSK[bq]])
                    p.op("tensor", lambda e, w=w, qn=qn, prp=prp: e.matmul(prp[:, :w], lhsT=prm, rhs=qn[:, :w], start=True, stop=True), reads=["prm", kqn], writes=[PSK[brp]])
                    p.op(G, lambda e, w=w, l0=l0, qn=qn, t1=t1: e.tensor_tensor(out=t1[:, :w], in0=qn[:, :w], in1=cs_c[:, l0:l0 + w], op=ALU.mult), reads=[kqn, "cs_c"], writes=[kt1])
                    p.op(V, lambda e, w=w, l0=l0, prp=prp, t2=t2: e.tensor_tensor(out=t2[:, :w], in0=prp[:, :w], in1=cs_s[:, l0:l0 + w], op=ALU.mult), reads=[PSK[brp], "cs_s"], writes=[kt2])
                    if dst2 is None:
                        p.op(V, lambda e, t0=t0, w=w, t1=t1, t2=t2: e.tensor_tensor(out=dst[:, t0:t0 + w], in0=t1[:, :w], in1=t2[:, :w], op=ALU.add), reads=[kt1, kt2], writes=[(dkey, gi)])
                    else:
                        p.op(V, lambda e, t0=t0, w=w, t1=t1, t2=t2: e.tensor_tensor(out=dst[0:64, t0:t0 + w], in0=t1[0:64, :w], in1=t2[0:64, :w], op=ALU.add),
                             reads=[kt1, kt2, "kTz"], writes=[(dkey, gi)])
                        p.op(G, lambda e, t0=t0, w=w, t1=t1, t2=t2: e.tensor_tensor(out=dst2[64:128, t0:t0 + w], in0=t1[64:128, :w], in1=t2[64:128, :w], op=ALU.add),
                             reads=[kt1, kt2, "kTz"], writes=[(dkey, gi, 1)])

        ar.mark()
        wkv = ar.take([8, 512], BF)
        p.dma("gpsimd", wkv, attn_w_qkv[j][:, 1024:1536].rearrange("(k p) n -> p k n", p=128), writes=["wkv"])
        for n in range(2):
            qk_chunk(wkv, "wkv", n * 128, kT[:, 2 * n, :], ("kT", n), 1, dst2=kT[:, 2 * n + 1, :])
        for tt in range(18):
            pv = psb[4 + tt % 2]
            for k in range(8):
                p.op("tensor", lambda e, pv=pv, k=k, tt=tt: e.matmul(pv[:, 0:256], lhsT=hT[:, k, tt * 128:(tt + 1) * 128], rhs=wkv[:, k, 256:512],
                                                                  start=(k == 0), stop=(k == 7)), reads=["wkv"] + hk(k), writes=[PSK[4 + tt % 2]])
            p.op(A, lambda e, pv=pv, tt=tt: e.activation(out=vS[:, tt, :, 0:64], in_=pv[:, 0:256].rearrange("p (h d) -> p h d", h=4), func=AF.Identity),
                 reads=[PSK[4 + tt % 2]], writes=[("vS", tt)])
        ar.release()
        ar.mark()
        wq = ar.take([8, 512], BF)
        for qh in range(2):
            p.dma("gpsimd", wq, attn_w_qkv[j][:, qh * 512:(qh + 1) * 512].rearrange("(k p) n -> p k n", p=128), writes=["wq"])
            for n4 in range(4):
                n = qh * 4 + n4
                qk_chunk(wq, "wq", n4 * 128, qT[:, n, :], ("qT", n), 0)
        ar.release()
        ar.release()
        ar.mark()
        wo = hT.rearrange("p k t -> p (k t)")[:, 0:16 * 1024].rearrange("p (h n) -> p h n", h=16)
        oTg = ar.take([16, 512], BF)
        pT = [ar.take([512], BF) for _ in range(4)]
        bcs = ar.take([512], F32)
        rec = ar.take([512], F32)
        ones_f = ar.take([64], F32)
        p.dma("gpsimd", wo[0:64], attn_w_o[j].rearrange("(h d) n -> d h n", d=64), writes=["wo"] + hall)
        p.op(G, lambda e: e.memset(wo[64:128], 0.0), writes=["wo2"] + hall)
        p.op(V, lambda e: e.memset(oTg[64:128], 0.0), writes=["oTg2"])
        p.op(V, lambda e: e.memset(ones_f, 1.0), writes=["ones_f"])
        vkeys = [("vS", tt) for tt in range(18)] + ["vS1"]
        SCALE = HD ** -0.5
        qgroups = [(gi, t0, w, c) for gi, (t0, w, c) in enumerate(groups) if not (c == 1 and last)]
        scnt = 0
        for (gi, t0, w, c) in qgroups:
            ktiles = range(2) if c == 1 else range(18)
            nkt = len(ktiles)
            pending = []
            for h in range(16):
                kv = h // 4
                half = kv % 2
                lo = 64 * half
                perm_pos = QPERM.index(h)
                qn_, qhalf = perm_pos // 2, perm_pos % 2
                assert qhalf == half
                po = psb[4 + h % 2]
                kts = list(ktiles)

                def score(ti, ps_, bs, lo=lo, kv=kv, qn_=qn_, t0=t0, w=w, gi=gi):
                    kt = kts[ti]
                    p.op("tensor", lambda e: e.matmul(
                        ps_[:, :w], lhsT=kT[:, kv, kt * 128:(kt + 1) * 128], rhs=qT[:, qn_, t0:t0 + w], start=True, stop=True),
                        reads=[(("kT", kv // 2), g_) for g_ in range(5)] + [(("kT", kv // 2), g_, 1) for g_ in range(5)] + ["kTz", (("qT", qn_), gi)], writes=[PSK[bs]])

                LOOK = 3
                slots = []
                for ti in range(min(LOOK, nkt)):
                    bs = scnt % 4
                    scnt += 1
                    slots.append(bs)
                    score(ti, psb[bs], bs)
                for ti in range(nkt):
                    bs = slots[ti]
                    ps_ = psb[bs]
                    pt_ = pT[bs]
                    kt = kts[ti]
                    p.op(A, lambda e, ps_=ps_, pt_=pt_, w=w: e.activation(out=pt_[:, :w], in_=ps_[:, :w], func=AF.Exp, scale=SCALE),
                         reads=[PSK[bs]], writes=[("pT", bs)])
                    if ti == min(8, nkt - 1) and pending:
                        pending.pop(0)()
                    if ti + LOOK < nkt:
                        nb_ = scnt % 4
                        scnt += 1
                        slots.append(nb_)
                        score(ti + LOOK, psb[nb_], nb_)
                    p.op("tensor", lambda e, po=po, pt_=pt_, kt=kt, kv=kv, w=w, ti=ti, nkt=nkt: e.matmul(
                        po[0:65, :w], lhsT=vS[:, kt, kv, :], rhs=pt_[:, :w], start=(ti == 0), stop=(ti == nkt - 1)),
                        reads=vkeys + [("pT", bs)], writes=[PSK[4 + h % 2]])
                def norm_head(po=po, h=h, w=w):
                    p.op(V, lambda e: e.reciprocal(out=rec[64:65, :w], in_=po[64:65, :w]), reads=[PSK[4 + h % 2]], writes=["rec", PSK[4 + h % 2]])
                    p.op("tensor", lambda e: e.matmul(psb[6][0:64, :w], lhsT=ones_f[64:65, 0:64], rhs=rec[64:65, :w], start=True, stop=True),
                         reads=["rec", "ones_f"], writes=[PSK[6]])
                    p.op(V, lambda e: e.tensor_copy(out=bcs[0:64, :w], in_=psb[6][0:64, :w]), reads=[PSK[6]], writes=["bcs", PSK[6]])
                    p.op(V, lambda e: e.tensor_tensor(out=oTg[0:64, h, :w], in0=po[0:64, :w], in1=bcs[0:64, :w], op=ALU.mult),
                         reads=[PSK[4 + h % 2], "bcs"], writes=[("oTg", h), PSK[4 + h % 2]])
                pending.append(norm_head)
            while pending:
                pending.pop(0)()
            for n in range(8):
                pw = psb[7] if n % 2 == 0 else psb[6]
                pwk = PSK[7] if n % 2 == 0 else PSK[6]
                for h in range(16):
                    p.op("tensor", lambda e, pw=pw, h=h, n=n, w=w: e.matmul(pw[:, :w], lhsT=wo[:, h, n * 128:(n + 1) * 128], rhs=oTg[:, h, :w],
                                                                       start=(h == 0), stop=(h == 15)), reads=["wo", "wo2", "oTg2", ("oTg", h)], writes=[pwk])
                p.op(V, lambda e, pw=pw, n=n, t0=t0, w=w, c=c: e.scalar_tensor_tensor(
                    out=xT[:, n, t0:t0 + w], in0=pw[:, :w], scalar=mcol(i, 2, n, c), in1=xT[:, n, t0:t0 + w], op0=ALU.mult, op1=ALU.add),
                    reads=[pwk, ("modT", i), ("xT", n)], writes=[("xT", n)])
        ar.release()
        ar.release()

    hT = ar.take([8, T], BF)
    for i in range(DEPTH) if not opts.get("s5_stop") else []:
        last = i == DEPTH - 1
        if i == 0 or not opts.get("mod_side", True):
            mod_phase(i)
        if opts.get("only_mod"):
            dump(modT[:, 0].rearrange("p a b -> p (a b)"), 96, [("modT", 0)])
            break
        if mixers and i % 2 == 1 and opts.get("attn", True):
            attn_phase(i)
            if opts.get("stop_after") == (i, "mix"):
                break
        if mixers and i % 2 == 0 and opts.get("s5", True):
            s5_phase(i)
            if opts.get("stop_after") == (i, "mix"):
                break
        groups = token_groups(include_ctx=not last)
        norm_phase(hT, "hT", lambda k, c, i=i: gs2[:, i, k, c:c + 1], lambda k, c, i=i: mcol(i, 3, k, c), groups,
                   [("gs2", i), ("modT", i)])
        ffn_phase(i, hT, "hT", groups, side_layer=(i + 1 if (not last and opts.get("mod_side", True)) else None))

    if opts.get("s5_stop"):
        try:
            mod_phase(0)
            s5_phase(0)
        except StopBuild:
            pass
        p.emit()
        return nc, p
    if opts.get("only_mod"):
        p.emit()
        return nc, p
    if opts.get("stop_after"):
        for k in range(8):
            p.dma("sync", outT[k * 128:(k + 1) * 128, :], xT[:, k, CL:T], reads=[("xT", k)])
            p.dma("sync", dbg[:, k * 256:(k + 1) * 256], xT[:, k, 0:CL], reads=[("xT", k)])
        p.emit()
        return nc, p
    ar.mark()
    sq = ar.take([8, 512], BF)
    rstd = ar.take([512], F32)
    ost = [ar.take([8, 512], F32) for _ in range(2)]
    pss = psb[6]
    for gi, (t0, w, c) in enumerate(token_groups(include_ctx=False)):
        o_t = ost[gi % 2]
        for k in range(8):
            p.op("scalar", lambda e, k=k, t0=t0, w=w: e.activation(out=sq[:, k, :w], in_=xT[:, k, t0:t0 + w], func=AF.Square),
                 reads=[("xT", k)], writes=[("fsq", k)])
        for k in range(8):
            p.op("tensor", lambda e, k=k, w=w: e.matmul(pss[:, :w], lhsT=ones_bf, rhs=sq[:, k, :w], start=(k == 0), stop=(k == 7)),
                 reads=[("fsq", k), "ones_bf"], writes=[PSK[6]])
        p.op("scalar", lambda e, w=w: e.activation(out=rstd[:, :w], in_=pss[:, :w], func=AF.Sqrt, bias=EPS, scale=1.0 / D),
             reads=[PSK[6]], writes=["frstd"])
        p.op("vector", lambda e, w=w: e.reciprocal(out=rstd[:, :w], in_=rstd[:, :w]), reads=["frstd"], writes=["frstd"])
        for k in range(8):
            p.op("vector", lambda e, k=k, t0=t0, w=w, o_t=o_t: e.scalar_tensor_tensor(
                out=o_t[:, k, :w], in0=xT[:, k, t0:t0 + w], scalar=gfin_s[:, k:k + 1], in1=rstd[:, :w], op0=ALU.mult, op1=ALU.mult),
                reads=[("xT", k), "frstd", "gfin_s"], writes=[("ost", gi % 2, k)])
            p.dma("sync", outT[k * 128:(k + 1) * 128, t0 - CL:t0 - CL + w], o_t[:, k, :w], reads=[("ost", gi % 2, k)])
    ar.release()
    p.emit()
    return nc, p


def _cols(v):
    return np.ascontiguousarray(np.asarray(v, np.float32).reshape(-1, 128).T)


def make_in_maps(inputs):
    x = np.asarray(inputs["x"], np.float32)
    ctx = np.asarray(inputs["ctx"], np.float32)
    c = np.asarray(inputs["c"], np.float32)
    c_ctx = np.asarray(inputs["c_ctx"], np.float32)
    shared = {
        "adab": np.ascontiguousarray(np.stack([_cols(inputs["ada_b"][i]) for i in range(DEPTH)], axis=1)),
        "gmix": np.ascontiguousarray(np.stack([_cols(inputs["norm_mix_g"][i]) for i in range(DEPTH)], axis=1)),
        "gffn": np.ascontiguousarray(np.stack([_cols(inputs["norm_ffn_g"][i]) for i in range(DEPTH)], axis=1)),
        "gfin": _cols(inputs["final_g"]),
        "ada_w": np.ascontiguousarray(inputs["ada_w"], np.float32),
        "ffn_w1": np.ascontiguousarray(inputs["ffn_w1"], np.float32),
        "ffn_w2": np.ascontiguousarray(inputs["ffn_w2"], np.float32),
        "ident": np.eye(128, dtype=np.float32),
    }
    f32 = lambda a: np.ascontiguousarray(a, dtype=np.float32)
    a_re = np.asarray(inputs["s5_a_re"]); a_im = np.asarray(inputs["s5_a_im"]); ldt = np.asarray(inputs["s5_log_dt"])
    shared["s5ar"] = f32(a_re.transpose(0, 1, 3, 2).reshape(2, 128, 64))
    shared["s5ai"] = f32(a_im.transpose(0, 1, 3, 2).reshape(2, 128, 64))
    shared["s5ldt"] = f32(np.broadcast_to(ldt[:, :, None, :], (2, 2, 64, 64)).reshape(2, 128, 64))
    bre = np.asarray(inputs["s5_b_re"]).transpose(0, 1, 3, 2, 4).reshape(2, 128, 64, 16)
    bim = np.asarray(inputs["s5_b_im"]).transpose(0, 1, 3, 2, 4).reshape(2, 128, 64, 16)
    shared["s5b"] = f32(np.stack([bre, bim], axis=2))
    cre = np.asarray(inputs["s5_c_re"]).transpose(0, 1, 4, 2, 3).reshape(2, 128, 64, 16)
    cim = np.asarray(inputs["s5_c_im"]).transpose(0, 1, 4, 2, 3).reshape(2, 128, 64, 16)
    shared["s5c"] = f32(np.stack([cre, cim], axis=2))
    shared["s5d"] = f32(np.stack([_cols(inputs["s5_d"][j]) for j in range(2)], axis=1))
    ex = np.zeros((128, 35), np.float32)
    ex[:64, 0:17] = np.arange(17); ex[64:, 0:17] = 16 - np.arange(17)
    ex[:64, 17:33] = 15 - np.arange(16); ex[64:, 17:33] = np.arange(16)
    ex[:, 33] = 1.0; ex[:, 34] = 16.0
    shared["expo"] = ex
    gl = np.arange(128) // 16
    hi = np.arange(128) % 16
    parm = np.stack([(gl % 2 == 0), (gl % 2 == 1)], axis=1).astype(np.float32)
    shared["parm"] = f32(parm)
    shared["dmask"] = f32((gl[:, None, None] == np.arange(8)[None, :, None]) * (hi[:, None, None] == np.arange(16)[None, None, :]))
    shared["s5_w_glu"] = f32(inputs["s5_w_glu"])
    wqkv = np.asarray(inputs["attn_w_qkv"], np.float32)
    qcols = np.concatenate([np.arange(h * 64, (h + 1) * 64) for h in QPERM])
    shared["attn_w_qkv"] = f32(np.concatenate([wqkv[:, :, qcols], wqkv[:, :, 1024:]], axis=2))
    shared["attn_w_o"] = f32(inputs["attn_w_o"])
    tpos = np.arange(L)
    rowp = (tpos // 64).astype(np.float64); colp = (tpos % 64).astype(np.float64)
    invf = 10000.0 ** (-np.arange(16, dtype=np.float64) / 16)
    ang = np.zeros((64, L))
    ang[0:16] = invf[:, None] * rowp[None, :]; ang[16:32] = ang[0:16]
    ang[32:48] = invf[:, None] * colp[None, :]; ang[48:64] = ang[32:48]
    shared["rope_c"] = f32(np.tile(np.cos(ang), (2, 1)))
    shared["rope_s"] = f32(np.tile(np.sin(ang), (2, 1)))
    bdm = np.zeros((128, 128), np.float32)
    bdm[:64, :64] = 1.0 / 64; bdm[64:, 64:] = 1.0 / 64
    shared["bd"] = bdm
    pr = np.zeros((128, 128), np.float32)
    for base in (0, 32, 64, 96):
        for d_ in range(16):
            pr[base + d_ + 16, base + d_] = -1.0
            pr[base + d_, base + d_ + 16] = 1.0
    shared["prot"] = pr
    qg = np.asarray(inputs["attn_q_g"], np.float32); kg = np.asarray(inputs["attn_k_g"], np.float32)
    shared["qkg"] = f32(np.stack([np.tile(qg, (1, 2)).T, np.tile(kg, (1, 2)).T], axis=2))
    maps = []
    for b in range(8):
        m = dict(shared)
        m["xin"] = np.ascontiguousarray(np.concatenate([ctx[b], x[b]], axis=0).T)
        m["ccols"] = np.ascontiguousarray(np.stack([_cols(c[b]), _cols(c_ctx)], axis=2))
        maps.append(m)
    return maps


def kernel(**inputs):
    nc, _ = build()
    maps = make_in_maps(inputs)
    res = run_bass_kernel_spmd(nc, maps, core_ids=list(range(8)))
    out = np.stack([np.ascontiguousarray(res.results[b]["outT"].T) for b in range(8)], axis=0)
    return out.astype(np.float32)
```
